# Optimizing a Trainium2 kernel written in Bass

```python
import math
import jax, jax.numpy as jnp
from jax import lax
import numpy as np

D_MODEL = 1024
BATCH = 16
SEQ = 4096
DEPTH = 1
DEC_BATCH = 2
DEC_SEQ = 8192
PAST_LEN = 128

D_RNN = D_MODEL
N_LRU_HEADS = 8
LRU_BLOCK = D_RNN // N_LRU_HEADS
LRU_C = 8.0
CONV_A_WIDTH = 4
CONV_A_LEFT = 2
D_FNET = D_MODEL // 2
N_FNET_GROUPS = 4
FNET_GROUP = D_FNET // N_FNET_GROUPS
D_FF = 2816
CONV_F_WIDTH = 3
CONV_F_LEFT = 1
N_MOD = 6
RMS_EPS = 1e-6
D_IN = 2 * D_RNN + D_FNET + 2 * D_MODEL

kernel_name = "hybrid_rglru_fnet_convffn_encoder"


def rms_norm(x, g):
    xf = x.astype(jnp.float32)
    y = xf * lax.rsqrt(jnp.mean(xf * xf, axis=-1, keepdims=True) + RMS_EPS)
    return (y * g.astype(jnp.float32)).astype(x.dtype)


def depthwise_conv(x, w, b, left):
    k_w = w.shape[0]
    s = x.shape[1]
    xp = jnp.pad(x, ((0, 0), (left, k_w - 1 - left), (0, 0)))
    out = b + xp[:, 0:s] * w[0]
    for k in range(1, k_w):
        out = out + xp[:, k:k + s] * w[k]
    return out


def _lru_combine(left, right):
    a_l, b_l = left
    a_r, b_r = right
    return (a_l * a_r, a_r * b_l + b_r)


def rg_lru(x, w_a, b_a, w_x, b_x, lam, reverse):
    bsz, s, w = x.shape
    xh = x.reshape(bsz, s, N_LRU_HEADS, LRU_BLOCK)
    r = jax.nn.sigmoid((jnp.einsum('bshi,hij->bshj', xh, w_a).reshape(bsz, s, w) + b_a).astype(jnp.float32))
    i = jax.nn.sigmoid((jnp.einsum('bshi,hij->bshj', xh, w_x).reshape(bsz, s, w) + b_x).astype(jnp.float32))
    log_a = -LRU_C * r * jax.nn.softplus(-lam.astype(jnp.float32))
    a = jnp.exp(log_a)
    mult = jnp.sqrt(jnp.maximum(-jnp.expm1(2.0 * log_a), 0.0))
    b = mult * i * x.astype(jnp.float32)
    _, h = lax.associative_scan(_lru_combine, (a, b), reverse=reverse, axis=1)
    return h.astype(x.dtype)


def fourier_mix(x):
    bsz, s, _ = x.shape
    xg = x.reshape(bsz, s, N_FNET_GROUPS, FNET_GROUP).astype(jnp.float32)
    f = jnp.fft.fft2(xg, axes=(1, 3), norm='ortho').real
    return f.reshape(bsz, s, D_FNET).astype(x.dtype)


def encoder_layer(x, c, w_ada, b_ada, g_pre_mix, w_in, conv_w, conv_b,
                  w_lru_a, b_lru_a, w_lru_x, b_lru_x, lru_lambda,
                  w_a_out, w_b_out, w_o, g_post_mix,
                  g_pre_ffn, w_up, ffn_conv_w, ffn_conv_b, w_down, g_post_ffn):
    mod = jax.nn.silu(c) @ w_ada + b_ada
    sh1, sc1, gt1, sh2, sc2, gt2 = [m[:, None, :] for m in jnp.split(mod, N_MOD, axis=-1)]

    h = rms_norm(x, g_pre_mix) * (1.0 + sc1) + sh1
    u = h @ w_in
    o1 = D_RNN
    o2 = o1 + D_RNN
    o3 = o2 + D_FNET
    o4 = o3 + D_MODEL
    xa, ya, xb, ga, gb = u[..., :o1], u[..., o1:o2], u[..., o2:o3], u[..., o3:o4], u[..., o4:]

    xa = depthwise_conv(xa, conv_w, conv_b, CONV_A_LEFT)
    h_fwd = rg_lru(xa, w_lru_a[0], b_lru_a[0], w_lru_x[0], b_lru_x[0], lru_lambda[0], reverse=False)
    h_bwd = rg_lru(xa, w_lru_a[1], b_lru_a[1], w_lru_x[1], b_lru_x[1], lru_lambda[1], reverse=True)
    y_a = ((h_fwd + h_bwd) * jax.nn.gelu(ya)) @ w_a_out

    y_b = fourier_mix(xb) @ w_b_out

    merged = jax.nn.sigmoid(ga) * y_a + jax.nn.sigmoid(gb) * y_b
    x = x + gt1 * rms_norm(merged @ w_o, g_post_mix)

    h = rms_norm(x, g_pre_ffn) * (1.0 + sc2) + sh2
    up = depthwise_conv(h @ w_up, ffn_conv_w, ffn_conv_b, CONV_F_LEFT)
    val, gate = up[..., :D_FF], up[..., D_FF:]
    f = (jax.nn.gelu(gate) * val) @ w_down
    x = x + gt2 * rms_norm(f, g_post_ffn)
    return x


def setup_inputs(seed: int = 0) -> dict:
    key = jax.random.key(seed)
    ks = jax.random.split(key, 32)
    f32 = jnp.float32
    nrm = lambda k, shape, scale: jax.random.normal(k, shape, f32) * scale
    p = jax.random.uniform(ks[13], (DEPTH, 2, D_RNN), f32, 0.9, 0.999)
    return {
        'x_prompt': nrm(ks[0], (BATCH, SEQ, D_MODEL), 1.0),
        'x_sample': nrm(ks[1], (DEC_BATCH, DEC_SEQ, D_MODEL), 1.0),
        'c_prompt': nrm(ks[2], (BATCH, D_MODEL), 1.0),
        'c_sample': nrm(ks[3], (DEC_BATCH, D_MODEL), 1.0),
        'w_ada': nrm(ks[4], (DEPTH, D_MODEL, N_MOD * D_MODEL), D_MODEL ** -0.5),
        'b_ada': nrm(ks[5], (DEPTH, N_MOD * D_MODEL), 0.02),
        'g_pre_mix': 1.0 + nrm(ks[6], (DEPTH, D_MODEL), 0.02),
        'w_in': nrm(ks[7], (DEPTH, D_MODEL, D_IN), D_MODEL ** -0.5),
        'conv_w': nrm(ks[8], (DEPTH, CONV_A_WIDTH, D_RNN), CONV_A_WIDTH ** -0.5),
        'conv_b': nrm(ks[9], (DEPTH, D_RNN), 0.02),
        'w_lru_a': nrm(ks[10], (DEPTH, 2, N_LRU_HEADS, LRU_BLOCK, LRU_BLOCK), LRU_BLOCK ** -0.5),
        'b_lru_a': nrm(ks[11], (DEPTH, 2, D_RNN), 0.02),
        'w_lru_x': nrm(ks[12], (DEPTH, 2, N_LRU_HEADS, LRU_BLOCK, LRU_BLOCK), LRU_BLOCK ** -0.5),
        'b_lru_x': nrm(ks[14], (DEPTH, 2, D_RNN), 0.02),
        'lru_lambda': jnp.log(p) - jnp.log1p(-p),
        'w_a_out': nrm(ks[15], (DEPTH, D_RNN, D_MODEL), D_RNN ** -0.5),
        'w_b_out': nrm(ks[16], (DEPTH, D_FNET, D_MODEL), D_FNET ** -0.5),
        'w_o': nrm(ks[17], (DEPTH, D_MODEL, D_MODEL), D_MODEL ** -0.5),
        'g_post_mix': 1.0 + nrm(ks[18], (DEPTH, D_MODEL), 0.02),
        'g_pre_ffn': 1.0 + nrm(ks[19], (DEPTH, D_MODEL), 0.02),
        'w_up': nrm(ks[20], (DEPTH, D_MODEL, 2 * D_FF), D_MODEL ** -0.5),
        'ffn_conv_w': nrm(ks[21], (DEPTH, CONV_F_WIDTH, 2 * D_FF), CONV_F_WIDTH ** -0.5),
        'ffn_conv_b': nrm(ks[22], (DEPTH, 2 * D_FF), 0.02),
        'w_down': nrm(ks[23], (DEPTH, D_FF, D_MODEL), D_FF ** -0.5),
        'g_post_ffn': 1.0 + nrm(ks[24], (DEPTH, D_MODEL), 0.02),
    }


def reference(x_prompt, x_sample, c_prompt, c_sample, w_ada, b_ada, g_pre_mix, w_in, conv_w, conv_b,
              w_lru_a, b_lru_a, w_lru_x, b_lru_x, lru_lambda, w_a_out, w_b_out, w_o, g_post_mix,
              g_pre_ffn, w_up, ffn_conv_w, ffn_conv_b, w_down, g_post_ffn):
    y_prompt = x_prompt
    y_sample = x_sample
    for l in range(DEPTH):
        layer_params = (w_ada[l], b_ada[l], g_pre_mix[l], w_in[l], conv_w[l], conv_b[l],
                        w_lru_a[l], b_lru_a[l], w_lru_x[l], b_lru_x[l], lru_lambda[l],
                        w_a_out[l], w_b_out[l], w_o[l], g_post_mix[l],
                        g_pre_ffn[l], w_up[l], ffn_conv_w[l], ffn_conv_b[l], w_down[l], g_post_ffn[l])
        y_prompt = encoder_layer(y_prompt, c_prompt, *layer_params)
        y_sample = encoder_layer(y_sample, c_sample, *layer_params)
    return (y_prompt, y_sample)
```

```python
import contextlib
import numpy as np
import concourse.bass as bass
import concourse.mybir as mybir
from concourse.bass_utils import run_bass_kernel_spmd

F32 = mybir.dt.float32
BF16 = mybir.dt.bfloat16
AF = mybir.ActivationFunctionType
ALU = mybir.AluOpType

D = 1024
DIN = 4608
DFF = 2816
NFF = 22
TB = 512
EPS = 1e-6
N_CORES = 8

def vec_layout(nseq):
    names = [("g_pre_mix", 8), ("conv_w", 32), ("conv_b", 8), ("b_lru_a", 16), ("b_lru_x", 16),
             ("lam", 16), ("g_pre_ffn", 8), ("ffn_conv_w", 132), ("ffn_conv_b", 44), ("b_ada", 48),
             ("c", 8 * nseq)]
    off = {}
    o = 0
    for n, k in names:
        off[n] = o
        o += k
    return off, o


class Buf:
    __slots__ = ("t", "name", "w", "r")

    def __init__(self, t, name):
        self.t = t
        self.name = name
        self.w = {}
        self.r = {}


class Trk:
    def __init__(self, nc, es):
        self.nc = nc
        self.es = es
        self.eng = {"pe": nc.tensor, "act": nc.scalar, "dve": nc.vector, "pool": nc.gpsimd, "sp": nc.sync}
        self.sem = {}
        self.cnt = {}
        self.waited = {}
        for e in self.eng:
            self.sem[e] = es.enter_context(nc.semaphore("s_" + e))
            self.cnt[e] = 0
        self.dsem = {}
        self.ninst = 0

    def _wait(self, e, tok):
        key, val, h = tok
        if key == ("eng", e) and e == "pe":
            return
        w = self.waited.setdefault(e, {})
        if w.get(key, 0) >= val:
            return
        self.eng[e].wait_ge(h, val)
        w[key] = val

    def _deps(self, e, reads, writes):
        for b in reads:
            for tok in b.w.values():
                self._wait(e, tok)
        for b in writes:
            for tok in b.w.values():
                self._wait(e, tok)
            for tok in b.r.values():
                self._wait(e, tok)

    def _mark(self, key, tok, reads, writes):
        for b in reads:
            b.r[key] = tok
        for b in writes:
            b.w = {key: tok}
            b.r = {}

    def op(self, e, fn, reads=(), writes=()):
        self._deps(e, reads, writes)
        inst = fn(self.eng[e])
        self.cnt[e] += 1
        self.ninst += 1
        inst.then_inc(self.sem[e], 1)
        self._mark(("eng", e), (("eng", e), self.cnt[e], self.sem[e]), reads, writes)
        return inst

    def dma(self, e, out, in_, reads=(), writes=(), key=None):
        self._deps(e, reads, writes)
        if key is None:
            key = writes[0].name if writes else reads[0].name
        key = (key, e)
        if key not in self.dsem:
            self.dsem[key] = [self.es.enter_context(self.nc.semaphore("d%d" % len(self.dsem))), 0]
        d = self.dsem[key]
        inst = self.eng[e].dma_start(out=out, in_=in_)
        d[1] += 16
        inst.then_inc(d[0], 16)
        self.ninst += 1
        self._mark(("dma", key), (("dma", key), d[1], d[0]), reads, writes)
        return inst

    def dma_barrier(self, engines=("sp", "pool")):
        for e in engines:
            for key, d in self.dsem.items():
                if d[1] > 0:
                    self._wait(e, (("dma", key), d[1], d[0]))

    def finish(self, e="sp"):
        self.dma_barrier((e,))
        for k in self.eng:
            if k != e and self.cnt[k] > 0:
                self._wait(e, (("eng", k), self.cnt[k], self.sem[k]))


class _Stop(Exception):
    pass


def build_nc(seq_lens, debug=False, stop=None, seg=None):
    nseq = len(seq_lens)
    TOT = sum(seq_lens)
    seq_off = [sum(seq_lens[:i]) for i in range(nseq)]
    Ls = sorted(set(s // 128 for s in seq_lens))
    VOFF, NVEC = vec_layout(nseq)

    nc = bass.Bass("TRN2", target_bir_lowering=False)

    def din(name, shape, dt=F32):
        return nc.dram_tensor(name, list(shape), dt, kind="ExternalInput").ap()

    skind = "ExternalOutput" if debug else "Internal"

    def dscr(name, shape, dt):
        return nc.dram_tensor(name, list(shape), dt, kind=skind).ap()

    xin = din("xin", [TOT, D])
    vec = din("vec", [NVEC, 128])
    w_ada = din("w_ada", [D, 6 * D])
    b_ada = din("b_ada", [1, 6 * D])
    w_in = din("w_in", [D, DIN])
    w_lru_a = din("w_lru_a", [16, 128, 128])
    w_lru_x = din("w_lru_x", [16, 128, 128])
    w_a_out = din("w_a_out", [D, D])
    w_b_out = din("w_b_out", [512, D])
    w_o = din("w_o", [D, D])
    w_up = din("w_up", [D, 2 * DFF])
    w_down = din("w_down", [DFF, D])
    g_post_mix = din("g_post_mix", [1, D])
    g_post_ffn = din("g_post_ffn", [1, D])
    tabA = din("tabA", [128, 256])
    tabW = din("tabW", [128, 256])
    tw_in = {L: din("tw%d" % L, [128, 3 * L]) for L in Ls}
    kron_in = {L: din("kron%d" % L, [128, 512]) for L in Ls}
    out_lens = [(seg[1] if (seg is not None and seg[0] == i) else S) for i, S in enumerate(seq_lens)]
    out_off = [sum(out_lens[:i]) for i in range(nseq)]
    yout = nc.dram_tensor("yout", [sum(out_lens), D], F32, kind="ExternalOutput").ap()
    if seg is not None:
        ntv = (seg[1] + 256) // 128
        segidx = din("segidx", [128, ntv], mybir.dt.int32)
        segidxl = din("segidxl", [128, ntv + 1], mybir.dt.int32)
        segflag = din("segflag", [128, 2])

    XA = [dscr("XA%d" % s, [D, S], BF16) for s, S in enumerate(seq_lens)]
    GY = [dscr("GY%d" % s, [D, S], BF16) for s, S in enumerate(seq_lens)]
    TA = [dscr("TA%d" % s, [D, S], BF16) for s, S in enumerate(seq_lens)]
    TBG = [dscr("TBG%d" % s, [D, S], BF16) for s, S in enumerate(seq_lens)]
    XC = [dscr("XC%d" % s, [D, S], BF16) for s, S in enumerate(seq_lens)]
    HF = [dscr("HF%d" % s, [D, S], BF16) for s, S in enumerate(seq_lens)]
    YBG = [dscr("YBG%d" % s, [D, S], BF16) for s, S in enumerate(seq_lens)]
    XB = [dscr("XB%d" % s, [S, 512], BF16) for s, S in enumerate(seq_lens)]
    YT = [dscr("YT%d" % s, [S, 1024], BF16) for s, S in enumerate(seq_lens)]
    X1 = [dscr("X1_%d" % s, [S, D], F32) for s, S in enumerate(seq_lens)]
    PQD = [dscr("PQD%d" % s, [D, S], BF16) for s, S in enumerate(seq_lens)]

    dbg_ab = [nc.dram_tensor("dbg_%s" % n, [128, TB], F32, kind="ExternalOutput").ap() for n in "abcd"] if debug else None
    dbg_done = []

    def fm(T):
        return T.rearrange("(ft p) s -> p ft s", p=128)

    es = contextlib.ExitStack()
    try:
      with es:
        T = Trk(nc, es)

        def stop_here(tag):
            if stop == tag:
                T.finish("sp")
                raise _Stop()

        def sb(name, shape, dt=F32):
            return Buf(es.enter_context(nc.sbuf_tensor(name, list(shape), dt)), name)

        def ps(name, shape, dt=F32):
            return Buf(es.enter_context(nc.psum_tensor(name, list(shape), dt)), name)

        NMM = 6
        ps_mm = [ps("ps_mm%d" % i, [128, 512]) for i in range(NMM)]
        ps_tr = [ps("ps_tr%d" % i, [128, 1024], BF16) for i in range(2)]
        mm_i = [0]
        tr_i = [0]

        def next_mm():
            b = ps_mm[mm_i[0] % NMM]
            mm_i[0] += 1
            return b

        def next_tr():
            b = ps_tr[tr_i[0] % 2]
            tr_i[0] += 1
            return b

        ident_f = sb("ident_f", [128, 128])
        ident_b = sb("ident_b", [128, 128], BF16)
        ones_f = sb("ones_f", [128, 128])
        nhalf = sb("nhalf", [128, 1])
        cols = sb("cols", [128, NVEC])
        dcols = sb("dcols", [128, 96])
        modc = sb("modc", [128, 48, nseq])
        G1 = sb("G1", [128, 8, nseq])
        G2 = sb("G2", [128, 8, nseq])
        GT = [[None] * nseq, [sb("GT1_%d" % s, [128, D]) for s in range(nseq)]]

        T.op("pool", lambda e: e.memset(ident_f.t[:], 0.0), writes=[ident_f])
        T.op("pool", lambda e: e.affine_select(out=ident_f.t[:], in_=ident_f.t[:], pattern=[[-1, 128]],
                                               compare_op=ALU.not_equal, fill=1.0, base=0,
                                               channel_multiplier=1), reads=[ident_f], writes=[ident_f])
        T.op("dve", lambda e: e.tensor_copy(ident_b.t[:], ident_f.t[:]), reads=[ident_f], writes=[ident_b])
        T.op("pool", lambda e: e.memset(ones_f.t[:], 1.0), writes=[ones_f])
        T.op("pool", lambda e: e.memset(nhalf.t[:], -0.5), writes=[nhalf])

        class Ring:
            def __init__(self, name, shape, dt, n):
                self.b = [sb("%s%d" % (name, i), shape, dt) for i in range(n)]
                self.i = 0

            def next(self):
                b = self.b[self.i % len(self.b)]
                self.i += 1
                return b

        cv_i = [0]
        cv_eng = ("act", "dve")

        def load_w_bf16(dst, src3, kcn, n, stage_ring, nchunk=2048):
            cv_i[0] += 1
            for kc in range(kcn):
                for n0 in range(0, n, nchunk):
                    nn = min(nchunk, n - n0)
                    st = stage_ring.next()
                    T.dma("sp", st.t[:, 0:nn], src3[:, kc, n0:n0 + nn], writes=[st])
                    e = cv_eng[cv_i[0] % 2]
                    if e == "act":
                        T.op(e, lambda en: en.activation(out=dst.t[:, kc, n0:n0 + nn], in_=st.t[:, 0:nn], func=AF.Copy),
                             reads=[st], writes=[dst])
                    else:
                        T.op(e, lambda en: en.tensor_copy(dst.t[:, kc, n0:n0 + nn], st.t[:, 0:nn]),
                             reads=[st], writes=[dst])

        def rstd_from_ss(ss, rs):
            T.op("dve", lambda e: e.tensor_scalar(out=ss.t[:], in0=ss.t[:], scalar1=1.0 / D, scalar2=EPS,
                                                  op0=ALU.mult, op1=ALU.add), reads=[ss], writes=[ss])
            T.op("pool", lambda e: e.tensor_tensor(out=rs.t[:], in0=ss.t[:], in1=nhalf.t[:], op=ALU.pow),
                 reads=[ss, nhalf], writes=[rs])

        def tm_epilogue(oh, sq_scale, ss2, rs, junk, GTb, xres, outb):
            for hh in range(2):
                T.op("act", lambda e: e.activation(out=junk.t[:, hh * 512:(hh + 1) * 512], in_=oh[hh].t[:], func=AF.Square,
                                                   scale=sq_scale, accum_out=ss2.t[:, hh:hh + 1]),
                     reads=[oh[hh]], writes=[junk, ss2])
            T.op("dve", lambda e: e.tensor_tensor(out=ss2.t[:, 0:1], in0=ss2.t[:, 0:1], in1=ss2.t[:, 1:2], op=ALU.add),
                 reads=[ss2], writes=[ss2])
            T.op("dve", lambda e: e.tensor_scalar(out=ss2.t[:, 0:1], in0=ss2.t[:, 0:1], scalar1=1.0 / D, scalar2=EPS,
                                                  op0=ALU.mult, op1=ALU.add), reads=[ss2], writes=[ss2])
            T.op("pool", lambda e: e.tensor_tensor(out=rs.t[:, 0:1], in0=ss2.t[:, 0:1], in1=nhalf.t[:], op=ALU.pow),
                 reads=[ss2, nhalf], writes=[rs])
            for hh in range(2):
                hs = slice(hh * 512, (hh + 1) * 512)
                T.op("dve", lambda e: e.tensor_tensor(out=junk.t[:, hs], in0=oh[hh].t[:], in1=GTb.t[:, hs], op=ALU.mult),
                     reads=[oh[hh], GTb], writes=[junk])
                T.op("dve", lambda e: e.scalar_tensor_tensor(out=outb.t[:, hs], in0=junk.t[:, hs], scalar=rs.t[:, 0:1],
                                                             in1=xres.t[:, hs], op0=ALU.mult, op1=ALU.add),
                     reads=[junk, rs, xres], writes=[outb])

        with contextlib.ExitStack() as esG0:
            for s_ in range(nseq):
                GT[0][s_] = Buf(esG0.enter_context(nc.sbuf_tensor("GT0_%d" % s_, [128, D], F32)), "GT0_%d" % s_)
            with contextlib.ExitStack() as es0:
                def sb0(name, shape, dt=F32):
                    return Buf(es0.enter_context(nc.sbuf_tensor(name, list(shape), dt)), name)

                nchunks = (NVEC + 127) // 128
                for ci in range(nchunks):
                    r0 = ci * 128
                    r = min(128, NVEC - r0)
                    vt = sb0("vec%d" % ci, [128, 128])
                    T.dma("sp", vt.t[0:r, :], vec[r0:r0 + r, :], writes=[vt])
                    pm = next_mm()
                    T.op("pe", lambda e: e.transpose(pm.t[:, 0:r], vt.t[0:r, :], ident_f.t[0:r, 0:r]),
                         reads=[vt, ident_f], writes=[pm])
                    T.op("dve", lambda e: e.tensor_copy(cols.t[:, r0:r0 + r], pm.t[:, 0:r]), reads=[pm], writes=[cols])

                o_ba, o_bx, o_lam = VOFF["b_lru_a"], VOFF["b_lru_x"], VOFF["lam"]
                T.op("dve", lambda e: e.tensor_scalar(out=dcols.t[:, 0:16], in0=cols.t[:, o_ba:o_ba + 16], scalar1=0.5,
                                                      scalar2=None, op0=ALU.mult), reads=[cols], writes=[dcols])
                T.op("dve", lambda e: e.tensor_scalar(out=dcols.t[:, 16:32], in0=cols.t[:, o_bx:o_bx + 16], scalar1=0.5,
                                                      scalar2=None, op0=ALU.mult), reads=[cols], writes=[dcols])
                T.op("act", lambda e: e.activation(out=dcols.t[:, 64:80], in_=cols.t[:, o_lam:o_lam + 16], func=AF.Exp,
                                                   scale=-1.0), reads=[cols], writes=[dcols])
                T.op("act", lambda e: e.activation(out=dcols.t[:, 80:96], in_=dcols.t[:, 64:80], func=AF.Ln,
                                                   bias=1.0, scale=1.0), reads=[dcols], writes=[dcols])
                T.op("dve", lambda e: e.tensor_scalar(out=dcols.t[:, 32:48], in0=dcols.t[:, 80:96], scalar1=-4.0,
                                                      scalar2=None, op0=ALU.mult), reads=[dcols], writes=[dcols])
                T.op("dve", lambda e: e.tensor_scalar(out=dcols.t[:, 48:64], in0=dcols.t[:, 80:96], scalar1=-8.0,
                                                      scalar2=None, op0=ALU.mult), reads=[dcols], writes=[dcols])
                siluc = sb0("siluc", [128, 8 * nseq])
                oc = VOFF["c"]
                T.op("act", lambda e: e.activation(out=siluc.t[:], in_=cols.t[:, oc:oc + 8 * nseq], func=AF.Silu),
                     reads=[cols], writes=[siluc])
                silr = sb0("silr", [128, 8, nseq])
                for s in range(nseq):
                    T.op("dve", lambda e: e.tensor_copy(silr.t[:, :, s], siluc.t[:, s * 8:(s + 1) * 8]),
                         reads=[siluc], writes=[silr])
                bcs = []
                for s in range(nseq):
                    bc = sb0("bc%d" % s, [128, 8, 128])
                    for kc in range(8):
                        T.op("dve", lambda e: e.tensor_scalar(out=bc.t[:, kc, :], in0=ones_f.t[:],
                                                              scalar1=siluc.t[:, s * 8 + kc:s * 8 + kc + 1],
                                                              scalar2=None, op0=ALU.mult),
                             reads=[ones_f, siluc], writes=[bc])
                    bcs.append(bc)
                brow = sb0("brow", [128, 2, D])
                grow = sb0("grow", [128, 2, D])
                T.dma("sp", brow.t[:, 0, :], b_ada[:, 2 * D:3 * D].partition_broadcast(128), writes=[brow])
                T.dma("sp", brow.t[:, 1, :], b_ada[:, 5 * D:6 * D].partition_broadcast(128), writes=[brow])
                T.dma("sp", grow.t[:, 0, :], g_post_mix.partition_broadcast(128), writes=[grow])
                T.dma("sp", grow.t[:, 1, :], g_post_ffn.partition_broadcast(128), writes=[grow])

                wring = [sb0("wada%d" % i, [128, 8, 512]) for i in range(2)]
                w_ada3 = w_ada.rearrange("(kc p) n -> p kc n", p=128)
                oba = VOFF["b_ada"]
                for ch in range(12):
                    wt = wring[ch % 2]
                    T.dma("sp", wt.t[:], w_ada3[:, :, ch * 512:(ch + 1) * 512], writes=[wt])
                    which = {4: 0, 5: 0, 10: 1, 11: 1}.get(ch)
                    if which is None:
                        for j in range(4):
                            nt = ch * 4 + j
                            pm = next_mm()
                            for kc in range(8):
                                T.op("pe", lambda e: e.matmul(pm.t[:, 0:nseq], lhsT=wt.t[:, kc, j * 128:(j + 1) * 128],
                                                              rhs=silr.t[:, kc, :], start=(kc == 0), stop=(kc == 7)),
                                     reads=[wt, silr], writes=[pm])
                            T.op("dve", lambda e: e.tensor_scalar(out=modc.t[:, nt, :], in0=pm.t[:, 0:nseq],
                                                                  scalar1=cols.t[:, oba + nt:oba + nt + 1], scalar2=None,
                                                                  op0=ALU.add), reads=[pm, cols], writes=[modc])
                    else:
                        half = (ch % 2)
                        if ch in (4, 10):
                            half = 0
                        else:
                            half = 1
                        fac = 0.5 if which == 0 else 1.0
                        for s in range(nseq):
                            pm = next_mm()
                            for kc in range(8):
                                T.op("pe", lambda e: e.matmul(pm.t[:], lhsT=bcs[s].t[:, kc, :], rhs=wt.t[:, kc, :],
                                                              start=(kc == 0), stop=(kc == 7)),
                                     reads=[bcs[s], wt], writes=[pm])
                            dst = GT[which][s]
                            sl = slice(half * 512, (half + 1) * 512)
                            T.op("dve", lambda e: e.tensor_tensor(out=dst.t[:, sl], in0=pm.t[:], in1=brow.t[:, which, sl],
                                                                  op=ALU.add), reads=[pm, brow], writes=[dst])
                            T.op("dve", lambda e: e.scalar_tensor_tensor(out=dst.t[:, sl], in0=dst.t[:, sl], scalar=fac,
                                                                         in1=grow.t[:, which, sl], op0=ALU.mult,
                                                                         op1=ALU.mult), reads=[dst, grow], writes=[dst])
                og1, og2 = VOFF["g_pre_mix"], VOFF["g_pre_ffn"]
                for ft in range(8):
                    T.op("dve", lambda e: e.tensor_scalar(out=G1.t[:, ft, :], in0=modc.t[:, 8 + ft, :], scalar1=1.0,
                                                          scalar2=cols.t[:, og1 + ft:og1 + ft + 1], op0=ALU.add,
                                                          op1=ALU.mult), reads=[modc, cols], writes=[G1])
                    T.op("dve", lambda e: e.tensor_scalar(out=G2.t[:, ft, :], in0=modc.t[:, 32 + ft, :], scalar1=1.0,
                                                          scalar2=cols.t[:, og2 + ft:og2 + ft + 1], op0=ALU.add,
                                                          op1=ALU.mult), reads=[modc, cols], writes=[G2])
                T.finish("sp")

            def norm_transpose(xt, hT, col0, Gm, SHbase, s, xn_ring, ss_ring, junk):
                ss = ss_ring.next()
                rs = ss_ring.next()
                xn = xn_ring.next()
                T.op("act", lambda e: e.activation(out=xn.t[:], in_=xt.t[:], func=AF.Square, accum_out=ss.t[:]),
                     reads=[xt], writes=[xn, ss])
                rstd_from_ss(ss, rs)
                T.op("act", lambda e: e.activation(out=xn.t[:], in_=xt.t[:], func=AF.Copy, scale=rs.t[:]),
                     reads=[xt, rs], writes=[xn])
                pt = next_tr()
                for ft in range(8):
                    T.op("pe", lambda e: e.transpose(pt.t[:, ft * 128:(ft + 1) * 128], xn.t[:, ft * 128:(ft + 1) * 128],
                                                     ident_b.t[:]), reads=[xn, ident_b], writes=[pt])
                for ft in range(8):
                    T.op("dve", lambda e: e.tensor_scalar(out=hT.t[:, ft, col0:col0 + 128],
                                                          in0=pt.t[:, ft * 128:(ft + 1) * 128],
                                                          scalar1=Gm.t[:, ft, s:s + 1],
                                                          scalar2=modc.t[:, SHbase + ft, s:s + 1],
                                                          op0=ALU.mult, op1=ALU.add),
                         reads=[pt, Gm, modc], writes=[hT])

            stop_here("0")
            with contextlib.ExitStack() as esA:
                def sbA(name, shape, dt=F32):
                    return Buf(esA.enter_context(nc.sbuf_tensor(name, list(shape), dt)), name)

                class RingA(Ring):
                    def __init__(self, name, shape, dt, n):
                        self.b = [sbA("%s%d" % (name, i), shape, dt) for i in range(n)]
                        self.i = 0

                win = sbA("win", [128, 8, DIN], BF16)
                with contextlib.ExitStack() as esS:
                    stg = Ring.__new__(Ring)
                    stg.b = [Buf(esS.enter_context(nc.sbuf_tensor("stgA%d" % i, [128, 2304], F32)), "stgA%d" % i) for i in range(2)]
                    stg.i = 0
                    load_w_bf16(win, w_in.rearrange("(kc p) n -> p kc n", p=128), 8, DIN, stg, nchunk=2304)
                    T.finish("sp")
                xring = RingA("xA", [128, D], F32, 2)
                xnring = RingA("xnA", [128, D], BF16, 2)
                ssring = RingA("ssA", [128, 1], F32, 8)
                junk = None
                hring = RingA("hT", [128, 8, TB], BF16, 2)
                oring = RingA("oA", [128, 4, TB], BF16, 3)
                xbring = RingA("xbA", [128, 512], BF16, 2)

                def norm_tile_A(s, t0n, i, hTn):
                    xt = xring.next()
                    r0 = seq_off[s] + t0n + i * 128
                    T.dma("sp", xt.t[:], xin[r0:r0 + 128, :], writes=[xt])
                    norm_transpose(xt, hTn, i * 128, G1, 0, s, xnring, ssring, junk)

                blocksA = [(s, t0) for s, S in enumerate(seq_lens) for t0 in range(0, S, TB)]
                hT_next = hring.next()
                for i in range(4):
                    norm_tile_A(blocksA[0][0], blocksA[0][1], i, hT_next)
                for bi, (s, t0) in enumerate(blocksA):
                    S = seq_lens[s]
                    if True:
                        hT = hT_next
                        nxt = blocksA[bi + 1] if bi + 1 < len(blocksA) else None
                        if nxt is not None:
                            hT_next = hring.next()
                        for grp in range(9):
                            if nxt is not None and grp in (1, 3, 5, 7):
                                norm_tile_A(nxt[0], nxt[1], (grp - 1) // 2, hT_next)
                            if grp == 4:
                                continue
                            ot = oring.next()
                            for j4 in range(4):
                                j = grp * 4 + j4
                                pm = next_mm()
                                for kc in range(8):
                                    T.op("pe", lambda e: e.matmul(pm.t[:], lhsT=win.t[:, kc, j * 128:(j + 1) * 128],
                                                                  rhs=hT.t[:, kc, :], start=(kc == 0), stop=(kc == 7)),
                                         reads=[win, hT], writes=[pm])
                                if grp < 2:
                                    T.op("dve", lambda e: e.tensor_copy(ot.t[:, j4, :], pm.t[:]), reads=[pm], writes=[ot])
                                elif grp < 4:
                                    T.op("act", lambda e: e.activation(out=ot.t[:, j4, :], in_=pm.t[:],
                                                                       func=AF.Gelu_apprx_tanh), reads=[pm], writes=[ot])
                                else:
                                    T.op("act", lambda e: e.activation(out=ot.t[:, j4, :], in_=pm.t[:], func=AF.Tanh,
                                                                       scale=0.5), reads=[pm], writes=[ot])
                            if grp < 2:
                                dstT, f0 = XA[s], grp * 4
                            elif grp < 4:
                                dstT, f0 = GY[s], (grp - 2) * 4
                            elif grp < 7:
                                dstT, f0 = TA[s], (grp - 5) * 4
                            else:
                                dstT, f0 = TBG[s], (grp - 7) * 4
                            T.dma("pool", fm(dstT)[:, f0:f0 + 4, t0:t0 + TB], ot.t[:], reads=[ot])
                        for i in range(4):
                            pm = next_mm()
                            for kc in range(8):
                                T.op("pe", lambda e: e.matmul(pm.t[:], lhsT=hT.t[:, kc, i * 128:(i + 1) * 128],
                                                              rhs=win.t[:, kc, 2048:2560], start=(kc == 0), stop=(kc == 7)),
                                     reads=[hT, win], writes=[pm])
                            xb = xbring.next()
                            T.op("dve", lambda e: e.tensor_copy(xb.t[:], pm.t[:]), reads=[pm], writes=[xb])
                            T.dma("pool", XB[s][t0 + i * 128:t0 + (i + 1) * 128, :], xb.t[:], reads=[xb])
                T.finish("sp")
            T.dma_barrier()

            stop_here("A")
            with contextlib.ExitStack() as esD:
                def sbD(name, shape, dt=F32):
                    return Buf(esD.enter_context(nc.sbuf_tensor(name, list(shape), dt)), name)

                class RingD(Ring):
                    def __init__(self, name, shape, dt, n):
                        self.b = [sbD("%s%d" % (name, i), shape, dt) for i in range(n)]
                        self.i = 0

                Smax = max(seq_lens)
                PQ = sbD("PQ", [128, 4, Smax], BF16)
                PQa = Buf(PQ.t, "PQa")
                PQd = Buf(PQ.t, "PQd")
                Wp = sbD("Wp", [128, 8, D], BF16)
                tAb = sbD("tAb", [128, 256], BF16)
                with contextlib.ExitStack() as esS:
                    def sbS(name, shape, dt=F32):
                        return Buf(esS.enter_context(nc.sbuf_tensor(name, list(shape), dt)), name)
                    tA = sbS("tA", [128, 256])
                    tW = sbS("tW", [128, 256])
                    wb = sbS("wb", [128, 4, D])
                    T.dma("sp", tA.t[:], tabA, writes=[tA])
                    T.dma("sp", tW.t[:], tabW, writes=[tW])
                    T.dma("sp", wb.t[:], w_b_out.rearrange("(g p) n -> p g n", p=128), writes=[wb])
                    T.op("dve", lambda e: e.tensor_copy(tAb.t[:], tA.t[:]), reads=[tA], writes=[tAb])
                    for g in range(4):
                        for pq in range(2):
                            for nh in range(2):
                                pm = next_mm()
                                T.op("pe", lambda e: e.matmul(pm.t[:], lhsT=tW.t[:, pq * 128:(pq + 1) * 128],
                                                              rhs=wb.t[:, g, nh * 512:(nh + 1) * 512], start=True, stop=True),
                                     reads=[tW, wb], writes=[pm])
                                T.op("dve", lambda e: e.tensor_copy(Wp.t[:, g * 2 + pq, nh * 512:(nh + 1) * 512], pm.t[:]),
                                     reads=[pm], writes=[Wp])
                    T.finish("sp")
                twt = {}
                krb = {}
                for L in Ls:
                    twt[L] = sbD("twsb%d" % L, [128, 3 * L])
                    T.dma("sp", twt[L].t[:], tw_in[L], writes=[twt[L]])
                    kf = sbD("krf%d" % L, [128, 512])
                    T.dma("sp", kf.t[:], kron_in[L], writes=[kf])
                    krb[L] = sbD("krb%d" % L, [128, 512], BF16)
                    T.op("dve", lambda e: e.tensor_copy(krb[L].t[:], kf.t[:]), reads=[kf], writes=[krb[L]])

                stop_here("D0")
                BCH = 8
                d1ring = RingD("d1", [128, BCH, 512], BF16, 2)
                t12 = RingD("t12", [128, 2, 512], F32, 2)
                ytring = RingD("yt", [128, 2, 512], BF16, 3)
                zring = RingD("z", [128, 2, 512], BF16, 3)
                tbring = RingD("tbD", [128, 4, TB], BF16, 2)
                ybring = RingD("ybD", [128, 4, TB], BF16, 2)
                pqring = RingD("pqD", [128, 8, TB], BF16, 2)

                for s, S in enumerate(seq_lens):
                    L = S // 128
                    Q = 128 // L
                    tw = twt[L]
                    xb3 = XB[s].rearrange("(a b) c -> a b c", b=L)
                    yt3 = YT[s].rearrange("(a b) c -> a b c", b=L)
                    for b0 in range(0, L, BCH):
                        d1 = d1ring.next()
                        T.dma("sp", d1.t[:], xb3[:, b0:b0 + BCH, :], writes=[d1])
                        for bb in range(BCH):
                            beta = b0 + bb
                            pr = next_mm()
                            pi = next_mm()
                            T.op("pe", lambda e: e.matmul(pr.t[:], lhsT=tAb.t[:, 0:128], rhs=d1.t[:, bb, :], start=True,
                                                          stop=True), reads=[tAb, d1], writes=[pr])
                            T.op("pe", lambda e: e.matmul(pi.t[:], lhsT=tAb.t[:, 128:256], rhs=d1.t[:, bb, :], start=True,
                                                          stop=True), reads=[tAb, d1], writes=[pi])
                            tt = t12.next()
                            tc_ = tw.t[:, beta:beta + 1]
                            ts_ = tw.t[:, L + beta:L + beta + 1]
                            nts_ = tw.t[:, 2 * L + beta:2 * L + beta + 1]
                            T.op("act", lambda e: e.activation(out=tt.t[:, 0, :], in_=pr.t[:], func=AF.Copy, scale=tc_),
                                 reads=[pr, tw], writes=[tt])
                            T.op("act", lambda e: e.activation(out=tt.t[:, 1, :], in_=pi.t[:], func=AF.Copy, scale=tc_),
                                 reads=[pi, tw], writes=[tt])
                            yt = ytring.next()
                            T.op("dve", lambda e: e.scalar_tensor_tensor(out=yt.t[:, 0, :], in0=pi.t[:], scalar=nts_,
                                                                         in1=tt.t[:, 0, :], op0=ALU.mult, op1=ALU.add),
                                 reads=[pi, tw, tt], writes=[yt])
                            T.op("dve", lambda e: e.scalar_tensor_tensor(out=yt.t[:, 1, :], in0=pr.t[:], scalar=ts_,
                                                                         in1=tt.t[:, 1, :], op0=ALU.mult, op1=ALU.add),
                                 reads=[pr, tw, tt], writes=[yt])
                            T.dma("pool", yt3[:, beta, :], yt.t[:].rearrange("p a c -> p (a c)"), reads=[yt])
                    T.dma_barrier()
                    stop_here("D1")
                    kr = krb[L]
                    for g2 in range(2):
                      for ka in range(L):
                        z = zring.next()
                        T.dma("sp", z.t[:].rearrange("p a c -> p (a c)"), YT[s][ka * 128:(ka + 1) * 128, :], writes=[z])
                        if True:
                            pms = [next_mm(), next_mm()]
                            for gg in range(2):
                                g = g2 * 2 + gg
                                pm = pms[gg]
                                T.op("pe", lambda e: e.matmul(pm.t[:, 0:256], lhsT=z.t[:, 0, g * 128:(g + 1) * 128],
                                                              rhs=kr.t[:, 0:256], start=True, stop=False),
                                     reads=[z, kr], writes=[pm])
                                T.op("pe", lambda e: e.matmul(pm.t[:, 0:256], lhsT=z.t[:, 1, g * 128:(g + 1) * 128],
                                                              rhs=kr.t[:, 256:512], start=False, stop=True),
                                     reads=[z, kr], writes=[pm])
                                src = pm.t[:, 0:256].rearrange("p (c b q) -> p c b q", c=2, q=Q)
                                dst = PQ.t[:, gg * 2:gg * 2 + 2, 0:S].rearrange("p c (b r) -> p c b r", r=128)[:, :, :, Q * ka:Q * ka + Q]
                                T.op("dve", lambda e: e.tensor_copy(dst, src), reads=[pm], writes=[PQd])
                      for c4 in range(4):
                        T.dma("pool", PQD[s][(g2 * 4 + c4) * 128:(g2 * 4 + c4 + 1) * 128, :], PQ.t[:, c4, 0:S], reads=[PQa, PQd], key="PQ")
                    T.dma_barrier()
                    stop_here("D2")
                    for t0 in range(0, S, TB):
                        pqb = pqring.next()
                        T.dma("sp", pqb.t[:], fm(PQD[s])[:, :, t0:t0 + TB], writes=[pqb])
                        for nh in range(2):
                            tb = tbring.next()
                            T.dma("sp", tb.t[:], fm(TBG[s])[:, nh * 4:(nh + 1) * 4, t0:t0 + TB], writes=[tb])
                            yb = ybring.next()
                            for j4 in range(4):
                                nt = nh * 4 + j4
                                pm = next_mm()
                                for k8 in range(8):
                                    T.op("pe", lambda e: e.matmul(pm.t[:], lhsT=Wp.t[:, k8, nt * 128:(nt + 1) * 128],
                                                                  rhs=pqb.t[:, k8, :], start=(k8 == 0), stop=(k8 == 7)),
                                         reads=[Wp, pqb], writes=[pm])
                                T.op("dve", lambda e: e.scalar_tensor_tensor(out=yb.t[:, j4, :], in0=tb.t[:, j4, :], scalar=1.0,
                                                                             in1=pm.t[:], op0=ALU.add, op1=ALU.mult),
                                     reads=[tb, pm], writes=[yb])
                            T.dma("pool", fm(YBG[s])[:, nh * 4:(nh + 1) * 4, t0:t0 + TB], yb.t[:], reads=[yb])
                T.finish("sp")
            T.dma_barrier()

            stop_here("D")
            with contextlib.ExitStack() as esB:
                def sbB(name, shape, dt=F32):
                    return Buf(esB.enter_context(nc.sbuf_tensor(name, list(shape), dt)), name)

                def mkring(stack, name, shape, dt, n):
                    r = Ring.__new__(Ring)
                    r.b = [Buf(stack.enter_context(nc.sbuf_tensor("%s%d" % (name, i), list(shape), dt)), "%s%d" % (name, i))
                           for i in range(n)]
                    r.i = 0
                    return r

                wla = sbB("wla", [128, 16, 128], BF16)
                wlx = sbB("wlx", [128, 16, 128], BF16)
                wao = sbB("wao", [128, 8, D], BF16)
                wo = sbB("wo", [128, 8, D], BF16)
                stg = mkring(esB, "stgB", [128, D], F32, 2)
                for (dst, src) in ((wla, w_lru_a), (wlx, w_lru_x)):
                    src3 = src.rearrange("h i j -> i h j")
                    for h0 in range(0, 16, 8):
                        st = stg.next()
                        T.dma("sp", st.t[:].rearrange("p (h j) -> p h j", h=8), src3[:, h0:h0 + 8, :], writes=[st])
                        T.op("dve", lambda e: e.tensor_copy(dst.t[:, h0:h0 + 8, :],
                                                            st.t[:].rearrange("p (h j) -> p h j", h=8)),
                             reads=[st], writes=[dst])
                load_w_bf16(wao, w_a_out.rearrange("(kc p) n -> p kc n", p=128), 8, D, stg, nchunk=1024)
                load_w_bf16(wo, w_o.rearrange("(kc p) n -> p kc n", p=128), 8, D, stg, nchunk=1024)
                ocw = VOFF["conv_w"]
                ocb = VOFF["conv_b"]
                carry = sbB("carry", [128, 8])

                xcring = mkring(esB, "xc", [128, 8, TB], BF16, 2)
                trr = mkring(esB, "trB", [128, TB], F32, 1)
                br = mkring(esB, "bB", [128, TB], F32, 2)
                hr = mkring(esB, "hB", [128, TB], F32, 2)
                a2all_t = esB.enter_context(nc.sbuf_tensor("a2all", [128, 8, TB], F32))
                aall_t = esB.enter_context(nc.sbuf_tensor("aall", [128, 8, TB], F32))
                mall_t = esB.enter_context(nc.sbuf_tensor("mall", [128, 8, TB], F32))
                tiall_t = [esB.enter_context(nc.sbuf_tensor("tiall%d" % k, [128, 8, TB], BF16)) for k in range(2)]
                a2s = [Buf(a2all_t, "a2s%d" % i) for i in range(8)]
                aas = [Buf(aall_t, "aas%d" % i) for i in range(8)]
                mms = [Buf(mall_t, "mms%d" % i) for i in range(8)]
                tis = [[Buf(tiall_t[k], "tis%d_%d" % (k, i)) for i in range(8)] for k in range(2)]

                def lru_phase1(xc, d, k):
                    for ft in range(8):
                        pr = next_mm()
                        pi = next_mm()
                        T.op("pe", lambda e: e.matmul(pr.t[:], lhsT=wla.t[:, d * 8 + ft, :], rhs=xc.t[:, ft, :], start=True,
                                                      stop=True), reads=[wla, xc], writes=[pr])
                        T.op("pe", lambda e: e.matmul(pi.t[:], lhsT=wlx.t[:, d * 8 + ft, :], rhs=xc.t[:, ft, :], start=True,
                                                      stop=True), reads=[wlx, xc], writes=[pi])
                        ci = d * 8 + ft
                        t_r = trr.next()
                        T.op("act", lambda e: e.activation(out=t_r.t[:], in_=pr.t[:], func=AF.Tanh, scale=0.5,
                                                           bias=dcols.t[:, ci:ci + 1]), reads=[pr, dcols], writes=[t_r])
                        T.op("act", lambda e: e.activation(out=tiall_t[k][:, ft, :], in_=pi.t[:], func=AF.Tanh, scale=0.5,
                                                           bias=dcols.t[:, 16 + ci:16 + ci + 1]), reads=[pi, dcols],
                             writes=[tis[k][ft]])
                        T.op("act", lambda e: e.activation(out=a2all_t[:, ft, :], in_=t_r.t[:], func=AF.Exp,
                                                           scale=dcols.t[:, 48 + ci:48 + ci + 1],
                                                           bias=dcols.t[:, 48 + ci:48 + ci + 1]), reads=[t_r, dcols],
                             writes=[a2s[ft]])

                def lru_sqrt():
                    for ft in range(8):
                        T.op("act", lambda e: e.activation(out=aall_t[:, ft, :], in_=a2all_t[:, ft, :], func=AF.Sqrt),
                             reads=[a2s[ft]], writes=[aas[ft]])
                        T.op("act", lambda e: e.activation(out=mall_t[:, ft, :], in_=a2all_t[:, ft, :], func=AF.Sqrt,
                                                           scale=-1.0, bias=1.0), reads=[a2s[ft]], writes=[mms[ft]])

                def lru_dve(xc, k, reverse, post):
                    for ft in range(8):
                        b = br.next()
                        T.op("dve", lambda e: e.scalar_tensor_tensor(out=b.t[:], in0=tiall_t[k][:, ft, :], scalar=1.0,
                                                                     in1=mall_t[:, ft, :], op0=ALU.add, op1=ALU.mult),
                             reads=[tis[k][ft], mms[ft]], writes=[b])
                        T.op("dve", lambda e: e.scalar_tensor_tensor(out=b.t[:], in0=b.t[:], scalar=0.5, in1=xc.t[:, ft, :],
                                                                     op0=ALU.mult, op1=ALU.mult), reads=[b, xc], writes=[b])
                        ec = (TB - 1) if reverse else 0
                        T.op("dve", lambda e: e.scalar_tensor_tensor(out=b.t[:, ec:ec + 1], in0=aall_t[:, ft, ec:ec + 1],
                                                                     scalar=carry.t[:, ft:ft + 1], in1=b.t[:, ec:ec + 1],
                                                                     op0=ALU.mult, op1=ALU.add),
                             reads=[aas[ft], b, carry], writes=[b])
                        h = hr.next()
                        if not reverse:
                            T.op("dve", lambda e: e.tensor_tensor_scan(out=h.t[:], data0=aall_t[:, ft, :], data1=b.t[:],
                                                                       initial=0.0, op0=ALU.mult, op1=ALU.add),
                                 reads=[aas[ft], b], writes=[h])
                            T.op("dve", lambda e: e.tensor_copy(carry.t[:, ft:ft + 1], h.t[:, TB - 1:TB]),
                                 reads=[h], writes=[carry])
                        else:
                            T.op("dve", lambda e: e.tensor_tensor_scan(out=h.t[:, ::-1], data0=aall_t[:, ft, ::-1],
                                                                       data1=b.t[:, ::-1], initial=0.0, op0=ALU.mult,
                                                                       op1=ALU.add), reads=[aas[ft], b], writes=[h])
                            T.op("dve", lambda e: e.tensor_copy(carry.t[:, ft:ft + 1], h.t[:, 0:1]),
                                 reads=[h], writes=[carry])
                        post(ft, h)

                with contextlib.ExitStack() as esF:
                    dg = Buf(esF.enter_context(nc.sbuf_tensor("dg", [128, 4, 8, 128], BF16)), "dg")
                    for k in range(4):
                        for ft in range(8):
                            T.op("dve", lambda e: e.tensor_scalar(out=dg.t[:, k, ft, :], in0=ident_f.t[:],
                                                                  scalar1=cols.t[:, ocw + k * 8 + ft:ocw + k * 8 + ft + 1],
                                                                  scalar2=None, op0=ALU.mult), reads=[ident_f, cols], writes=[dg])
                    xaring = mkring(esF, "xah", [128, 8, TB + 3], BF16, 2)
                    hbring = mkring(esF, "hbf", [128, 8, TB], BF16, 2)
                    blocksF = [(s, t0) for s, S in enumerate(seq_lens) for t0 in range(0, S, TB)]

                    def prep_f(bi):
                        s, t0 = blocksF[bi]
                        S = seq_lens[s]
                        xa = xaring.next()
                        lo, hi = max(0, t0 - 2), min(S, t0 + TB + 1)
                        if lo != t0 - 2 or hi != t0 + TB + 1:
                            T.op("dve", lambda e: e.memset(xa.t[:], 0.0), writes=[xa])
                        c0 = lo - (t0 - 2)
                        T.dma("sp", xa.t[:, :, c0:c0 + (hi - lo)], fm(XA[s])[:, :, lo:hi], writes=[xa])
                        xc = xcring.next()
                        for ft in range(8):
                            pm = next_mm()
                            for k in range(4):
                                T.op("pe", lambda e: e.matmul(pm.t[:], lhsT=dg.t[:, k, ft, :], rhs=xa.t[:, ft, k:k + TB],
                                                              start=(k == 0), stop=(k == 3)), reads=[dg, xa], writes=[pm])
                            T.op("act", lambda e: e.activation(out=xc.t[:, ft, :], in_=pm.t[:], func=AF.Identity,
                                                               bias=cols.t[:, ocb + ft:ocb + ft + 1], scale=1.0),
                                 reads=[pm, cols], writes=[xc])
                        T.dma("pool", fm(XC[s])[:, :, t0:t0 + TB], xc.t[:], reads=[xc])
                        lru_phase1(xc, 0, bi % 2)
                        return xc, bi % 2

                    cur = prep_f(0)
                    for bi, (s, t0) in enumerate(blocksF):
                        xc, k = cur
                        if t0 == 0:
                            T.op("dve", lambda e: e.memset(carry.t[:], 0.0), writes=[carry])
                        lru_sqrt()
                        if bi + 1 < len(blocksF):
                            cur = prep_f(bi + 1)
                        hb = hbring.next()

                        def post_f(ft, h, hb=hb):
                            T.op("dve", lambda e: e.tensor_copy(hb.t[:, ft, :], h.t[:]), reads=[h], writes=[hb])
                        lru_dve(xc, k, False, post_f)
                        T.dma("pool", fm(HF[s])[:, :, t0:t0 + TB], hb.t[:], reads=[hb])
                    T.finish("sp")
                T.dma_barrier()

                with contextlib.ExitStack() as esR:
                    hfring = mkring(esR, "hfB", [128, 8, TB], BF16, 1)
                    gyring = mkring(esR, "gyB", [128, 8, TB], BF16, 1)
                    taring = mkring(esR, "taB", [128, 8, TB], BF16, 1)
                    ybgring = mkring(esR, "ybgB", [128, 8, TB], BF16, 1)
                    tmpr = mkring(esR, "tmpB", [128, TB], F32, 1)
                    x1ring = mkring(esR, "x1B", [128, D], F32, 1)
                    ssring = mkring(esR, "ssB", [128, 2], F32, 8)
                    xring = stg
                    blocksR = [(s, t0) for s, S in enumerate(seq_lens) for t0 in range(S - TB, -1, -TB)]

                    def prep_b(bi):
                        s, t0 = blocksR[bi]
                        xc = xcring.next()
                        T.dma("sp", xc.t[:], fm(XC[s])[:, :, t0:t0 + TB], writes=[xc])
                        lru_phase1(xc, 1, bi % 2)
                        return xc, bi % 2

                    cur = prep_b(0)
                    for bi, (s, t0) in enumerate(blocksR):
                        S = seq_lens[s]
                        xc, k = cur
                        if t0 == S - TB:
                            T.op("dve", lambda e: e.memset(carry.t[:], 0.0), writes=[carry])
                        lru_sqrt()
                        if bi + 1 < len(blocksR):
                            cur = prep_b(bi + 1)
                        hf = hfring.next()
                        T.dma("sp", hf.t[:], fm(HF[s])[:, :, t0:t0 + TB], writes=[hf])
                        gy = gyring.next()
                        T.dma("sp", gy.t[:], fm(GY[s])[:, :, t0:t0 + TB], writes=[gy])
                        ta = taring.next()
                        T.dma("sp", ta.t[:], fm(TA[s])[:, :, t0:t0 + TB], writes=[ta])
                        ybg = ybgring.next()
                        T.dma("sp", ybg.t[:], fm(YBG[s])[:, :, t0:t0 + TB], writes=[ybg])

                        def post_b(ft, h, hf=hf, gy=gy):
                            tmp = tmpr.next()
                            T.op("dve", lambda e: e.tensor_tensor(out=tmp.t[:], in0=h.t[:], in1=hf.t[:, ft, :],
                                                                  op=ALU.add), reads=[h, hf], writes=[tmp])
                            T.op("dve", lambda e: e.tensor_tensor(out=gy.t[:, ft, :], in0=tmp.t[:], in1=gy.t[:, ft, :],
                                                                  op=ALU.mult), reads=[tmp, gy], writes=[gy])
                        lru_dve(xc, k, True, post_b)
                        z = gy
                        for nt in range(8):
                            pm = next_mm()
                            for kc in range(8):
                                T.op("pe", lambda e: e.matmul(pm.t[:], lhsT=wao.t[:, kc, nt * 128:(nt + 1) * 128],
                                                              rhs=z.t[:, kc, :], start=(kc == 0), stop=(kc == 7)),
                                     reads=[wao, z], writes=[pm])
                            tmp = tmpr.next()
                            T.op("dve", lambda e: e.scalar_tensor_tensor(out=tmp.t[:], in0=ta.t[:, nt, :], scalar=1.0,
                                                                         in1=pm.t[:], op0=ALU.add, op1=ALU.mult),
                                 reads=[ta, pm], writes=[tmp])
                            T.op("dve", lambda e: e.tensor_tensor(out=ybg.t[:, nt, :], in0=tmp.t[:], in1=ybg.t[:, nt, :],
                                                                  op=ALU.add), reads=[tmp, ybg], writes=[ybg])
                        m = ybg
                        for i in range(4):
                            oh = [next_mm(), next_mm()]
                            for nh in range(2):
                                for kc in range(8):
                                    T.op("pe", lambda e: e.matmul(oh[nh].t[:],
                                                                  lhsT=m.t[:, kc, i * 128:(i + 1) * 128],
                                                                  rhs=wo.t[:, kc, nh * 512:(nh + 1) * 512],
                                                                  start=(kc == 0), stop=(kc == 7)),
                                         reads=[m, wo], writes=[oh[nh]])
                            xt = xring.next()
                            r0 = seq_off[s] + t0 + i * 128
                            T.dma("sp", xt.t[:], xin[r0:r0 + 128, :], writes=[xt])
                            ss = ssring.next()
                            rs = ssring.next()
                            x1 = x1ring.next()
                            tm_epilogue(oh, 0.5, ss, rs, x1, GT[0][s], xt, x1)
                            T.dma("pool", X1[s][t0 + i * 128:t0 + (i + 1) * 128, :], x1.t[:], reads=[x1])
                    T.finish("sp")
                T.finish("sp")
            T.dma_barrier()

        stop_here("B")
        with contextlib.ExitStack() as esC:
            def sbC(name, shape, dt=F32):
                return Buf(esC.enter_context(nc.sbuf_tensor(name, list(shape), dt)), name)

            class RingC(Ring):
                def __init__(self, name, shape, dt, n):
                    self.b = [sbC("%s%d" % (name, i), shape, dt) for i in range(n)]
                    self.i = 0

            wup = sbC("wup", [128, 8, 2 * DFF], BF16)
            wdn = sbC("wdn", [128, NFF, D], BF16)
            stg = RingC("stgC", [128, D], F32, 2)
            load_w_bf16(wup, w_up.rearrange("(kc p) n -> p kc n", p=128), 8, 2 * DFF, stg, nchunk=1024)
            load_w_bf16(wdn, w_down.rearrange("(kc p) n -> p kc n", p=128), NFF, D, stg, nchunk=1024)
            xring = Ring.__new__(Ring)
            xring.b = [stg.b[0]]
            xring.i = 0
            xnring = RingC("xnC", [128, D], BF16, 1)
            ssring = RingC("ssC", [128, 1], F32, 8)
            hT2 = RingC("h2T", [128, 8, TB], BF16, 2)
            U = RingC("U", [128, 2, TB + 2], BF16, 2)
            UP = sbC("UP", [128, NFF, 2, 2], BF16)
            tv = RingC("tv", [128, TB], F32, 2)
            tg = RingC("tg", [128, TB], F32, 1)
            gg = RingC("gg", [128, TB], BF16, 1)
            GTt = sbC("GTt", [128, NFF, TB], BF16)
            x1l = Ring.__new__(Ring)
            x1l.b = [stg.b[1]]
            x1l.i = 0
            ofw, ofb = VOFF["ffn_conv_w"], VOFF["ffn_conv_b"]
            x1l0 = x1l.b[0]
            T.op("pool", lambda e: e.memset(x1l0.t[:], 0.0), writes=[x1l0])

            ssring2 = RingC("ssC2", [128, 2], F32, 8)
            xjunk = stg.b[0]

            SEGP = 128
            if seg is not None:
                NTV = (seg[1] + 2 * SEGP) // 128
                idx_sb = sbC("segidx_sb", [128, NTV], mybir.dt.int32)
                idxl_sb = sbC("segidxl_sb", [128, NTV + 1], mybir.dt.int32)
                flag_sb = sbC("segflag_sb", [128, 2])
                T.dma("sp", idx_sb.t[:], segidx, writes=[idx_sb])
                T.dma("sp", idxl_sb.t[:], segidxl, writes=[idxl_sb])
                T.dma("sp", flag_sb.t[:], segflag, writes=[flag_sb])

            def gather_rows(dst, src_rows, idx_ap, idx_buf):
                T._deps("pool", [idx_buf], [dst])
                key = (dst.name, "pool")
                if key not in T.dsem:
                    T.dsem[key] = [es.enter_context(nc.semaphore("d%d" % len(T.dsem))), 0]
                d = T.dsem[key]
                inst = nc.gpsimd.indirect_dma_start(out=dst.t[:], out_offset=None, in_=src_rows,
                                                    in_offset=bass.IndirectOffsetOnAxis(ap=idx_ap, axis=0))
                d[1] += 16
                inst.then_inc(d[0], 16)
                T._mark(("dma", key), (("dma", key), d[1], d[0]), [idx_buf], [dst])

            def emit_norm_tile(s, t0n, i, h2, segm):
                xt = xring.next()
                if not segm:
                    T.dma("sp", xt.t[:], X1[s][t0n + i * 128:t0n + (i + 1) * 128, :], writes=[xt])
                else:
                    tl = (t0n + i * 128) // 128
                    gather_rows(xt, X1[s], idx_sb.t[:, tl:tl + 1], idx_sb)
                norm_transpose(xt, h2, i * 128, G2, 24, s, xnring, ssring, None)
                if segm:
                    tl = (t0n + i * 128) // 128
                    fl = 0 if tl == 0 else (1 if tl == NTV - 1 else None)
                    if fl is not None:
                        T.op("dve", lambda e: e.tensor_scalar(out=h2.t[:, :, i * 128:(i + 1) * 128],
                                                              in0=h2.t[:, :, i * 128:(i + 1) * 128],
                                                              scalar1=flag_sb.t[:, fl:fl + 1], scalar2=None, op0=ALU.mult),
                             reads=[h2, flag_sb], writes=[h2])

            for s, S in enumerate(seq_lens):
                segm = (seg is not None and seg[0] == s)
                if not segm:
                    blocks = [(j * TB, TB, False) for j in range(S // TB)] + [(S, 128, True)]
                else:
                    VL = seg[1] + 2 * SEGP
                    blocks = [(t, min(TB, VL - t), False) for t in range(0, VL, TB)]
                T.op("dve", lambda e: e.memset(UP.t[:], 0.0), writes=[UP])
                h2_next = hT2.next()
                for i in range(blocks[0][1] // 128):
                    emit_norm_tile(s, blocks[0][0], i, h2_next, segm)
                for bj, (t0, ncol, last) in enumerate(blocks):
                    h2 = h2_next
                    nxt = blocks[bj + 1] if (bj + 1 < len(blocks) and not blocks[bj + 1][2]) else None
                    if nxt is not None:
                        h2_next = hT2.next()
                    pend = None

                    def back(p):
                        pft, ptv, ptg = p
                        g_ = gg.next()
                        T.op("act", lambda e: e.activation(out=g_.t[:, 0:ncol], in_=ptg.t[:, 0:ncol],
                                                           func=AF.Gelu_apprx_tanh), reads=[ptg], writes=[g_])
                        return g_

                    def back2(p, g_):
                        pft, ptv, ptg = p
                        T.op("dve", lambda e: e.tensor_tensor(out=GTt.t[:, pft, 0:ncol], in0=g_.t[:, 0:ncol],
                                                              in1=ptv.t[:, 0:ncol], op=ALU.mult),
                             reads=[g_, ptv], writes=[GTt])

                    for ft in range(NFF):
                        u = U.next()
                        T.op("act", lambda e: e.activation(out=u.t[:, :, 0:2], in_=UP.t[:, ft, :, :], func=AF.Copy),
                             reads=[UP], writes=[u])
                        if not last:
                            for vg in range(2):
                                pm = next_mm()
                                c0 = vg * DFF + ft * 128
                                for kc in range(8):
                                    T.op("pe", lambda e: e.matmul(pm.t[:, 0:ncol], lhsT=wup.t[:, kc, c0:c0 + 128],
                                                                  rhs=h2.t[:, kc, 0:ncol], start=(kc == 0), stop=(kc == 7)),
                                         reads=[wup, h2], writes=[pm])
                                T.op("act", lambda e: e.activation(out=u.t[:, vg, 2:2 + ncol], in_=pm.t[:, 0:ncol], func=AF.Copy),
                                     reads=[pm], writes=[u])
                            T.op("act", lambda e: e.activation(out=UP.t[:, ft, :, :], in_=u.t[:, :, ncol:ncol + 2], func=AF.Copy),
                                 reads=[u], writes=[UP])
                        else:
                            T.op("dve", lambda e: e.memset(u.t[:, :, 2:2 + TB], 0.0), writes=[u])
                        tvv = tv.next()
                        tgg = tg.next()
                        cfv, cfg = ft, NFF + ft
                        T.op("act", lambda e: e.activation(out=tvv.t[:, 0:ncol], in_=u.t[:, 0, 0:ncol], func=AF.Identity,
                                                           scale=cols.t[:, ofw + cfv:ofw + cfv + 1],
                                                           bias=cols.t[:, ofb + cfv:ofb + cfv + 1]),
                             reads=[u, cols], writes=[tvv])
                        g_prev = back(pend) if pend is not None else None
                        T.op("act", lambda e: e.activation(out=tgg.t[:, 0:ncol], in_=u.t[:, 1, 0:ncol], func=AF.Identity,
                                                           scale=cols.t[:, ofw + cfg:ofw + cfg + 1],
                                                           bias=cols.t[:, ofb + cfg:ofb + cfg + 1]),
                             reads=[u, cols], writes=[tgg])
                        for k in (1, 2):
                            T.op("dve", lambda e: e.scalar_tensor_tensor(
                                out=tvv.t[:, 0:ncol], in0=u.t[:, 0, k:k + ncol],
                                scalar=cols.t[:, ofw + k * 44 + cfv:ofw + k * 44 + cfv + 1], in1=tvv.t[:, 0:ncol],
                                op0=ALU.mult, op1=ALU.add), reads=[u, cols, tvv], writes=[tvv])
                        if pend is not None:
                            back2(pend, g_prev)
                        for k in (1, 2):
                            T.op("dve", lambda e: e.scalar_tensor_tensor(
                                out=tgg.t[:, 0:ncol], in0=u.t[:, 1, k:k + ncol],
                                scalar=cols.t[:, ofw + k * 44 + cfg:ofw + k * 44 + cfg + 1], in1=tgg.t[:, 0:ncol],
                                op0=ALU.mult, op1=ALU.add), reads=[u, cols, tgg], writes=[tgg])
                        pend = (ft, tvv, tgg)
                        if nxt is not None and ft in (3, 7, 11, 15) and (ft - 3) // 4 < nxt[1] // 128:
                            emit_norm_tile(s, nxt[0], (ft - 3) // 4, h2_next, segm)
                    g_prev = back(pend)
                    back2(pend, g_prev)
                    for i in range(ncol // 128):
                        tk0 = t0 - 1 + i * 128
                        if not segm:
                            lo, hi = max(0, tk0), min(S, tk0 + 128)
                        else:
                            lo, hi = max(SEGP, tk0), min(SEGP + seg[1], tk0 + 128)
                        if hi <= lo:
                            continue
                        oh = [next_mm(), next_mm()]
                        for nh in range(2):
                            for kt in range(NFF):
                                T.op("pe", lambda e: e.matmul(oh[nh].t[:],
                                                              lhsT=GTt.t[:, kt, i * 128:(i + 1) * 128],
                                                              rhs=wdn.t[:, kt, nh * 512:(nh + 1) * 512],
                                                              start=(kt == 0), stop=(kt == NFF - 1)),
                                     reads=[GTt, wdn], writes=[oh[nh]])
                        p0 = lo - tk0
                        xl = x1l.next()
                        if not segm:
                            T.dma("sp", xl.t[p0:p0 + (hi - lo), :], X1[s][lo:hi, :], writes=[xl])
                            orow = out_off[s] + lo
                        else:
                            tl = (t0 + i * 128) // 128
                            gather_rows(xl, X1[s], idxl_sb.t[:, tl:tl + 1], idxl_sb)
                            orow = out_off[s] + lo - SEGP
                        ss = ssring2.next()
                        rs = ssring2.next()
                        tm_epilogue(oh, 1.0, ss, rs, xjunk, GT[1][s], xl, xl)
                        T.dma("pool", yout[orow:orow + (hi - lo), :], xl.t[p0:p0 + (hi - lo), :], reads=[xl])
            T.finish("sp")
        T.finish("sp")
    except _Stop:
        pass
    return nc


def dft_tables(seq_lens):
    p = np.arange(128)
    ang = 2.0 * np.pi * np.outer(p, p) / 128.0
    C, Sn = np.cos(ang), np.sin(ang)
    tabs = {"tabA": np.concatenate([C, Sn], 1).astype(np.float32),
            "tabW": (np.concatenate([C, -Sn], 1) / np.sqrt(128.0)).astype(np.float32)}
    for S in sorted(set(seq_lens)):
        L = S // 128
        Q = 128 // L
        ka = np.arange(128)[:, None]
        be = np.arange(L)[None, :]
        th = 2.0 * np.pi * (ka * be % S) / S
        sc = 1.0 / np.sqrt(float(S))
        tabs["tw%d" % L] = np.concatenate([np.cos(th) * sc, np.sin(th) * sc, -np.sin(th) * sc], 1).astype(np.float32)
        b = np.arange(L)
        a2 = 2.0 * np.pi * np.outer(b, b) / L
        KC = np.einsum("qp,bk->qbkp", np.eye(Q), np.cos(a2)).reshape(128, 128)
        KS = np.einsum("qp,bk->qbkp", np.eye(Q), np.sin(a2)).reshape(128, 128)
        tabs["kron%d" % L] = np.concatenate([KC, KS, -KS, KC], 1).astype(np.float32)
    return tabs


def make_vec(nseq, g_pre_mix, conv_w, conv_b, b_lru_a, b_lru_x, lru_lambda, g_pre_ffn, ffn_conv_w, ffn_conv_b,
             b_ada, cs):
    parts = [g_pre_mix.reshape(-1, 128), conv_w.reshape(-1, 128), conv_b.reshape(-1, 128),
             b_lru_a.reshape(-1, 128), b_lru_x.reshape(-1, 128), lru_lambda.reshape(-1, 128),
             g_pre_ffn.reshape(-1, 128), ffn_conv_w.reshape(-1, 128), ffn_conv_b.reshape(-1, 128),
             b_ada.reshape(-1, 128)] + [c.reshape(-1, 128) for c in cs]
    return np.ascontiguousarray(np.concatenate(parts, 0).astype(np.float32))


def seg_arrays(S, A, seglen, pad=128):
    ntv = (seglen + 2 * pad) // 128
    p = np.arange(128)[:, None]
    t = np.arange(ntv)[None, :]
    idx = np.clip(A - pad + 128 * t + p, 0, S - 1).astype(np.int32)
    tl = np.arange(ntv + 1)[None, :]
    idxl = np.clip(A - pad + 128 * tl - 1 + p, 0, S - 1).astype(np.int32)
    flag = np.zeros((128, 2), np.float32)
    flag[:, 0] = 1.0 if A - pad >= 0 else 0.0
    flag[:, 1] = 1.0 if A + seglen + pad <= S else 0.0
    return {"segidx": np.ascontiguousarray(idx), "segidxl": np.ascontiguousarray(idxl), "segflag": flag}


_NC_CACHE = {}


def kernel(x_prompt, x_sample, c_prompt, c_sample, w_ada, b_ada, g_pre_mix, w_in, conv_w, conv_b,
           w_lru_a, b_lru_a, w_lru_x, b_lru_x, lru_lambda, w_a_out, w_b_out, w_o, g_post_mix,
           g_pre_ffn, w_up, ffn_conv_w, ffn_conv_b, w_down, g_post_ffn):
    f = lambda a: np.ascontiguousarray(np.asarray(a, dtype=np.float32))
    x_prompt, x_sample, c_prompt, c_sample = f(x_prompt), f(x_sample), f(c_prompt), f(c_sample)
    B, S1, _ = x_prompt.shape
    B2, S2, _ = x_sample.shape
    per = B // N_CORES
    seq_lens = [S1] * per + [S2]
    nq = N_CORES // B2
    seglen = S2 // nq
    key = tuple(seq_lens)
    if key not in _NC_CACHE:
        _NC_CACHE[key] = build_nc(seq_lens, seg=(per, seglen))
    nc = _NC_CACHE[key]
    tabs = dft_tables(seq_lens)
    shared = {
        "w_ada": f(w_ada[0]), "b_ada": f(b_ada[0]).reshape(1, -1), "w_in": f(w_in[0]),
        "w_lru_a": f(w_lru_a[0]).reshape(16, 128, 128), "w_lru_x": f(w_lru_x[0]).reshape(16, 128, 128),
        "w_a_out": f(w_a_out[0]), "w_b_out": f(w_b_out[0]), "w_o": f(w_o[0]), "w_up": f(w_up[0]),
        "w_down": f(w_down[0]), "g_post_mix": f(g_post_mix[0]).reshape(1, -1),
        "g_post_ffn": f(g_post_ffn[0]).reshape(1, -1),
    }
    shared.update(tabs)
    in_maps = []
    for c in range(N_CORES):
        ps_ = list(range(c * per, (c + 1) * per))
        sm = c % B2
        xin = np.concatenate([x_prompt[p] for p in ps_] + [x_sample[sm]], 0)
        cs = [c_prompt[p] for p in ps_] + [c_sample[sm]]
        vecp = make_vec(len(seq_lens), f(g_pre_mix[0]), f(conv_w[0]), f(conv_b[0]), f(b_lru_a[0]), f(b_lru_x[0]),
                        f(lru_lambda[0]), f(g_pre_ffn[0]), f(ffn_conv_w[0]), f(ffn_conv_b[0]), f(b_ada[0]), cs)
        m = dict(shared)
        m["xin"] = np.ascontiguousarray(xin)
        m["vec"] = vecp
        m.update(seg_arrays(S2, (c // B2) * seglen, seglen))
        in_maps.append(m)
    res = run_bass_kernel_spmd(nc, in_maps, core_ids=list(range(N_CORES)))
    y_prompt = np.empty_like(x_prompt)
    y_sample = np.empty_like(x_sample)
    sm_of = lambda c: c % B2
    for c in range(N_CORES):
        yo = res.results[c]["yout"]
        for i in range(per):
            y_prompt[c * per + i] = yo[i * S1:(i + 1) * S1]
        q = c // B2
        y_sample[sm_of(c)][q * seglen:(q + 1) * seglen] = yo[per * S1:per * S1 + seglen]
    return (y_prompt, y_sample)
```

```python
import contextlib
import numpy as np
import concourse.bass as bass
import concourse.mybir as mybir
from concourse.bass_utils import run_bass_kernel_spmd

F32 = mybir.dt.float32
BF16 = mybir.dt.bfloat16
AF = mybir.ActivationFunctionType
ALU = mybir.AluOpType

D = 1024
DIN = 4608
DFF = 2816
NFF = 22
TB = 512
EPS = 1e-6
N_CORES = 8

def vec_layout(nseq):
    names = [("g_pre_mix", 8), ("conv_w", 32), ("conv_b", 8), ("b_lru_a", 16), ("b_lru_x", 16),
             ("lam", 16), ("g_pre_ffn", 8), ("ffn_conv_w", 132), ("ffn_conv_b", 44), ("b_ada", 48),
             ("c", 8 * nseq)]
    off = {}
    o = 0
    for n, k in names:
        off[n] = o
        o += k
    return off, o


class Buf:
    __slots__ = ("t", "name", "w", "r")

    def __init__(self, t, name):
        self.t = t
        self.name = name
        self.w = {}
        self.r = {}


class Trk:
    def __init__(self, nc, es):
        self.nc = nc
        self.es = es
        self.eng = {"pe": nc.tensor, "act": nc.scalar, "dve": nc.vector, "pool": nc.gpsimd, "sp": nc.sync}
        self.sem = {}
        self.cnt = {}
        self.waited = {}
        for e in self.eng:
            self.sem[e] = es.enter_context(nc.semaphore("s_" + e))
            self.cnt[e] = 0
        self.dsem = {}
        self.ninst = 0

    def _wait(self, e, tok):
        key, val, h = tok
        if key == ("eng", e) and e == "pe":
            return
        w = self.waited.setdefault(e, {})
        if w.get(key, 0) >= val:
            return
        self.eng[e].wait_ge(h, val)
        w[key] = val

    def _deps(self, e, reads, writes):
        for b in reads:
            for tok in b.w.values():
                self._wait(e, tok)
        for b in writes:
            for tok in b.w.values():
                self._wait(e, tok)
            for tok in b.r.values():
                self._wait(e, tok)

    def _mark(self, key, tok, reads, writes):
        for b in reads:
            b.r[key] = tok
        for b in writes:
            b.w = {key: tok}
            b.r = {}

    def op(self, e, fn, reads=(), writes=()):
        self._deps(e, reads, writes)
        inst = fn(self.eng[e])
        self.cnt[e] += 1
        self.ninst += 1
        inst.then_inc(self.sem[e], 1)
        self._mark(("eng", e), (("eng", e), self.cnt[e], self.sem[e]), reads, writes)
        return inst

    def dma(self, e, out, in_, reads=(), writes=(), key=None):
        self._deps(e, reads, writes)
        if key is None:
            key = writes[0].name if writes else reads[0].name
        key = (key, e)
        if key not in self.dsem:
            self.dsem[key] = [self.es.enter_context(self.nc.semaphore("d%d" % len(self.dsem))), 0]
        d = self.dsem[key]
        inst = self.eng[e].dma_start(out=out, in_=in_)
        d[1] += 16
        inst.then_inc(d[0], 16)
        self.ninst += 1
        self._mark(("dma", key), (("dma", key), d[1], d[0]), reads, writes)
        return inst

    def dma_barrier(self, engines=("sp", "pool")):
        for e in engines:
            for key, d in self.dsem.items():
                if d[1] > 0:
                    self._wait(e, (("dma", key), d[1], d[0]))

    def finish(self, e="sp"):
        self.dma_barrier((e,))
        for k in self.eng:
            if k != e and self.cnt[k] > 0:
                self._wait(e, (("eng", k), self.cnt[k], self.sem[k]))


class _Stop(Exception):
    pass


def build_nc(seq_lens, debug=False, stop=None, seg=None):
    nseq = len(seq_lens)
    TOT = sum(seq_lens)
    seq_off = [sum(seq_lens[:i]) for i in range(nseq)]
    Ls = sorted(set(s // 128 for s in seq_lens))
    VOFF, NVEC = vec_layout(nseq)

    nc = bass.Bass("TRN2", target_bir_lowering=False)

    def din(name, shape, dt=F32):
        return nc.dram_tensor(name, list(shape), dt, kind="ExternalInput").ap()

    skind = "ExternalOutput" if debug else "Internal"

    def dscr(name, shape, dt):
        return nc.dram_tensor(name, list(shape), dt, kind=skind).ap()

    xin = din("xin", [TOT, D])
    vec = din("vec", [NVEC, 128])
    w_ada = din("w_ada", [D, 6 * D])
    b_ada = din("b_ada", [1, 6 * D])
    w_in = din("w_in", [D, DIN])
    w_lru_a = din("w_lru_a", [16, 128, 128])
    w_lru_x = din("w_lru_x", [16, 128, 128])
    w_a_out = din("w_a_out", [D, D])
    w_b_out = din("w_b_out", [512, D])
    w_o = din("w_o", [D, D])
    w_up = din("w_up", [D, 2 * DFF])
    w_down = din("w_down", [DFF, D])
    g_post_mix = din("g_post_mix", [1, D])
    g_post_ffn = din("g_post_ffn", [1, D])
    tabA = din("tabA", [128, 256])
    tabW = din("tabW", [128, 256])
    tw_in = {L: din("tw%d" % L, [128, 3 * L]) for L in Ls}
    kron_in = {L: din("kron%d" % L, [128, 512]) for L in Ls}
    out_lens = [(seg[1] if (seg is not None and seg[0] == i) else S) for i, S in enumerate(seq_lens)]
    out_off = [sum(out_lens[:i]) for i in range(nseq)]
    yout = nc.dram_tensor("yout", [sum(out_lens), D], F32, kind="ExternalOutput").ap()
    if seg is not None:
        ntv = (seg[1] + 256) // 128
        segidx = din("segidx", [128, ntv], mybir.dt.int32)
        segidxl = din("segidxl", [128, ntv + 1], mybir.dt.int32)
        segflag = din("segflag", [128, 2])

    XA = [dscr("XA%d" % s, [D, S], BF16) for s, S in enumerate(seq_lens)]
    GY = [dscr("GY%d" % s, [D, S], BF16) for s, S in enumerate(seq_lens)]
    TA = [dscr("TA%d" % s, [D, S], BF16) for s, S in enumerate(seq_lens)]
    TBG = [dscr("TBG%d" % s, [D, S], BF16) for s, S in enumerate(seq_lens)]
    XC = [dscr("XC%d" % s, [D, S], BF16) for s, S in enumerate(seq_lens)]
    HF = [dscr("HF%d" % s, [D, S], BF16) for s, S in enumerate(seq_lens)]
    YBG = [dscr("YBG%d" % s, [D, S], BF16) for s, S in enumerate(seq_lens)]
    XB = [dscr("XB%d" % s, [S, 512], BF16) for s, S in enumerate(seq_lens)]
    YT = [dscr("YT%d" % s, [S, 1024], BF16) for s, S in enumerate(seq_lens)]
    X1 = [dscr("X1_%d" % s, [S, D], F32) for s, S in enumerate(seq_lens)]
    PQD = [dscr("PQD%d" % s, [D, S], BF16) for s, S in enumerate(seq_lens)]

    dbg_ab = [nc.dram_tensor("dbg_%s" % n, [128, TB], F32, kind="ExternalOutput").ap() for n in "abcd"] if debug else None
    dbg_done = []

    def fm(T):
        return T.rearrange("(ft p) s -> p ft s", p=128)

    es = contextlib.ExitStack()
    try:
      with es:
        T = Trk(nc, es)

        def stop_here(tag):
            if stop == tag:
                T.finish("sp")
                raise _Stop()

        def sb(name, shape, dt=F32):
            return Buf(es.enter_context(nc.sbuf_tensor(name, list(shape), dt)), name)

        def ps(name, shape, dt=F32):
            return Buf(es.enter_context(nc.psum_tensor(name, list(shape), dt)), name)

        NMM = 6
        ps_mm = [ps("ps_mm%d" % i, [128, 512]) for i in range(NMM)]
        ps_tr = [ps("ps_tr%d" % i, [128, 1024], BF16) for i in range(2)]
        mm_i = [0]
        tr_i = [0]

        def next_mm():
            b = ps_mm[mm_i[0] % NMM]
            mm_i[0] += 1
            return b

        def next_tr():
            b = ps_tr[tr_i[0] % 2]
            tr_i[0] += 1
            return b

        ident_f = sb("ident_f", [128, 128])
        ident_b = sb("ident_b", [128, 128], BF16)
        ones_f = sb("ones_f", [128, 128])
        nhalf = sb("nhalf", [128, 1])
        cols = sb("cols", [128, NVEC])
        dcols = sb("dcols", [128, 96])
        modc = sb("modc", [128, 48, nseq])
        G1 = sb("G1", [128, 8, nseq])
        G2 = sb("G2", [128, 8, nseq])
        GT = [[None] * nseq, [sb("GT1_%d" % s, [128, D]) for s in range(nseq)]]

        T.op("pool", lambda e: e.memset(ident_f.t[:], 0.0), writes=[ident_f])
        T.op("pool", lambda e: e.affine_select(out=ident_f.t[:], in_=ident_f.t[:], pattern=[[-1, 128]],
                                               compare_op=ALU.not_equal, fill=1.0, base=0,
                                               channel_multiplier=1), reads=[ident_f], writes=[ident_f])
        T.op("dve", lambda e: e.tensor_copy(ident_b.t[:], ident_f.t[:]), reads=[ident_f], writes=[ident_b])
        T.op("pool", lambda e: e.memset(ones_f.t[:], 1.0), writes=[ones_f])
        T.op("pool", lambda e: e.memset(nhalf.t[:], -0.5), writes=[nhalf])

        class Ring:
            def __init__(self, name, shape, dt, n):
                self.b = [sb("%s%d" % (name, i), shape, dt) for i in range(n)]
                self.i = 0

            def next(self):
                b = self.b[self.i % len(self.b)]
                self.i += 1
                return b

        cv_i = [0]
        cv_eng = ("act", "dve")

        def load_w_bf16(dst, src3, kcn, n, stage_ring, nchunk=2048):
            cv_i[0] += 1
            for kc in range(kcn):
                for n0 in range(0, n, nchunk):
                    nn = min(nchunk, n - n0)
                    st = stage_ring.next()
                    T.dma("sp", st.t[:, 0:nn], src3[:, kc, n0:n0 + nn], writes=[st])
                    e = cv_eng[cv_i[0] % 2]
                    if e == "act":
                        T.op(e, lambda en: en.activation(out=dst.t[:, kc, n0:n0 + nn], in_=st.t[:, 0:nn], func=AF.Copy),
                             reads=[st], writes=[dst])
                    else:
                        T.op(e, lambda en: en.tensor_copy(dst.t[:, kc, n0:n0 + nn], st.t[:, 0:nn]),
                             reads=[st], writes=[dst])

        def rstd_from_ss(ss, rs):
            T.op("dve", lambda e: e.tensor_scalar(out=ss.t[:], in0=ss.t[:], scalar1=1.0 / D, scalar2=EPS,
                                                  op0=ALU.mult, op1=ALU.add), reads=[ss], writes=[ss])
            T.op("pool", lambda e: e.tensor_tensor(out=rs.t[:], in0=ss.t[:], in1=nhalf.t[:], op=ALU.pow),
                 reads=[ss, nhalf], writes=[rs])

        def tm_epilogue(oh, sq_scale, ss2, rs, junk, GTb, xres, outb):
            for hh in range(2):
                T.op("act", lambda e: e.activation(out=junk.t[:, hh * 512:(hh + 1) * 512], in_=oh[hh].t[:], func=AF.Square,
                                                   scale=sq_scale, accum_out=ss2.t[:, hh:hh + 1]),
                     reads=[oh[hh]], writes=[junk, ss2])
            T.op("dve", lambda e: e.tensor_tensor(out=ss2.t[:, 0:1], in0=ss2.t[:, 0:1], in1=ss2.t[:, 1:2], op=ALU.add),
                 reads=[ss2], writes=[ss2])
            T.op("dve", lambda e: e.tensor_scalar(out=ss2.t[:, 0:1], in0=ss2.t[:, 0:1], scalar1=1.0 / D, scalar2=EPS,
                                                  op0=ALU.mult, op1=ALU.add), reads=[ss2], writes=[ss2])
            T.op("pool", lambda e: e.tensor_tensor(out=rs.t[:, 0:1], in0=ss2.t[:, 0:1], in1=nhalf.t[:], op=ALU.pow),
                 reads=[ss2, nhalf], writes=[rs])
            for hh in range(2):
                hs = slice(hh * 512, (hh + 1) * 512)
                T.op("dve", lambda e: e.tensor_tensor(out=junk.t[:, hs], in0=oh[hh].t[:], in1=GTb.t[:, hs], op=ALU.mult),
                     reads=[oh[hh], GTb], writes=[junk])
                T.op("dve", lambda e: e.scalar_tensor_tensor(out=outb.t[:, hs], in0=junk.t[:, hs], scalar=rs.t[:, 0:1],
                                                             in1=xres.t[:, hs], op0=ALU.mult, op1=ALU.add),
                     reads=[junk, rs, xres], writes=[outb])

        with contextlib.ExitStack() as esG0:
            for s_ in range(nseq):
                GT[0][s_] = Buf(esG0.enter_context(nc.sbuf_tensor("GT0_%d" % s_, [128, D], F32)), "GT0_%d" % s_)
            with contextlib.ExitStack() as es0:
                def sb0(name, shape, dt=F32):
                    return Buf(es0.enter_context(nc.sbuf_tensor(name, list(shape), dt)), name)

                nchunks = (NVEC + 127) // 128
                for ci in range(nchunks):
                    r0 = ci * 128
                    r = min(128, NVEC - r0)
                    vt = sb0("vec%d" % ci, [128, 128])
                    T.dma("sp", vt.t[0:r, :], vec[r0:r0 + r, :], writes=[vt])
                    pm = next_mm()
                    T.op("pe", lambda e: e.transpose(pm.t[:, 0:r], vt.t[0:r, :], ident_f.t[0:r, 0:r]),
                         reads=[vt, ident_f], writes=[pm])
                    T.op("dve", lambda e: e.tensor_copy(cols.t[:, r0:r0 + r], pm.t[:, 0:r]), reads=[pm], writes=[cols])

                o_ba, o_bx, o_lam = VOFF["b_lru_a"], VOFF["b_lru_x"], VOFF["lam"]
                T.op("dve", lambda e: e.tensor_scalar(out=dcols.t[:, 0:16], in0=cols.t[:, o_ba:o_ba + 16], scalar1=0.5,
                                                      scalar2=None, op0=ALU.mult), reads=[cols], writes=[dcols])
                T.op("dve", lambda e: e.tensor_scalar(out=dcols.t[:, 16:32], in0=cols.t[:, o_bx:o_bx + 16], scalar1=0.5,
                                                      scalar2=None, op0=ALU.mult), reads=[cols], writes=[dcols])
                T.op("act", lambda e: e.activation(out=dcols.t[:, 64:80], in_=cols.t[:, o_lam:o_lam + 16], func=AF.Exp,
                                                   scale=-1.0), reads=[cols], writes=[dcols])
                T.op("act", lambda e: e.activation(out=dcols.t[:, 80:96], in_=dcols.t[:, 64:80], func=AF.Ln,
                                                   bias=1.0, scale=1.0), reads=[dcols], writes=[dcols])
                T.op("dve", lambda e: e.tensor_scalar(out=dcols.t[:, 32:48], in0=dcols.t[:, 80:96], scalar1=-4.0,
                                                      scalar2=None, op0=ALU.mult), reads=[dcols], writes=[dcols])
                T.op("dve", lambda e: e.tensor_scalar(out=dcols.t[:, 48:64], in0=dcols.t[:, 80:96], scalar1=-8.0,
                                                      scalar2=None, op0=ALU.mult), reads=[dcols], writes=[dcols])
                siluc = sb0("siluc", [128, 8 * nseq])
                oc = VOFF["c"]
                T.op("act", lambda e: e.activation(out=siluc.t[:], in_=cols.t[:, oc:oc + 8 * nseq], func=AF.Silu),
                     reads=[cols], writes=[siluc])
                silr = sb0("silr", [128, 8, nseq])
                for s in range(nseq):
                    T.op("dve", lambda e: e.tensor_copy(silr.t[:, :, s], siluc.t[:, s * 8:(s + 1) * 8]),
                         reads=[siluc], writes=[silr])
                bcs = []
                for s in range(nseq):
                    bc = sb0("bc%d" % s, [128, 8, 128])
                    for kc in range(8):
                        T.op("dve", lambda e: e.tensor_scalar(out=bc.t[:, kc, :], in0=ones_f.t[:],
                                                              scalar1=siluc.t[:, s * 8 + kc:s * 8 + kc + 1],
                                                              scalar2=None, op0=ALU.mult),
                             reads=[ones_f, siluc], writes=[bc])
                    bcs.append(bc)
                brow = sb0("brow", [128, 2, D])
                grow = sb0("grow", [128, 2, D])
                T.dma("sp", brow.t[:, 0, :], b_ada[:, 2 * D:3 * D].partition_broadcast(128), writes=[brow])
                T.dma("sp", brow.t[:, 1, :], b_ada[:, 5 * D:6 * D].partition_broadcast(128), writes=[brow])
                T.dma("sp", grow.t[:, 0, :], g_post_mix.partition_broadcast(128), writes=[grow])
                T.dma("sp", grow.t[:, 1, :], g_post_ffn.partition_broadcast(128), writes=[grow])

                wring = [sb0("wada%d" % i, [128, 8, 512]) for i in range(2)]
                w_ada3 = w_ada.rearrange("(kc p) n -> p kc n", p=128)
                oba = VOFF["b_ada"]
                for ch in range(12):
                    wt = wring[ch % 2]
                    T.dma("sp", wt.t[:], w_ada3[:, :, ch * 512:(ch + 1) * 512], writes=[wt])
                    which = {4: 0, 5: 0, 10: 1, 11: 1}.get(ch)
                    if which is None:
                        for j in range(4):
                            nt = ch * 4 + j
                            pm = next_mm()
                            for kc in range(8):
                                T.op("pe", lambda e: e.matmul(pm.t[:, 0:nseq], lhsT=wt.t[:, kc, j * 128:(j + 1) * 128],
                                                              rhs=silr.t[:, kc, :], start=(kc == 0), stop=(kc == 7)),
                                     reads=[wt, silr], writes=[pm])
                            T.op("dve", lambda e: e.tensor_scalar(out=modc.t[:, nt, :], in0=pm.t[:, 0:nseq],
                                                                  scalar1=cols.t[:, oba + nt:oba + nt + 1], scalar2=None,
                                                                  op0=ALU.add), reads=[pm, cols], writes=[modc])
                    else:
                        half = (ch % 2)
                        if ch in (4, 10):
                            half = 0
                        else:
                            half = 1
                        fac = 0.5 if which == 0 else 1.0
                        for s in range(nseq):
                            pm = next_mm()
                            for kc in range(8):
                                T.op("pe", lambda e: e.matmul(pm.t[:], lhsT=bcs[s].t[:, kc, :], rhs=wt.t[:, kc, :],
                                                              start=(kc == 0), stop=(kc == 7)),
                                     reads=[bcs[s], wt], writes=[pm])
                            dst = GT[which][s]
                            sl = slice(half * 512, (half + 1) * 512)
                            T.op("dve", lambda e: e.tensor_tensor(out=dst.t[:, sl], in0=pm.t[:], in1=brow.t[:, which, sl],
                                                                  op=ALU.add), reads=[pm, brow], writes=[dst])
                            T.op("dve", lambda e: e.scalar_tensor_tensor(out=dst.t[:, sl], in0=dst.t[:, sl], scalar=fac,
                                                                         in1=grow.t[:, which, sl], op0=ALU.mult,
                                                                         op1=ALU.mult), reads=[dst, grow], writes=[dst])
                og1, og2 = VOFF["g_pre_mix"], VOFF["g_pre_ffn"]
                for ft in range(8):
                    T.op("dve", lambda e: e.tensor_scalar(out=G1.t[:, ft, :], in0=modc.t[:, 8 + ft, :], scalar1=1.0,
                                                          scalar2=cols.t[:, og1 + ft:og1 + ft + 1], op0=ALU.add,
                                                          op1=ALU.mult), reads=[modc, cols], writes=[G1])
                    T.op("dve", lambda e: e.tensor_scalar(out=G2.t[:, ft, :], in0=modc.t[:, 32 + ft, :], scalar1=1.0,
                                                          scalar2=cols.t[:, og2 + ft:og2 + ft + 1], op0=ALU.add,
                                                          op1=ALU.mult), reads=[modc, cols], writes=[G2])
                T.finish("sp")

            def norm_transpose(xt, hT, col0, Gm, SHbase, s, xn_ring, ss_ring, junk):
                ss = ss_ring.next()
                rs = ss_ring.next()
                xn = xn_ring.next()
                T.op("act", lambda e: e.activation(out=xn.t[:], in_=xt.t[:], func=AF.Square, accum_out=ss.t[:]),
                     reads=[xt], writes=[xn, ss])
                rstd_from_ss(ss, rs)
                T.op("act", lambda e: e.activation(out=xn.t[:], in_=xt.t[:], func=AF.Copy, scale=rs.t[:]),
                     reads=[xt, rs], writes=[xn])
                pt = next_tr()
                for ft in range(8):
                    T.op("pe", lambda e: e.transpose(pt.t[:, ft * 128:(ft + 1) * 128], xn.t[:, ft * 128:(ft + 1) * 128],
                                                     ident_b.t[:]), reads=[xn, ident_b], writes=[pt])
                for ft in range(8):
                    T.op("dve", lambda e: e.tensor_scalar(out=hT.t[:, ft, col0:col0 + 128],
                                                          in0=pt.t[:, ft * 128:(ft + 1) * 128],
                                                          scalar1=Gm.t[:, ft, s:s + 1],
                                                          scalar2=modc.t[:, SHbase + ft, s:s + 1],
                                                          op0=ALU.mult, op1=ALU.add),
                         reads=[pt, Gm, modc], writes=[hT])

            stop_here("0")
            with contextlib.ExitStack() as esA:
                def sbA(name, shape, dt=F32):
                    return Buf(esA.enter_context(nc.sbuf_tensor(name, list(shape), dt)), name)

                class RingA(Ring):
                    def __init__(self, name, shape, dt, n):
                        self.b = [sbA("%s%d" % (name, i), shape, dt) for i in range(n)]
                        self.i = 0

                win = sbA("win", [128, 8, DIN], BF16)
                with contextlib.ExitStack() as esS:
                    stg = Ring.__new__(Ring)
                    stg.b = [Buf(esS.enter_context(nc.sbuf_tensor("stgA%d" % i, [128, 2304], F32)), "stgA%d" % i) for i in range(2)]
                    stg.i = 0
                    load_w_bf16(win, w_in.rearrange("(kc p) n -> p kc n", p=128), 8, DIN, stg, nchunk=2304)
                    T.finish("sp")
                xring = RingA("xA", [128, D], F32, 2)
                xnring = RingA("xnA", [128, D], BF16, 2)
                ssring = RingA("ssA", [128, 1], F32, 8)
                junk = None
                hring = RingA("hT", [128, 8, TB], BF16, 2)
                oring = RingA("oA", [128, 4, TB], BF16, 3)
                xbring = RingA("xbA", [128, 512], BF16, 2)

                def norm_tile_A(s, t0n, i, hTn):
                    xt = xring.next()
                    r0 = seq_off[s] + t0n + i * 128
                    T.dma("sp", xt.t[:], xin[r0:r0 + 128, :], writes=[xt])
                    norm_transpose(xt, hTn, i * 128, G1, 0, s, xnring, ssring, junk)

                blocksA = [(s, t0) for s, S in enumerate(seq_lens) for t0 in range(0, S, TB)]
                hT_next = hring.next()
                for i in range(4):
                    norm_tile_A(blocksA[0][0], blocksA[0][1], i, hT_next)
                for bi, (s, t0) in enumerate(blocksA):
                    S = seq_lens[s]
                    if True:
                        hT = hT_next
                        nxt = blocksA[bi + 1] if bi + 1 < len(blocksA) else None
                        if nxt is not None:
                            hT_next = hring.next()
                        for grp in range(9):
                            if nxt is not None and grp in (1, 3, 5, 7):
                                norm_tile_A(nxt[0], nxt[1], (grp - 1) // 2, hT_next)
                            if grp == 4:
                                continue
                            ot = oring.next()
                            for j4 in range(4):
                                j = grp * 4 + j4
                                pm = next_mm()
                                for kc in range(8):
                                    T.op("pe", lambda e: e.matmul(pm.t[:], lhsT=win.t[:, kc, j * 128:(j + 1) * 128],
                                                                  rhs=hT.t[:, kc, :], start=(kc == 0), stop=(kc == 7)),
                                         reads=[win, hT], writes=[pm])
                                if grp < 2:
                                    T.op("dve", lambda e: e.tensor_copy(ot.t[:, j4, :], pm.t[:]), reads=[pm], writes=[ot])
                                elif grp < 4:
                                    T.op("act", lambda e: e.activation(out=ot.t[:, j4, :], in_=pm.t[:],
                                                                       func=AF.Gelu_apprx_tanh), reads=[pm], writes=[ot])
                                else:
                                    T.op("act", lambda e: e.activation(out=ot.t[:, j4, :], in_=pm.t[:], func=AF.Tanh,
                                                                       scale=0.5), reads=[pm], writes=[ot])
                            if grp < 2:
                                dstT, f0 = XA[s], grp * 4
                            elif grp < 4:
                                dstT, f0 = GY[s], (grp - 2) * 4
                            elif grp < 7:
                                dstT, f0 = TA[s], (grp - 5) * 4
                            else:
                                dstT, f0 = TBG[s], (grp - 7) * 4
                            T.dma("pool", fm(dstT)[:, f0:f0 + 4, t0:t0 + TB], ot.t[:], reads=[ot])
                        for i in range(4):
                            pm = next_mm()
                            for kc in range(8):
                                T.op("pe", lambda e: e.matmul(pm.t[:], lhsT=hT.t[:, kc, i * 128:(i + 1) * 128],
                                                              rhs=win.t[:, kc, 2048:2560], start=(kc == 0), stop=(kc == 7)),
                                     reads=[hT, win], writes=[pm])
                            xb = xbring.next()
                            T.op("dve", lambda e: e.tensor_copy(xb.t[:], pm.t[:]), reads=[pm], writes=[xb])
                            T.dma("pool", XB[s][t0 + i * 128:t0 + (i + 1) * 128, :], xb.t[:], reads=[xb])
                T.finish("sp")
            T.dma_barrier()

            stop_here("A")
            with contextlib.ExitStack() as esD:
                def sbD(name, shape, dt=F32):
                    return Buf(esD.enter_context(nc.sbuf_tensor(name, list(shape), dt)), name)

                class RingD(Ring):
                    def __init__(self, name, shape, dt, n):
                        self.b = [sbD("%s%d" % (name, i), shape, dt) for i in range(n)]
                        self.i = 0

                Smax = max(seq_lens)
                PQ = sbD("PQ", [128, 4, Smax], BF16)
                PQa = Buf(PQ.t, "PQa")
                PQd = Buf(PQ.t, "PQd")
                Wp = sbD("Wp", [128, 8, D], BF16)
                tAb = sbD("tAb", [128, 256], BF16)
                with contextlib.ExitStack() as esS:
                    def sbS(name, shape, dt=F32):
                        return Buf(esS.enter_context(nc.sbuf_tensor(name, list(shape), dt)), name)
                    tA = sbS("tA", [128, 256])
                    tW = sbS("tW", [128, 256])
                    wb = sbS("wb", [128, 4, D])
                    T.dma("sp", tA.t[:], tabA, writes=[tA])
                    T.dma("sp", tW.t[:], tabW, writes=[tW])
                    T.dma("sp", wb.t[:], w_b_out.rearrange("(g p) n -> p g n", p=128), writes=[wb])
                    T.op("dve", lambda e: e.tensor_copy(tAb.t[:], tA.t[:]), reads=[tA], writes=[tAb])
                    for g in range(4):
                        for pq in range(2):
                            for nh in range(2):
                                pm = next_mm()
                                T.op("pe", lambda e: e.matmul(pm.t[:], lhsT=tW.t[:, pq * 128:(pq + 1) * 128],
                                                              rhs=wb.t[:, g, nh * 512:(nh + 1) * 512], start=True, stop=True),
                                     reads=[tW, wb], writes=[pm])
                                T.op("dve", lambda e: e.tensor_copy(Wp.t[:, g * 2 + pq, nh * 512:(nh + 1) * 512], pm.t[:]),
                                     reads=[pm], writes=[Wp])
                    T.finish("sp")
                twt = {}
                krb = {}
                for L in Ls:
                    twt[L] = sbD("twsb%d" % L, [128, 3 * L])
                    T.dma("sp", twt[L].t[:], tw_in[L], writes=[twt[L]])
                    kf = sbD("krf%d" % L, [128, 512])
                    T.dma("sp", kf.t[:], kron_in[L], writes=[kf])
                    krb[L] = sbD("krb%d" % L, [128, 512], BF16)
                    T.op("dve", lambda e: e.tensor_copy(krb[L].t[:], kf.t[:]), reads=[kf], writes=[krb[L]])

                stop_here("D0")
                BCH = 8
                d1ring = RingD("d1", [128, BCH, 512], BF16, 2)
                t12 = RingD("t12", [128, 2, 512], F32, 2)
                ytring = RingD("yt", [128, 2, 512], BF16, 3)
                zring = RingD("z", [128, 2, 512], BF16, 3)
                tbring = RingD("tbD", [128, 4, TB], BF16, 2)
                ybring = RingD("ybD", [128, 4, TB], BF16, 2)
                pqring = RingD("pqD", [128, 8, TB], BF16, 2)

                for s, S in enumerate(seq_lens):
                    L = S // 128
                    Q = 128 // L
                    tw = twt[L]
                    xb3 = XB[s].rearrange("(a b) c -> a b c", b=L)
                    yt3 = YT[s].rearrange("(a b) c -> a b c", b=L)
                    for b0 in range(0, L, BCH):
                        d1 = d1ring.next()
                        T.dma("sp", d1.t[:], xb3[:, b0:b0 + BCH, :], writes=[d1])
                        for bb in range(BCH):
                            beta = b0 + bb
                            pr = next_mm()
                            pi = next_mm()
                            T.op("pe", lambda e: e.matmul(pr.t[:], lhsT=tAb.t[:, 0:128], rhs=d1.t[:, bb, :], start=True,
                                                          stop=True), reads=[tAb, d1], writes=[pr])
                            T.op("pe", lambda e: e.matmul(pi.t[:], lhsT=tAb.t[:, 128:256], rhs=d1.t[:, bb, :], start=True,
                                                          stop=True), reads=[tAb, d1], writes=[pi])
                            tt = t12.next()
                            tc_ = tw.t[:, beta:beta + 1]
                            ts_ = tw.t[:, L + beta:L + beta + 1]
                            nts_ = tw.t[:, 2 * L + beta:2 * L + beta + 1]
                            T.op("act", lambda e: e.activation(out=tt.t[:, 0, :], in_=pr.t[:], func=AF.Copy, scale=tc_),
                                 reads=[pr, tw], writes=[tt])
                            T.op("act", lambda e: e.activation(out=tt.t[:, 1, :], in_=pi.t[:], func=AF.Copy, scale=tc_),
                                 reads=[pi, tw], writes=[tt])
                            yt = ytring.next()
                            T.op("dve", lambda e: e.scalar_tensor_tensor(out=yt.t[:, 0, :], in0=pi.t[:], scalar=nts_,
                                                                         in1=tt.t[:, 0, :], op0=ALU.mult, op1=ALU.add),
                                 reads=[pi, tw, tt], writes=[yt])
                            T.op("dve", lambda e: e.scalar_tensor_tensor(out=yt.t[:, 1, :], in0=pr.t[:], scalar=ts_,
                                                                         in1=tt.t[:, 1, :], op0=ALU.mult, op1=ALU.add),
                                 reads=[pr, tw, tt], writes=[yt])
                            T.dma("pool", yt3[:, beta, :], yt.t[:].rearrange("p a c -> p (a c)"), reads=[yt])
                    T.dma_barrier()
                    stop_here("D1")
                    kr = krb[L]
                    for g2 in range(2):
                      for ka in range(L):
                        z = zring.next()
                        T.dma("sp", z.t[:].rearrange("p a c -> p (a c)"), YT[s][ka * 128:(ka + 1) * 128, :], writes=[z])
                        if True:
                            pms = [next_mm(), next_mm()]
                            for gg in range(2):
                                g = g2 * 2 + gg
                                pm = pms[gg]
                                T.op("pe", lambda e: e.matmul(pm.t[:, 0:256], lhsT=z.t[:, 0, g * 128:(g + 1) * 128],
                                                              rhs=kr.t[:, 0:256], start=True, stop=False),
                                     reads=[z, kr], writes=[pm])
                                T.op("pe", lambda e: e.matmul(pm.t[:, 0:256], lhsT=z.t[:, 1, g * 128:(g + 1) * 128],
                                                              rhs=kr.t[:, 256:512], start=False, stop=True),
                                     reads=[z, kr], writes=[pm])
                                src = pm.t[:, 0:256].rearrange("p (c b q) -> p c b q", c=2, q=Q)
                                dst = PQ.t[:, gg * 2:gg * 2 + 2, 0:S].rearrange("p c (b r) -> p c b r", r=128)[:, :, :, Q * ka:Q * ka + Q]
                                T.op("dve", lambda e: e.tensor_copy(dst, src), reads=[pm], writes=[PQd])
                      for c4 in range(4):
                        T.dma("pool", PQD[s][(g2 * 4 + c4) * 128:(g2 * 4 + c4 + 1) * 128, :], PQ.t[:, c4, 0:S], reads=[PQa, PQd], key="PQ")
                    T.dma_barrier()
                    stop_here("D2")
                    for t0 in range(0, S, TB):
                        pqb = pqring.next()
                        T.dma("sp", pqb.t[:], fm(PQD[s])[:, :, t0:t0 + TB], writes=[pqb])
                        for nh in range(2):
                            tb = tbring.next()
                            T.dma("sp", tb.t[:], fm(TBG[s])[:, nh * 4:(nh + 1) * 4, t0:t0 + TB], writes=[tb])
                            yb = ybring.next()
                            for j4 in range(4):
                                nt = nh * 4 + j4
                                pm = next_mm()
                                for k8 in range(8):
                                    T.op("pe", lambda e: e.matmul(pm.t[:], lhsT=Wp.t[:, k8, nt * 128:(nt + 1) * 128],
                                                                  rhs=pqb.t[:, k8, :], start=(k8 == 0), stop=(k8 == 7)),
                                         reads=[Wp, pqb], writes=[pm])
                                T.op("dve", lambda e: e.scalar_tensor_tensor(out=yb.t[:, j4, :], in0=tb.t[:, j4, :], scalar=1.0,
                                                                             in1=pm.t[:], op0=ALU.add, op1=ALU.mult),
                                     reads=[tb, pm], writes=[yb])
                            T.dma("pool", fm(YBG[s])[:, nh * 4:(nh + 1) * 4, t0:t0 + TB], yb.t[:], reads=[yb])
                T.finish("sp")
            T.dma_barrier()

            stop_here("D")
            with contextlib.ExitStack() as esB:
                def sbB(name, shape, dt=F32):
                    return Buf(esB.enter_context(nc.sbuf_tensor(name, list(shape), dt)), name)

                def mkring(stack, name, shape, dt, n):
                    r = Ring.__new__(Ring)
                    r.b = [Buf(stack.enter_context(nc.sbuf_tensor("%s%d" % (name, i), list(shape), dt)), "%s%d" % (name, i))
                           for i in range(n)]
                    r.i = 0
                    return r

                wla = sbB("wla", [128, 16, 128], BF16)
                wlx = sbB("wlx", [128, 16, 128], BF16)
                wao = sbB("wao", [128, 8, D], BF16)
                wo = sbB("wo", [128, 8, D], BF16)
                stg = mkring(esB, "stgB", [128, D], F32, 2)
                for (dst, src) in ((wla, w_lru_a), (wlx, w_lru_x)):
                    src3 = src.rearrange("h i j -> i h j")
                    for h0 in range(0, 16, 8):
                        st = stg.next()
                        T.dma("sp", st.t[:].rearrange("p (h j) -> p h j", h=8), src3[:, h0:h0 + 8, :], writes=[st])
                        T.op("dve", lambda e: e.tensor_copy(dst.t[:, h0:h0 + 8, :],
                                                            st.t[:].rearrange("p (h j) -> p h j", h=8)),
                             reads=[st], writes=[dst])
                load_w_bf16(wao, w_a_out.rearrange("(kc p) n -> p kc n", p=128), 8, D, stg, nchunk=1024)
                load_w_bf16(wo, w_o.rearrange("(kc p) n -> p kc n", p=128), 8, D, stg, nchunk=1024)
                ocw = VOFF["conv_w"]
                ocb = VOFF["conv_b"]
                carry = sbB("carry", [128, 8])

                xcring = mkring(esB, "xc", [128, 8, TB], BF16, 2)
                trr = mkring(esB, "trB", [128, TB], F32, 1)
                br = mkring(esB, "bB", [128, TB], F32, 2)
                hr = mkring(esB, "hB", [128, TB], F32, 2)
                a2all_t = esB.enter_context(nc.sbuf_tensor("a2all", [128, 8, TB], F32))
                aall_t = esB.enter_context(nc.sbuf_tensor("aall", [128, 8, TB], F32))
                mall_t = esB.enter_context(nc.sbuf_tensor("mall", [128, 8, TB], F32))
                tiall_t = [esB.enter_context(nc.sbuf_tensor("tiall%d" % k, [128, 8, TB], BF16)) for k in range(2)]
                a2s = [Buf(a2all_t, "a2s%d" % i) for i in range(8)]
                aas = [Buf(aall_t, "aas%d" % i) for i in range(8)]
                mms = [Buf(mall_t, "mms%d" % i) for i in range(8)]
                tis = [[Buf(tiall_t[k], "tis%d_%d" % (k, i)) for i in range(8)] for k in range(2)]

                def lru_phase1(xc, d, k):
                    for ft in range(8):
                        pr = next_mm()
                        pi = next_mm()
                        T.op("pe", lambda e: e.matmul(pr.t[:], lhsT=wla.t[:, d * 8 + ft, :], rhs=xc.t[:, ft, :], start=True,
                                                      stop=True), reads=[wla, xc], writes=[pr])
                        T.op("pe", lambda e: e.matmul(pi.t[:], lhsT=wlx.t[:, d * 8 + ft, :], rhs=xc.t[:, ft, :], start=True,
                                                      stop=True), reads=[wlx, xc], writes=[pi])
                        ci = d * 8 + ft
                        t_r = trr.next()
                        T.op("act", lambda e: e.activation(out=t_r.t[:], in_=pr.t[:], func=AF.Tanh, scale=0.5,
                                                           bias=dcols.t[:, ci:ci + 1]), reads=[pr, dcols], writes=[t_r])
                        T.op("act", lambda e: e.activation(out=tiall_t[k][:, ft, :], in_=pi.t[:], func=AF.Tanh, scale=0.5,
                                                           bias=dcols.t[:, 16 + ci:16 + ci + 1]), reads=[pi, dcols],
                             writes=[tis[k][ft]])
                        T.op("act", lambda e: e.activation(out=a2all_t[:, ft, :], in_=t_r.t[:], func=AF.Exp,
                                                           scale=dcols.t[:, 48 + ci:48 + ci + 1],
                                                           bias=dcols.t[:, 48 + ci:48 + ci + 1]), reads=[t_r, dcols],
                             writes=[a2s[ft]])

                def lru_sqrt():
                    for ft in range(8):
                        T.op("act", lambda e: e.activation(out=aall_t[:, ft, :], in_=a2all_t[:, ft, :], func=AF.Sqrt),
                             reads=[a2s[ft]], writes=[aas[ft]])
                        T.op("act", lambda e: e.activation(out=mall_t[:, ft, :], in_=a2all_t[:, ft, :], func=AF.Sqrt,
                                                           scale=-(1.0 - 2.0 ** -22), bias=1.0), reads=[a2s[ft]], writes=[mms[ft]])

                def lru_dve(xc, k, reverse, post):
                    for ft in range(8):
                        b = br.next()
                        T.op("dve", lambda e: e.scalar_tensor_tensor(out=b.t[:], in0=tiall_t[k][:, ft, :], scalar=1.0,
                                                                     in1=mall_t[:, ft, :], op0=ALU.add, op1=ALU.mult),
                             reads=[tis[k][ft], mms[ft]], writes=[b])
                        T.op("dve", lambda e: e.scalar_tensor_tensor(out=b.t[:], in0=b.t[:], scalar=0.5, in1=xc.t[:, ft, :],
                                                                     op0=ALU.mult, op1=ALU.mult), reads=[b, xc], writes=[b])
                        ec = (TB - 1) if reverse else 0
                        T.op("dve", lambda e: e.scalar_tensor_tensor(out=b.t[:, ec:ec + 1], in0=aall_t[:, ft, ec:ec + 1],
                                                                     scalar=carry.t[:, ft:ft + 1], in1=b.t[:, ec:ec + 1],
                                                                     op0=ALU.mult, op1=ALU.add),
                             reads=[aas[ft], b, carry], writes=[b])
                        h = hr.next()
                        if not reverse:
                            T.op("dve", lambda e: e.tensor_tensor_scan(out=h.t[:], data0=aall_t[:, ft, :], data1=b.t[:],
                                                                       initial=0.0, op0=ALU.mult, op1=ALU.add),
                                 reads=[aas[ft], b], writes=[h])
                            T.op("dve", lambda e: e.tensor_copy(carry.t[:, ft:ft + 1], h.t[:, TB - 1:TB]),
                                 reads=[h], writes=[carry])
                        else:
                            T.op("dve", lambda e: e.tensor_tensor_scan(out=h.t[:, ::-1], data0=aall_t[:, ft, ::-1],
                                                                       data1=b.t[:, ::-1], initial=0.0, op0=ALU.mult,
                                                                       op1=ALU.add), reads=[aas[ft], b], writes=[h])
                            T.op("dve", lambda e: e.tensor_copy(carry.t[:, ft:ft + 1], h.t[:, 0:1]),
                                 reads=[h], writes=[carry])
                        post(ft, h)

                with contextlib.ExitStack() as esF:
                    dg = Buf(esF.enter_context(nc.sbuf_tensor("dg", [128, 4, 8, 128], BF16)), "dg")
                    for k in range(4):
                        for ft in range(8):
                            T.op("dve", lambda e: e.tensor_scalar(out=dg.t[:, k, ft, :], in0=ident_f.t[:],
                                                                  scalar1=cols.t[:, ocw + k * 8 + ft:ocw + k * 8 + ft + 1],
                                                                  scalar2=None, op0=ALU.mult), reads=[ident_f, cols], writes=[dg])
                    xaring = mkring(esF, "xah", [128, 8, TB + 3], BF16, 2)
                    hbring = mkring(esF, "hbf", [128, 8, TB], BF16, 2)
                    blocksF = [(s, t0) for s, S in enumerate(seq_lens) for t0 in range(0, S, TB)]

                    def prep_f(bi):
                        s, t0 = blocksF[bi]
                        S = seq_lens[s]
                        xa = xaring.next()
                        lo, hi = max(0, t0 - 2), min(S, t0 + TB + 1)
                        if lo != t0 - 2 or hi != t0 + TB + 1:
                            T.op("dve", lambda e: e.memset(xa.t[:], 0.0), writes=[xa])
                        c0 = lo - (t0 - 2)
                        T.dma("sp", xa.t[:, :, c0:c0 + (hi - lo)], fm(XA[s])[:, :, lo:hi], writes=[xa])
                        xc = xcring.next()
                        for ft in range(8):
                            pm = next_mm()
                            for k in range(4):
                                T.op("pe", lambda e: e.matmul(pm.t[:], lhsT=dg.t[:, k, ft, :], rhs=xa.t[:, ft, k:k + TB],
                                                              start=(k == 0), stop=(k == 3)), reads=[dg, xa], writes=[pm])
                            T.op("act", lambda e: e.activation(out=xc.t[:, ft, :], in_=pm.t[:], func=AF.Identity,
                                                               bias=cols.t[:, ocb + ft:ocb + ft + 1], scale=1.0),
                                 reads=[pm, cols], writes=[xc])
                        T.dma("pool", fm(XC[s])[:, :, t0:t0 + TB], xc.t[:], reads=[xc])
                        lru_phase1(xc, 0, bi % 2)
                        return xc, bi % 2

                    cur = prep_f(0)
                    for bi, (s, t0) in enumerate(blocksF):
                        xc, k = cur
                        if t0 == 0:
                            T.op("dve", lambda e: e.memset(carry.t[:], 0.0), writes=[carry])
                        lru_sqrt()
                        if bi + 1 < len(blocksF):
                            cur = prep_f(bi + 1)
                        hb = hbring.next()

                        def post_f(ft, h, hb=hb):
                            T.op("dve", lambda e: e.tensor_copy(hb.t[:, ft, :], h.t[:]), reads=[h], writes=[hb])
                        lru_dve(xc, k, False, post_f)
                        T.dma("pool", fm(HF[s])[:, :, t0:t0 + TB], hb.t[:], reads=[hb])
                    T.finish("sp")
                T.dma_barrier()

                with contextlib.ExitStack() as esR:
                    hfring = mkring(esR, "hfB", [128, 8, TB], BF16, 1)
                    gyring = mkring(esR, "gyB", [128, 8, TB], BF16, 1)
                    taring = mkring(esR, "taB", [128, 8, TB], BF16, 1)
                    ybgring = mkring(esR, "ybgB", [128, 8, TB], BF16, 1)
                    tmpr = mkring(esR, "tmpB", [128, TB], F32, 1)
                    x1ring = mkring(esR, "x1B", [128, D], F32, 1)
                    ssring = mkring(esR, "ssB", [128, 2], F32, 8)
                    xring = stg
                    blocksR = [(s, t0) for s, S in enumerate(seq_lens) for t0 in range(S - TB, -1, -TB)]

                    def prep_b(bi):
                        s, t0 = blocksR[bi]
                        xc = xcring.next()
                        T.dma("sp", xc.t[:], fm(XC[s])[:, :, t0:t0 + TB], writes=[xc])
                        lru_phase1(xc, 1, bi % 2)
                        return xc, bi % 2

                    cur = prep_b(0)
                    for bi, (s, t0) in enumerate(blocksR):
                        S = seq_lens[s]
                        xc, k = cur
                        if t0 == S - TB:
                            T.op("dve", lambda e: e.memset(carry.t[:], 0.0), writes=[carry])
                        lru_sqrt()
                        if bi + 1 < len(blocksR):
                            cur = prep_b(bi + 1)
                        hf = hfring.next()
                        T.dma("sp", hf.t[:], fm(HF[s])[:, :, t0:t0 + TB], writes=[hf])
                        gy = gyring.next()
                        T.dma("sp", gy.t[:], fm(GY[s])[:, :, t0:t0 + TB], writes=[gy])
                        ta = taring.next()
                        T.dma("sp", ta.t[:], fm(TA[s])[:, :, t0:t0 + TB], writes=[ta])
                        ybg = ybgring.next()
                        T.dma("sp", ybg.t[:], fm(YBG[s])[:, :, t0:t0 + TB], writes=[ybg])

                        def post_b(ft, h, hf=hf, gy=gy):
                            tmp = tmpr.next()
                            T.op("dve", lambda e: e.tensor_tensor(out=tmp.t[:], in0=h.t[:], in1=hf.t[:, ft, :],
                                                                  op=ALU.add), reads=[h, hf], writes=[tmp])
                            T.op("dve", lambda e: e.tensor_tensor(out=gy.t[:, ft, :], in0=tmp.t[:], in1=gy.t[:, ft, :],
                                                                  op=ALU.mult), reads=[tmp, gy], writes=[gy])
                        lru_dve(xc, k, True, post_b)
                        z = gy
                        for nt in range(8):
                            pm = next_mm()
                            for kc in range(8):
                                T.op("pe", lambda e: e.matmul(pm.t[:], lhsT=wao.t[:, kc, nt * 128:(nt + 1) * 128],
                                                              rhs=z.t[:, kc, :], start=(kc == 0), stop=(kc == 7)),
                                     reads=[wao, z], writes=[pm])
                            tmp = tmpr.next()
                            T.op("dve", lambda e: e.scalar_tensor_tensor(out=tmp.t[:], in0=ta.t[:, nt, :], scalar=1.0,
                                                                         in1=pm.t[:], op0=ALU.add, op1=ALU.mult),
                                 reads=[ta, pm], writes=[tmp])
                            T.op("dve", lambda e: e.tensor_tensor(out=ybg.t[:, nt, :], in0=tmp.t[:], in1=ybg.t[:, nt, :],
                                                                  op=ALU.add), reads=[tmp, ybg], writes=[ybg])
                        m = ybg
                        for i in range(4):
                            oh = [next_mm(), next_mm()]
                            for nh in range(2):
                                for kc in range(8):
                                    T.op("pe", lambda e: e.matmul(oh[nh].t[:],
                                                                  lhsT=m.t[:, kc, i * 128:(i + 1) * 128],
                                                                  rhs=wo.t[:, kc, nh * 512:(nh + 1) * 512],
                                                                  start=(kc == 0), stop=(kc == 7)),
                                         reads=[m, wo], writes=[oh[nh]])
                            xt = xring.next()
                            r0 = seq_off[s] + t0 + i * 128
                            T.dma("sp", xt.t[:], xin[r0:r0 + 128, :], writes=[xt])
                            ss = ssring.next()
                            rs = ssring.next()
                            x1 = x1ring.next()
                            tm_epilogue(oh, 0.5, ss, rs, x1, GT[0][s], xt, x1)
                            T.dma("pool", X1[s][t0 + i * 128:t0 + (i + 1) * 128, :], x1.t[:], reads=[x1])
                    T.finish("sp")
                T.finish("sp")
            T.dma_barrier()

        stop_here("B")
        with contextlib.ExitStack() as esC:
            def sbC(name, shape, dt=F32):
                return Buf(esC.enter_context(nc.sbuf_tensor(name, list(shape), dt)), name)

            class RingC(Ring):
                def __init__(self, name, shape, dt, n):
                    self.b = [sbC("%s%d" % (name, i), shape, dt) for i in range(n)]
                    self.i = 0

            wup = sbC("wup", [128, 8, 2 * DFF], BF16)
            wdn = sbC("wdn", [128, NFF, D], BF16)
            stg = RingC("stgC", [128, D], F32, 2)
            load_w_bf16(wup, w_up.rearrange("(kc p) n -> p kc n", p=128), 8, 2 * DFF, stg, nchunk=1024)
            load_w_bf16(wdn, w_down.rearrange("(kc p) n -> p kc n", p=128), NFF, D, stg, nchunk=1024)
            xring = Ring.__new__(Ring)
            xring.b = [stg.b[0]]
            xring.i = 0
            xnring = RingC("xnC", [128, D], BF16, 1)
            ssring = RingC("ssC", [128, 1], F32, 8)
            hT2 = RingC("h2T", [128, 8, TB], BF16, 2)
            U = RingC("U", [128, 2, TB + 2], BF16, 2)
            UP = sbC("UP", [128, NFF, 2, 2], BF16)
            tv = RingC("tv", [128, TB], F32, 2)
            tg = RingC("tg", [128, TB], F32, 1)
            gg = RingC("gg", [128, TB], BF16, 1)
            GTt = sbC("GTt", [128, NFF, TB], BF16)
            x1l = Ring.__new__(Ring)
            x1l.b = [stg.b[1]]
            x1l.i = 0
            ofw, ofb = VOFF["ffn_conv_w"], VOFF["ffn_conv_b"]
            x1l0 = x1l.b[0]
            T.op("pool", lambda e: e.memset(x1l0.t[:], 0.0), writes=[x1l0])

            ssring2 = RingC("ssC2", [128, 2], F32, 8)
            xjunk = stg.b[0]

            SEGP = 128
            if seg is not None:
                NTV = (seg[1] + 2 * SEGP) // 128
                idx_sb = sbC("segidx_sb", [128, NTV], mybir.dt.int32)
                idxl_sb = sbC("segidxl_sb", [128, NTV + 1], mybir.dt.int32)
                flag_sb = sbC("segflag_sb", [128, 2])
                T.dma("sp", idx_sb.t[:], segidx, writes=[idx_sb])
                T.dma("sp", idxl_sb.t[:], segidxl, writes=[idxl_sb])
                T.dma("sp", flag_sb.t[:], segflag, writes=[flag_sb])

            def gather_rows(dst, src_rows, idx_ap, idx_buf):
                T._deps("pool", [idx_buf], [dst])
                key = (dst.name, "pool")
                if key not in T.dsem:
                    T.dsem[key] = [es.enter_context(nc.semaphore("d%d" % len(T.dsem))), 0]
                d = T.dsem[key]
                inst = nc.gpsimd.indirect_dma_start(out=dst.t[:], out_offset=None, in_=src_rows,
                                                    in_offset=bass.IndirectOffsetOnAxis(ap=idx_ap, axis=0))
                d[1] += 16
                inst.then_inc(d[0], 16)
                T._mark(("dma", key), (("dma", key), d[1], d[0]), [idx_buf], [dst])

            def emit_norm_tile(s, t0n, i, h2, segm):
                xt = xring.next()
                if not segm:
                    T.dma("sp", xt.t[:], X1[s][t0n + i * 128:t0n + (i + 1) * 128, :], writes=[xt])
                else:
                    tl = (t0n + i * 128) // 128
                    gather_rows(xt, X1[s], idx_sb.t[:, tl:tl + 1], idx_sb)
                norm_transpose(xt, h2, i * 128, G2, 24, s, xnring, ssring, None)
                if segm:
                    tl = (t0n + i * 128) // 128
                    fl = 0 if tl == 0 else (1 if tl == NTV - 1 else None)
                    if fl is not None:
                        T.op("dve", lambda e: e.tensor_scalar(out=h2.t[:, :, i * 128:(i + 1) * 128],
                                                              in0=h2.t[:, :, i * 128:(i + 1) * 128],
                                                              scalar1=flag_sb.t[:, fl:fl + 1], scalar2=None, op0=ALU.mult),
                             reads=[h2, flag_sb], writes=[h2])

            for s, S in enumerate(seq_lens):
                segm = (seg is not None and seg[0] == s)
                if not segm:
                    blocks = [(j * TB, TB, False) for j in range(S // TB)] + [(S, 128, True)]
                else:
                    VL = seg[1] + 2 * SEGP
                    blocks = [(t, min(TB, VL - t), False) for t in range(0, VL, TB)]
                T.op("dve", lambda e: e.memset(UP.t[:], 0.0), writes=[UP])
                h2_next = hT2.next()
                for i in range(blocks[0][1] // 128):
                    emit_norm_tile(s, blocks[0][0], i, h2_next, segm)
                for bj, (t0, ncol, last) in enumerate(blocks):
                    h2 = h2_next
                    nxt = blocks[bj + 1] if (bj + 1 < len(blocks) and not blocks[bj + 1][2]) else None
                    if nxt is not None:
                        h2_next = hT2.next()
                    pend = None

                    def back(p):
                        pft, ptv, ptg = p
                        g_ = gg.next()
                        T.op("act", lambda e: e.activation(out=g_.t[:, 0:ncol], in_=ptg.t[:, 0:ncol],
                                                           func=AF.Gelu_apprx_tanh), reads=[ptg], writes=[g_])
                        return g_

                    def back2(p, g_):
                        pft, ptv, ptg = p
                        T.op("dve", lambda e: e.tensor_tensor(out=GTt.t[:, pft, 0:ncol], in0=g_.t[:, 0:ncol],
                                                              in1=ptv.t[:, 0:ncol], op=ALU.mult),
                             reads=[g_, ptv], writes=[GTt])

                    for ft in range(NFF):
                        u = U.next()
                        T.op("act", lambda e: e.activation(out=u.t[:, :, 0:2], in_=UP.t[:, ft, :, :], func=AF.Copy),
                             reads=[UP], writes=[u])
                        if not last:
                            for vg in range(2):
                                pm = next_mm()
                                c0 = vg * DFF + ft * 128
                                for kc in range(8):
                                    T.op("pe", lambda e: e.matmul(pm.t[:, 0:ncol], lhsT=wup.t[:, kc, c0:c0 + 128],
                                                                  rhs=h2.t[:, kc, 0:ncol], start=(kc == 0), stop=(kc == 7)),
                                         reads=[wup, h2], writes=[pm])
                                T.op("act", lambda e: e.activation(out=u.t[:, vg, 2:2 + ncol], in_=pm.t[:, 0:ncol], func=AF.Copy),
                                     reads=[pm], writes=[u])
                            T.op("act", lambda e: e.activation(out=UP.t[:, ft, :, :], in_=u.t[:, :, ncol:ncol + 2], func=AF.Copy),
                                 reads=[u], writes=[UP])
                        else:
                            T.op("dve", lambda e: e.memset(u.t[:, :, 2:2 + TB], 0.0), writes=[u])
                        tvv = tv.next()
                        tgg = tg.next()
                        cfv, cfg = ft, NFF + ft
                        T.op("act", lambda e: e.activation(out=tvv.t[:, 0:ncol], in_=u.t[:, 0, 0:ncol], func=AF.Identity,
                                                           scale=cols.t[:, ofw + cfv:ofw + cfv + 1],
                                                           bias=cols.t[:, ofb + cfv:ofb + cfv + 1]),
                             reads=[u, cols], writes=[tvv])
                        g_prev = back(pend) if pend is not None else None
                        T.op("act", lambda e: e.activation(out=tgg.t[:, 0:ncol], in_=u.t[:, 1, 0:ncol], func=AF.Identity,
                                                           scale=cols.t[:, ofw + cfg:ofw + cfg + 1],
                                                           bias=cols.t[:, ofb + cfg:ofb + cfg + 1]),
                             reads=[u, cols], writes=[tgg])
                        for k in (1, 2):
                            T.op("dve", lambda e: e.scalar_tensor_tensor(
                                out=tvv.t[:, 0:ncol], in0=u.t[:, 0, k:k + ncol],
                                scalar=cols.t[:, ofw + k * 44 + cfv:ofw + k * 44 + cfv + 1], in1=tvv.t[:, 0:ncol],
                                op0=ALU.mult, op1=ALU.add), reads=[u, cols, tvv], writes=[tvv])
                        if pend is not None:
                            back2(pend, g_prev)
                        for k in (1, 2):
                            T.op("dve", lambda e: e.scalar_tensor_tensor(
                                out=tgg.t[:, 0:ncol], in0=u.t[:, 1, k:k + ncol],
                                scalar=cols.t[:, ofw + k * 44 + cfg:ofw + k * 44 + cfg + 1], in1=tgg.t[:, 0:ncol],
                                op0=ALU.mult, op1=ALU.add), reads=[u, cols, tgg], writes=[tgg])
                        pend = (ft, tvv, tgg)
                        if nxt is not None and ft in (3, 7, 11, 15) and (ft - 3) // 4 < nxt[1] // 128:
                            emit_norm_tile(s, nxt[0], (ft - 3) // 4, h2_next, segm)
                    g_prev = back(pend)
                    back2(pend, g_prev)
                    for i in range(ncol // 128):
                        tk0 = t0 - 1 + i * 128
                        if not segm:
                            lo, hi = max(0, tk0), min(S, tk0 + 128)
                        else:
                            lo, hi = max(SEGP, tk0), min(SEGP + seg[1], tk0 + 128)
                        if hi <= lo:
                            continue
                        oh = [next_mm(), next_mm()]
                        for nh in range(2):
                            for kt in range(NFF):
                                T.op("pe", lambda e: e.matmul(oh[nh].t[:],
                                                              lhsT=GTt.t[:, kt, i * 128:(i + 1) * 128],
                                                              rhs=wdn.t[:, kt, nh * 512:(nh + 1) * 512],
                                                              start=(kt == 0), stop=(kt == NFF - 1)),
                                     reads=[GTt, wdn], writes=[oh[nh]])
                        p0 = lo - tk0
                        xl = x1l.next()
                        if not segm:
                            T.dma("sp", xl.t[p0:p0 + (hi - lo), :], X1[s][lo:hi, :], writes=[xl])
                            orow = out_off[s] + lo
                        else:
                            tl = (t0 + i * 128) // 128
                            gather_rows(xl, X1[s], idxl_sb.t[:, tl:tl + 1], idxl_sb)
                            orow = out_off[s] + lo - SEGP
                        ss = ssring2.next()
                        rs = ssring2.next()
                        tm_epilogue(oh, 1.0, ss, rs, xjunk, GT[1][s], xl, xl)
                        T.dma("pool", yout[orow:orow + (hi - lo), :], xl.t[p0:p0 + (hi - lo), :], reads=[xl])
            T.finish("sp")
        T.finish("sp")
    except _Stop:
        pass
    return nc


def dft_tables(seq_lens):
    p = np.arange(128)
    ang = 2.0 * np.pi * np.outer(p, p) / 128.0
    C, Sn = np.cos(ang), np.sin(ang)
    tabs = {"tabA": np.concatenate([C, Sn], 1).astype(np.float32),
            "tabW": (np.concatenate([C, -Sn], 1) / np.sqrt(128.0)).astype(np.float32)}
    for S in sorted(set(seq_lens)):
        L = S // 128
        Q = 128 // L
        ka = np.arange(128)[:, None]
        be = np.arange(L)[None, :]
        th = 2.0 * np.pi * (ka * be % S) / S
        sc = 1.0 / np.sqrt(float(S))
        tabs["tw%d" % L] = np.concatenate([np.cos(th) * sc, np.sin(th) * sc, -np.sin(th) * sc], 1).astype(np.float32)
        b = np.arange(L)
        a2 = 2.0 * np.pi * np.outer(b, b) / L
        KC = np.einsum("qp,bk->qbkp", np.eye(Q), np.cos(a2)).reshape(128, 128)
        KS = np.einsum("qp,bk->qbkp", np.eye(Q), np.sin(a2)).reshape(128, 128)
        tabs["kron%d" % L] = np.concatenate([KC, KS, -KS, KC], 1).astype(np.float32)
    return tabs


def make_vec(nseq, g_pre_mix, conv_w, conv_b, b_lru_a, b_lru_x, lru_lambda, g_pre_ffn, ffn_conv_w, ffn_conv_b,
             b_ada, cs):
    parts = [g_pre_mix.reshape(-1, 128), conv_w.reshape(-1, 128), conv_b.reshape(-1, 128),
             b_lru_a.reshape(-1, 128), b_lru_x.reshape(-1, 128), lru_lambda.reshape(-1, 128),
             g_pre_ffn.reshape(-1, 128), ffn_conv_w.reshape(-1, 128), ffn_conv_b.reshape(-1, 128),
             b_ada.reshape(-1, 128)] + [c.reshape(-1, 128) for c in cs]
    return np.ascontiguousarray(np.concatenate(parts, 0).astype(np.float32))


def seg_arrays(S, A, seglen, pad=128):
    ntv = (seglen + 2 * pad) // 128
    p = np.arange(128)[:, None]
    t = np.arange(ntv)[None, :]
    idx = np.clip(A - pad + 128 * t + p, 0, S - 1).astype(np.int32)
    tl = np.arange(ntv + 1)[None, :]
    idxl = np.clip(A - pad + 128 * tl - 1 + p, 0, S - 1).astype(np.int32)
    flag = np.zeros((128, 2), np.float32)
    flag[:, 0] = 1.0 if A - pad >= 0 else 0.0
    flag[:, 1] = 1.0 if A + seglen + pad <= S else 0.0
    return {"segidx": np.ascontiguousarray(idx), "segidxl": np.ascontiguousarray(idxl), "segflag": flag}


_NC_CACHE = {}


def kernel(x_prompt, x_sample, c_prompt, c_sample, w_ada, b_ada, g_pre_mix, w_in, conv_w, conv_b,
           w_lru_a, b_lru_a, w_lru_x, b_lru_x, lru_lambda, w_a_out, w_b_out, w_o, g_post_mix,
           g_pre_ffn, w_up, ffn_conv_w, ffn_conv_b, w_down, g_post_ffn):
    f = lambda a: np.ascontiguousarray(np.asarray(a, dtype=np.float32))
    x_prompt, x_sample, c_prompt, c_sample = f(x_prompt), f(x_sample), f(c_prompt), f(c_sample)
    B, S1, _ = x_prompt.shape
    B2, S2, _ = x_sample.shape
    per = B // N_CORES
    seq_lens = [S1] * per + [S2]
    nq = N_CORES // B2
    seglen = S2 // nq
    key = tuple(seq_lens)
    if key not in _NC_CACHE:
        _NC_CACHE[key] = build_nc(seq_lens, seg=(per, seglen))
    nc = _NC_CACHE[key]
    tabs = dft_tables(seq_lens)
    shared = {
        "w_ada": f(w_ada[0]), "b_ada": f(b_ada[0]).reshape(1, -1), "w_in": f(w_in[0]),
        "w_lru_a": f(w_lru_a[0]).reshape(16, 128, 128), "w_lru_x": f(w_lru_x[0]).reshape(16, 128, 128),
        "w_a_out": f(w_a_out[0]), "w_b_out": f(w_b_out[0]), "w_o": f(w_o[0]), "w_up": f(w_up[0]),
        "w_down": f(w_down[0]), "g_post_mix": f(g_post_mix[0]).reshape(1, -1),
        "g_post_ffn": f(g_post_ffn[0]).reshape(1, -1),
    }
    shared.update(tabs)
    in_maps = []
    for c in range(N_CORES):
        ps_ = list(range(c * per, (c + 1) * per))
        sm = c % B2
        xin = np.concatenate([x_prompt[p] for p in ps_] + [x_sample[sm]], 0)
        cs = [c_prompt[p] for p in ps_] + [c_sample[sm]]
        vecp = make_vec(len(seq_lens), f(g_pre_mix[0]), f(conv_w[0]), f(conv_b[0]), f(b_lru_a[0]), f(b_lru_x[0]),
                        f(lru_lambda[0]), f(g_pre_ffn[0]), f(ffn_conv_w[0]), f(ffn_conv_b[0]), f(b_ada[0]), cs)
        m = dict(shared)
        m["xin"] = np.ascontiguousarray(xin)
        m["vec"] = vecp
        m.update(seg_arrays(S2, (c // B2) * seglen, seglen))
        in_maps.append(m)
    res = run_bass_kernel_spmd(nc, in_maps, core_ids=list(range(N_CORES)))
    y_prompt = np.empty_like(x_prompt)
    y_sample = np.empty_like(x_sample)
    sm_of = lambda c: c % B2
    for c in range(N_CORES):
        yo = res.results[c]["yout"]
        for i in range(per):
            y_prompt[c * per + i] = yo[i * S1:(i + 1) * S1]
        q = c // B2
        y_sample[sm_of(c)][q * seglen:(q + 1) * seglen] = yo[per * S1:per * S1 + seglen]
    return (y_prompt, y_sample)
```

```python
import contextlib
import numpy as np
import concourse.bass as bass
import concourse.mybir as mybir
from concourse.bass_utils import run_bass_kernel_spmd

F32 = mybir.dt.float32
BF16 = mybir.dt.bfloat16
AF = mybir.ActivationFunctionType
ALU = mybir.AluOpType

D = 1024
DIN = 4608
DFF = 2816
NFF = 22
TB = 512
EPS = 1e-6
N_CORES = 8

def vec_layout(nseq):
    names = [("g_pre_mix", 8), ("conv_w", 32), ("conv_b", 8), ("b_lru_a", 16), ("b_lru_x", 16),
             ("lam", 16), ("g_pre_ffn", 8), ("ffn_conv_w", 132), ("ffn_conv_b", 44), ("b_ada", 48),
             ("c", 8 * nseq)]
    off = {}
    o = 0
    for n, k in names:
        off[n] = o
        o += k
    return off, o


class Buf:
    __slots__ = ("t", "name", "w", "r")

    def __init__(self, t, name):
        self.t = t
        self.name = name
        self.w = {}
        self.r = {}


class Trk:
    def __init__(self, nc, es):
        self.nc = nc
        self.es = es
        self.eng = {"pe": nc.tensor, "act": nc.scalar, "dve": nc.vector, "pool": nc.gpsimd, "sp": nc.sync}
        self.sem = {}
        self.cnt = {}
        self.waited = {}
        for e in self.eng:
            self.sem[e] = es.enter_context(nc.semaphore("s_" + e))
            self.cnt[e] = 0
        self.dsem = {}
        self.ninst = 0

    def _wait(self, e, tok):
        key, val, h = tok
        if key == ("eng", e) and e == "pe":
            return
        w = self.waited.setdefault(e, {})
        if w.get(key, 0) >= val:
            return
        self.eng[e].wait_ge(h, val)
        w[key] = val

    def _deps(self, e, reads, writes):
        for b in reads:
            for tok in b.w.values():
                self._wait(e, tok)
        for b in writes:
            for tok in b.w.values():
                self._wait(e, tok)
            for tok in b.r.values():
                self._wait(e, tok)

    def _mark(self, key, tok, reads, writes):
        for b in reads:
            b.r[key] = tok
        for b in writes:
            b.w = {key: tok}
            b.r = {}

    def op(self, e, fn, reads=(), writes=()):
        self._deps(e, reads, writes)
        inst = fn(self.eng[e])
        self.cnt[e] += 1
        self.ninst += 1
        inst.then_inc(self.sem[e], 1)
        self._mark(("eng", e), (("eng", e), self.cnt[e], self.sem[e]), reads, writes)
        return inst

    def dma(self, e, out, in_, reads=(), writes=(), key=None):
        self._deps(e, reads, writes)
        if key is None:
            key = writes[0].name if writes else reads[0].name
        key = (key, e)
        if key not in self.dsem:
            self.dsem[key] = [self.es.enter_context(self.nc.semaphore("d%d" % len(self.dsem))), 0]
        d = self.dsem[key]
        inst = self.eng[e].dma_start(out=out, in_=in_)
        d[1] += 16
        inst.then_inc(d[0], 16)
        self.ninst += 1
        self._mark(("dma", key), (("dma", key), d[1], d[0]), reads, writes)
        return inst

    def dma_barrier(self, engines=("sp", "pool")):
        for e in engines:
            for key, d in self.dsem.items():
                if d[1] > 0:
                    self._wait(e, (("dma", key), d[1], d[0]))

    def finish(self, e="sp"):
        self.dma_barrier((e,))
        for k in self.eng:
            if k != e and self.cnt[k] > 0:
                self._wait(e, (("eng", k), self.cnt[k], self.sem[k]))


class _Stop(Exception):
    pass


def build_nc(seq_lens, debug=False, stop=None, seg=None):
    nseq = len(seq_lens)
    TOT = sum(seq_lens)
    seq_off = [sum(seq_lens[:i]) for i in range(nseq)]
    Ls = sorted(set(s // 128 for s in seq_lens))
    VOFF, NVEC = vec_layout(nseq)

    nc = bass.Bass("TRN2", target_bir_lowering=False)

    def din(name, shape, dt=F32):
        return nc.dram_tensor(name, list(shape), dt, kind="ExternalInput").ap()

    skind = "ExternalOutput" if debug else "Internal"

    def dscr(name, shape, dt):
        return nc.dram_tensor(name, list(shape), dt, kind=skind).ap()

    xin = din("xin", [TOT, D])
    vec = din("vec", [NVEC, 128])
    w_ada = din("w_ada", [D, 6 * D])
    b_ada = din("b_ada", [1, 6 * D])
    w_in = din("w_in", [D, DIN])
    w_lru_a = din("w_lru_a", [16, 128, 128])
    w_lru_x = din("w_lru_x", [16, 128, 128])
    w_a_out = din("w_a_out", [D, D])
    w_b_out = din("w_b_out", [512, D])
    w_o = din("w_o", [D, D])
    w_up = din("w_up", [D, 2 * DFF])
    w_down = din("w_down", [DFF, D])
    g_post_mix = din("g_post_mix", [1, D])
    g_post_ffn = din("g_post_ffn", [1, D])
    tabA = din("tabA", [128, 256])
    tabW = din("tabW", [128, 256])
    tw_in = {L: din("tw%d" % L, [128, 3 * L]) for L in Ls}
    kron_in = {L: din("kron%d" % L, [128, 512]) for L in Ls}
    out_lens = [(seg[1] if (seg is not None and seg[0] == i) else S) for i, S in enumerate(seq_lens)]
    out_off = [sum(out_lens[:i]) for i in range(nseq)]
    yout = nc.dram_tensor("yout", [sum(out_lens), D], F32, kind="ExternalOutput").ap()
    if seg is not None:
        ntv = (seg[1] + 256) // 128
        segidx = din("segidx", [128, ntv], mybir.dt.int32)
        segidxl = din("segidxl", [128, ntv + 1], mybir.dt.int32)
        segflag = din("segflag", [128, 2])

    XA = [dscr("XA%d" % s, [D, S], BF16) for s, S in enumerate(seq_lens)]
    GY = [dscr("GY%d" % s, [D, S], BF16) for s, S in enumerate(seq_lens)]
    TA = [dscr("TA%d" % s, [D, S], BF16) for s, S in enumerate(seq_lens)]
    TBG = [dscr("TBG%d" % s, [D, S], BF16) for s, S in enumerate(seq_lens)]
    XC = [dscr("XC%d" % s, [D, S], BF16) for s, S in enumerate(seq_lens)]
    HF = [dscr("HF%d" % s, [D, S], BF16) for s, S in enumerate(seq_lens)]
    YBG = [dscr("YBG%d" % s, [D, S], BF16) for s, S in enumerate(seq_lens)]
    XB = [dscr("XB%d" % s, [S, 512], BF16) for s, S in enumerate(seq_lens)]
    YT = [dscr("YT%d" % s, [S, 1024], BF16) for s, S in enumerate(seq_lens)]
    X1 = [dscr("X1_%d" % s, [S, D], F32) for s, S in enumerate(seq_lens)]
    PQD = [dscr("PQD%d" % s, [D, S], BF16) for s, S in enumerate(seq_lens)]

    dbg_ab = [nc.dram_tensor("dbg_%s" % n, [128, TB], F32, kind="ExternalOutput").ap() for n in "abcd"] if debug else None
    dbg_done = []

    def fm(T):
        return T.rearrange("(ft p) s -> p ft s", p=128)

    es = contextlib.ExitStack()
    try:
      with es:
        T = Trk(nc, es)

        def stop_here(tag):
            if stop == tag:
                T.finish("sp")
                raise _Stop()

        def sb(name, shape, dt=F32):
            return Buf(es.enter_context(nc.sbuf_tensor(name, list(shape), dt)), name)

        def ps(name, shape, dt=F32):
            return Buf(es.enter_context(nc.psum_tensor(name, list(shape), dt)), name)

        NMM = 6
        ps_mm = [ps("ps_mm%d" % i, [128, 512]) for i in range(NMM)]
        ps_tr = [ps("ps_tr%d" % i, [128, 1024], BF16) for i in range(2)]
        mm_i = [0]
        tr_i = [0]

        def next_mm():
            b = ps_mm[mm_i[0] % NMM]
            mm_i[0] += 1
            return b

        def next_tr():
            b = ps_tr[tr_i[0] % 2]
            tr_i[0] += 1
            return b

        ident_f = sb("ident_f", [128, 128])
        ident_b = sb("ident_b", [128, 128], BF16)
        ones_f = sb("ones_f", [128, 128])
        nhalf = sb("nhalf", [128, 1])
        cols = sb("cols", [128, NVEC])
        dcols = sb("dcols", [128, 96])
        modc = sb("modc", [128, 48, nseq])
        G1 = sb("G1", [128, 8, nseq])
        G2 = sb("G2", [128, 8, nseq])
        GT = [[None] * nseq, [sb("GT1_%d" % s, [128, D]) for s in range(nseq)]]

        T.op("pool", lambda e: e.memset(ident_f.t[:], 0.0), writes=[ident_f])
        T.op("pool", lambda e: e.affine_select(out=ident_f.t[:], in_=ident_f.t[:], pattern=[[-1, 128]],
                                               compare_op=ALU.not_equal, fill=1.0, base=0,
                                               channel_multiplier=1), reads=[ident_f], writes=[ident_f])
        T.op("dve", lambda e: e.tensor_copy(ident_b.t[:], ident_f.t[:]), reads=[ident_f], writes=[ident_b])
        T.op("pool", lambda e: e.memset(ones_f.t[:], 1.0), writes=[ones_f])
        T.op("pool", lambda e: e.memset(nhalf.t[:], -0.5), writes=[nhalf])

        class Ring:
            def __init__(self, name, shape, dt, n):
                self.b = [sb("%s%d" % (name, i), shape, dt) for i in range(n)]
                self.i = 0

            def next(self):
                b = self.b[self.i % len(self.b)]
                self.i += 1
                return b

        cv_i = [0]
        cv_eng = ("act", "dve")

        def load_w_bf16(dst, src3, kcn, n, stage_ring, nchunk=2048):
            cv_i[0] += 1
            for kc in range(kcn):
                for n0 in range(0, n, nchunk):
                    nn = min(nchunk, n - n0)
                    st = stage_ring.next()
                    T.dma("sp", st.t[:, 0:nn], src3[:, kc, n0:n0 + nn], writes=[st])
                    e = cv_eng[cv_i[0] % 2]
                    if e == "act":
                        T.op(e, lambda en: en.activation(out=dst.t[:, kc, n0:n0 + nn], in_=st.t[:, 0:nn], func=AF.Copy),
                             reads=[st], writes=[dst])
                    else:
                        T.op(e, lambda en: en.tensor_copy(dst.t[:, kc, n0:n0 + nn], st.t[:, 0:nn]),
                             reads=[st], writes=[dst])

        def rstd_from_ss(ss, rs):
            T.op("dve", lambda e: e.tensor_scalar(out=ss.t[:], in0=ss.t[:], scalar1=1.0 / D, scalar2=EPS,
                                                  op0=ALU.mult, op1=ALU.add), reads=[ss], writes=[ss])
            T.op("pool", lambda e: e.tensor_tensor(out=rs.t[:], in0=ss.t[:], in1=nhalf.t[:], op=ALU.pow),
                 reads=[ss, nhalf], writes=[rs])

        def tm_epilogue(oh, sq_scale, ss2, rs, junk, GTb, xres, outb):
            for hh in range(2):
                T.op("act", lambda e: e.activation(out=junk.t[:, hh * 512:(hh + 1) * 512], in_=oh[hh].t[:], func=AF.Square,
                                                   scale=sq_scale, accum_out=ss2.t[:, hh:hh + 1]),
                     reads=[oh[hh]], writes=[junk, ss2])
            T.op("dve", lambda e: e.tensor_tensor(out=ss2.t[:, 0:1], in0=ss2.t[:, 0:1], in1=ss2.t[:, 1:2], op=ALU.add),
                 reads=[ss2], writes=[ss2])
            T.op("dve", lambda e: e.tensor_scalar(out=ss2.t[:, 0:1], in0=ss2.t[:, 0:1], scalar1=1.0 / D, scalar2=EPS,
                                                  op0=ALU.mult, op1=ALU.add), reads=[ss2], writes=[ss2])
            T.op("pool", lambda e: e.tensor_tensor(out=rs.t[:, 0:1], in0=ss2.t[:, 0:1], in1=nhalf.t[:], op=ALU.pow),
                 reads=[ss2, nhalf], writes=[rs])
            for hh in range(2):
                hs = slice(hh * 512, (hh + 1) * 512)
                T.op("dve", lambda e: e.tensor_tensor(out=junk.t[:, hs], in0=oh[hh].t[:], in1=GTb.t[:, hs], op=ALU.mult),
                     reads=[oh[hh], GTb], writes=[junk])
                T.op("dve", lambda e: e.scalar_tensor_tensor(out=outb.t[:, hs], in0=junk.t[:, hs], scalar=rs.t[:, 0:1],
                                                             in1=xres.t[:, hs], op0=ALU.mult, op1=ALU.add),
                     reads=[junk, rs, xres], writes=[outb])

        with contextlib.ExitStack() as esG0:
            for s_ in range(nseq):
                GT[0][s_] = Buf(esG0.enter_context(nc.sbuf_tensor("GT0_%d" % s_, [128, D], F32)), "GT0_%d" % s_)
            with contextlib.ExitStack() as es0:
                def sb0(name, shape, dt=F32):
                    return Buf(es0.enter_context(nc.sbuf_tensor(name, list(shape), dt)), name)

                nchunks = (NVEC + 127) // 128
                for ci in range(nchunks):
                    r0 = ci * 128
                    r = min(128, NVEC - r0)
                    vt = sb0("vec%d" % ci, [128, 128])
                    T.dma("sp", vt.t[0:r, :], vec[r0:r0 + r, :], writes=[vt])
                    pm = next_mm()
                    T.op("pe", lambda e: e.transpose(pm.t[:, 0:r], vt.t[0:r, :], ident_f.t[0:r, 0:r]),
                         reads=[vt, ident_f], writes=[pm])
                    T.op("dve", lambda e: e.tensor_copy(cols.t[:, r0:r0 + r], pm.t[:, 0:r]), reads=[pm], writes=[cols])

                o_ba, o_bx, o_lam = VOFF["b_lru_a"], VOFF["b_lru_x"], VOFF["lam"]
                T.op("dve", lambda e: e.tensor_scalar(out=dcols.t[:, 0:16], in0=cols.t[:, o_ba:o_ba + 16], scalar1=0.5,
                                                      scalar2=None, op0=ALU.mult), reads=[cols], writes=[dcols])
                T.op("dve", lambda e: e.tensor_scalar(out=dcols.t[:, 16:32], in0=cols.t[:, o_bx:o_bx + 16], scalar1=0.5,
                                                      scalar2=None, op0=ALU.mult), reads=[cols], writes=[dcols])
                T.op("act", lambda e: e.activation(out=dcols.t[:, 64:80], in_=cols.t[:, o_lam:o_lam + 16], func=AF.Exp,
                                                   scale=-1.0), reads=[cols], writes=[dcols])
                T.op("act", lambda e: e.activation(out=dcols.t[:, 80:96], in_=dcols.t[:, 64:80], func=AF.Ln,
                                                   bias=1.0, scale=1.0), reads=[dcols], writes=[dcols])
                T.op("dve", lambda e: e.tensor_scalar(out=dcols.t[:, 32:48], in0=dcols.t[:, 80:96], scalar1=-4.0,
                                                      scalar2=None, op0=ALU.mult), reads=[dcols], writes=[dcols])
                T.op("dve", lambda e: e.tensor_scalar(out=dcols.t[:, 48:64], in0=dcols.t[:, 80:96], scalar1=-8.0,
                                                      scalar2=None, op0=ALU.mult), reads=[dcols], writes=[dcols])
                siluc = sb0("siluc", [128, 8 * nseq])
                oc = VOFF["c"]
                T.op("act", lambda e: e.activation(out=siluc.t[:], in_=cols.t[:, oc:oc + 8 * nseq], func=AF.Silu),
                     reads=[cols], writes=[siluc])
                silr = sb0("silr", [128, 8, nseq])
                for s in range(nseq):
                    T.op("dve", lambda e: e.tensor_copy(silr.t[:, :, s], siluc.t[:, s * 8:(s + 1) * 8]),
                         reads=[siluc], writes=[silr])
                bcs = []
                for s in range(nseq):
                    bc = sb0("bc%d" % s, [128, 8, 128], BF16)
                    for kc in range(8):
                        T.op("dve", lambda e: e.tensor_scalar(out=bc.t[:, kc, :], in0=ones_f.t[:],
                                                              scalar1=siluc.t[:, s * 8 + kc:s * 8 + kc + 1],
                                                              scalar2=None, op0=ALU.mult),
                             reads=[ones_f, siluc], writes=[bc])
                    bcs.append(bc)
                brow = sb0("brow", [128, 2, D])
                grow = sb0("grow", [128, 2, D])
                T.dma("sp", brow.t[:, 0, :], b_ada[:, 2 * D:3 * D].partition_broadcast(128), writes=[brow])
                T.dma("sp", brow.t[:, 1, :], b_ada[:, 5 * D:6 * D].partition_broadcast(128), writes=[brow])
                T.dma("sp", grow.t[:, 0, :], g_post_mix.partition_broadcast(128), writes=[grow])
                T.dma("sp", grow.t[:, 1, :], g_post_ffn.partition_broadcast(128), writes=[grow])

                wring = [sb0("wada%d" % i, [128, 8, 512]) for i in range(2)]
                wtb_a = sb0("wadab_a", [128, 4, 512], BF16)
                wtb_d = sb0("wadab_d", [128, 4, 512], BF16)
                w_ada3 = w_ada.rearrange("(kc p) n -> p kc n", p=128)
                oba = VOFF["b_ada"]
                for ch in range(12):
                    wt = wring[ch % 2]
                    T.dma("sp", wt.t[:], w_ada3[:, :, ch * 512:(ch + 1) * 512], writes=[wt])
                    which = {4: 0, 5: 0, 10: 1, 11: 1}.get(ch)
                    if which is None:
                        for j in range(4):
                            nt = ch * 4 + j
                            pm = next_mm()
                            for kc in range(8):
                                T.op("pe", lambda e: e.matmul(pm.t[:, 0:nseq], lhsT=wt.t[:, kc, j * 128:(j + 1) * 128],
                                                              rhs=silr.t[:, kc, :], start=(kc == 0), stop=(kc == 7)),
                                     reads=[wt, silr], writes=[pm])
                            T.op("dve", lambda e: e.tensor_scalar(out=modc.t[:, nt, :], in0=pm.t[:, 0:nseq],
                                                                  scalar1=cols.t[:, oba + nt:oba + nt + 1], scalar2=None,
                                                                  op0=ALU.add), reads=[pm, cols], writes=[modc])
                    else:
                        half = (ch % 2)
                        if ch in (4, 10):
                            half = 0
                        else:
                            half = 1
                        fac = 0.5 if which == 0 else 1.0
                        T.op("dve", lambda e: e.tensor_copy(wtb_d.t[:], wt.t[:, 0:4, :]), reads=[wt], writes=[wtb_d])
                        T.op("act", lambda e: e.activation(out=wtb_a.t[:], in_=wt.t[:, 4:8, :], func=AF.Copy),
                             reads=[wt], writes=[wtb_a])
                        for s in range(nseq):
                            pm = next_mm()
                            for kc in range(8):
                                wsrc = wtb_d if kc < 4 else wtb_a
                                T.op("pe", lambda e: e.matmul(pm.t[:], lhsT=bcs[s].t[:, kc, :], rhs=wsrc.t[:, kc % 4, :],
                                                              start=(kc == 0), stop=(kc == 7)),
                                     reads=[bcs[s], wsrc], writes=[pm])
                            dst = GT[which][s]
                            sl = slice(half * 512, (half + 1) * 512)
                            T.op("dve", lambda e: e.tensor_tensor(out=dst.t[:, sl], in0=pm.t[:], in1=brow.t[:, which, sl],
                                                                  op=ALU.add), reads=[pm, brow], writes=[dst])
                            T.op("dve", lambda e: e.scalar_tensor_tensor(out=dst.t[:, sl], in0=dst.t[:, sl], scalar=fac,
                                                                         in1=grow.t[:, which, sl], op0=ALU.mult,
                                                                         op1=ALU.mult), reads=[dst, grow], writes=[dst])
                og1, og2 = VOFF["g_pre_mix"], VOFF["g_pre_ffn"]
                for ft in range(8):
                    T.op("dve", lambda e: e.tensor_scalar(out=G1.t[:, ft, :], in0=modc.t[:, 8 + ft, :], scalar1=1.0,
                                                          scalar2=cols.t[:, og1 + ft:og1 + ft + 1], op0=ALU.add,
                                                          op1=ALU.mult), reads=[modc, cols], writes=[G1])
                    T.op("dve", lambda e: e.tensor_scalar(out=G2.t[:, ft, :], in0=modc.t[:, 32 + ft, :], scalar1=1.0,
                                                          scalar2=cols.t[:, og2 + ft:og2 + ft + 1], op0=ALU.add,
                                                          op1=ALU.mult), reads=[modc, cols], writes=[G2])
                T.finish("sp")

            def norm_transpose(xt, hT, col0, Gm, SHbase, s, xn_ring, ss_ring, junk):
                ss = ss_ring.next()
                rs = ss_ring.next()
                xn = xn_ring.next()
                T.op("act", lambda e: e.activation(out=xn.t[:], in_=xt.t[:], func=AF.Square, accum_out=ss.t[:]),
                     reads=[xt], writes=[xn, ss])
                rstd_from_ss(ss, rs)
                T.op("act", lambda e: e.activation(out=xn.t[:], in_=xt.t[:], func=AF.Copy, scale=rs.t[:]),
                     reads=[xt, rs], writes=[xn])
                pt = next_tr()
                for ft in range(8):
                    T.op("pe", lambda e: e.transpose(pt.t[:, ft * 128:(ft + 1) * 128], xn.t[:, ft * 128:(ft + 1) * 128],
                                                     ident_b.t[:]), reads=[xn, ident_b], writes=[pt])
                for ft in range(8):
                    T.op("dve", lambda e: e.tensor_scalar(out=hT.t[:, ft, col0:col0 + 128],
                                                          in0=pt.t[:, ft * 128:(ft + 1) * 128],
                                                          scalar1=Gm.t[:, ft, s:s + 1],
                                                          scalar2=modc.t[:, SHbase + ft, s:s + 1],
                                                          op0=ALU.mult, op1=ALU.add),
                         reads=[pt, Gm, modc], writes=[hT])

            stop_here("0")
            with contextlib.ExitStack() as esA:
                def sbA(name, shape, dt=F32):
                    return Buf(esA.enter_context(nc.sbuf_tensor(name, list(shape), dt)), name)

                class RingA(Ring):
                    def __init__(self, name, shape, dt, n):
                        self.b = [sbA("%s%d" % (name, i), shape, dt) for i in range(n)]
                        self.i = 0

                win = sbA("win", [128, 8, DIN], BF16)
                with contextlib.ExitStack() as esS:
                    stg = Ring.__new__(Ring)
                    stg.b = [Buf(esS.enter_context(nc.sbuf_tensor("stgA%d" % i, [128, 2304], F32)), "stgA%d" % i) for i in range(2)]
                    stg.i = 0
                    load_w_bf16(win, w_in.rearrange("(kc p) n -> p kc n", p=128), 8, DIN, stg, nchunk=2304)
                    T.finish("sp")
                xring = RingA("xA", [128, D], F32, 2)
                xnring = RingA("xnA", [128, D], BF16, 2)
                ssring = RingA("ssA", [128, 1], F32, 8)
                junk = None
                hring = RingA("hT", [128, 8, TB], BF16, 2)
                oring = RingA("oA", [128, 4, TB], BF16, 3)
                xbring = RingA("xbA", [128, 512], BF16, 2)

                def norm_tile_A(s, t0n, i, hTn):
                    xt = xring.next()
                    r0 = seq_off[s] + t0n + i * 128
                    T.dma("sp", xt.t[:], xin[r0:r0 + 128, :], writes=[xt])
                    norm_transpose(xt, hTn, i * 128, G1, 0, s, xnring, ssring, junk)

                blocksA = [(s, t0) for s, S in enumerate(seq_lens) for t0 in range(0, S, TB)]
                hT_next = hring.next()
                for i in range(4):
                    norm_tile_A(blocksA[0][0], blocksA[0][1], i, hT_next)
                for bi, (s, t0) in enumerate(blocksA):
                    S = seq_lens[s]
                    if True:
                        hT = hT_next
                        nxt = blocksA[bi + 1] if bi + 1 < len(blocksA) else None
                        if nxt is not None:
                            hT_next = hring.next()
                        for grp in range(9):
                            if nxt is not None and grp in (1, 3, 5, 7):
                                norm_tile_A(nxt[0], nxt[1], (grp - 1) // 2, hT_next)
                            if grp == 4:
                                continue
                            ot = oring.next()
                            for j4 in range(4):
                                j = grp * 4 + j4
                                pm = next_mm()
                                for kc in range(8):
                                    T.op("pe", lambda e: e.matmul(pm.t[:], lhsT=win.t[:, kc, j * 128:(j + 1) * 128],
                                                                  rhs=hT.t[:, kc, :], start=(kc == 0), stop=(kc == 7)),
                                         reads=[win, hT], writes=[pm])
                                if grp < 2:
                                    T.op("dve", lambda e: e.tensor_copy(ot.t[:, j4, :], pm.t[:]), reads=[pm], writes=[ot])
                                elif grp < 4:
                                    T.op("act", lambda e: e.activation(out=ot.t[:, j4, :], in_=pm.t[:],
                                                                       func=AF.Gelu_apprx_tanh), reads=[pm], writes=[ot])
                                else:
                                    T.op("act", lambda e: e.activation(out=ot.t[:, j4, :], in_=pm.t[:], func=AF.Tanh,
                                                                       scale=0.5), reads=[pm], writes=[ot])
                            if grp < 2:
                                dstT, f0 = XA[s], grp * 4
                            elif grp < 4:
                                dstT, f0 = GY[s], (grp - 2) * 4
                            elif grp < 7:
                                dstT, f0 = TA[s], (grp - 5) * 4
                            else:
                                dstT, f0 = TBG[s], (grp - 7) * 4
                            T.dma("pool", fm(dstT)[:, f0:f0 + 4, t0:t0 + TB], ot.t[:], reads=[ot])
                        for i in range(4):
                            pm = next_mm()
                            for kc in range(8):
                                T.op("pe", lambda e: e.matmul(pm.t[:], lhsT=hT.t[:, kc, i * 128:(i + 1) * 128],
                                                              rhs=win.t[:, kc, 2048:2560], start=(kc == 0), stop=(kc == 7)),
                                     reads=[hT, win], writes=[pm])
                            xb = xbring.next()
                            T.op("dve", lambda e: e.tensor_copy(xb.t[:], pm.t[:]), reads=[pm], writes=[xb])
                            T.dma("pool", XB[s][t0 + i * 128:t0 + (i + 1) * 128, :], xb.t[:], reads=[xb])
                T.finish("sp")
            T.dma_barrier()

            stop_here("A")
            with contextlib.ExitStack() as esD:
                def sbD(name, shape, dt=F32):
                    return Buf(esD.enter_context(nc.sbuf_tensor(name, list(shape), dt)), name)

                class RingD(Ring):
                    def __init__(self, name, shape, dt, n):
                        self.b = [sbD("%s%d" % (name, i), shape, dt) for i in range(n)]
                        self.i = 0

                Smax = max(seq_lens)
                PQ = sbD("PQ", [128, 4, Smax], BF16)
                PQa = Buf(PQ.t, "PQa")
                PQd = Buf(PQ.t, "PQd")
                Wp = sbD("Wp", [128, 8, D], BF16)
                tAb = sbD("tAb", [128, 256], BF16)
                with contextlib.ExitStack() as esS:
                    def sbS(name, shape, dt=F32):
                        return Buf(esS.enter_context(nc.sbuf_tensor(name, list(shape), dt)), name)
                    tA = sbS("tA", [128, 256])
                    tW = sbS("tW", [128, 256])
                    wb = sbS("wb", [128, 4, D])
                    T.dma("sp", tA.t[:], tabA, writes=[tA])
                    T.dma("sp", tW.t[:], tabW, writes=[tW])
                    T.dma("sp", wb.t[:], w_b_out.rearrange("(g p) n -> p g n", p=128), writes=[wb])
                    T.op("dve", lambda e: e.tensor_copy(tAb.t[:], tA.t[:]), reads=[tA], writes=[tAb])
                    for g in range(4):
                        for pq in range(2):
                            for nh in range(2):
                                pm = next_mm()
                                T.op("pe", lambda e: e.matmul(pm.t[:], lhsT=tW.t[:, pq * 128:(pq + 1) * 128],
                                                              rhs=wb.t[:, g, nh * 512:(nh + 1) * 512], start=True, stop=True),
                                     reads=[tW, wb], writes=[pm])
                                T.op("dve", lambda e: e.tensor_copy(Wp.t[:, g * 2 + pq, nh * 512:(nh + 1) * 512], pm.t[:]),
                                     reads=[pm], writes=[Wp])
                    T.finish("sp")
                twt = {}
                krb = {}
                for L in Ls:
                    twt[L] = sbD("twsb%d" % L, [128, 3 * L])
                    T.dma("sp", twt[L].t[:], tw_in[L], writes=[twt[L]])
                    kf = sbD("krf%d" % L, [128, 512])
                    T.dma("sp", kf.t[:], kron_in[L], writes=[kf])
                    krb[L] = sbD("krb%d" % L, [128, 512], BF16)
                    T.op("dve", lambda e: e.tensor_copy(krb[L].t[:], kf.t[:]), reads=[kf], writes=[krb[L]])

                stop_here("D0")
                BCH = 8
                d1ring = RingD("d1", [128, BCH, 512], BF16, 2)
                t12 = RingD("t12", [128, 2, 512], F32, 2)
                ytring = RingD("yt", [128, 2, 512], BF16, 3)
                zring = RingD("z", [128, 2, 512], BF16, 3)
                tbring = RingD("tbD", [128, 4, TB], BF16, 2)
                ybring = RingD("ybD", [128, 4, TB], BF16, 2)
                pqring = RingD("pqD", [128, 8, TB], BF16, 2)

                for s, S in enumerate(seq_lens):
                    L = S // 128
                    Q = 128 // L
                    tw = twt[L]
                    xb3 = XB[s].rearrange("(a b) c -> a b c", b=L)
                    yt3 = YT[s].rearrange("(a b) c -> a b c", b=L)
                    for b0 in range(0, L, BCH):
                        d1 = d1ring.next()
                        T.dma("sp", d1.t[:], xb3[:, b0:b0 + BCH, :], writes=[d1])
                        for bb in range(BCH):
                            beta = b0 + bb
                            pr = next_mm()
                            pi = next_mm()
                            T.op("pe", lambda e: e.matmul(pr.t[:], lhsT=tAb.t[:, 0:128], rhs=d1.t[:, bb, :], start=True,
                                                          stop=True), reads=[tAb, d1], writes=[pr])
                            T.op("pe", lambda e: e.matmul(pi.t[:], lhsT=tAb.t[:, 128:256], rhs=d1.t[:, bb, :], start=True,
                                                          stop=True), reads=[tAb, d1], writes=[pi])
                            tt = t12.next()
                            tc_ = tw.t[:, beta:beta + 1]
                            ts_ = tw.t[:, L + beta:L + beta + 1]
                            nts_ = tw.t[:, 2 * L + beta:2 * L + beta + 1]
                            T.op("act", lambda e: e.activation(out=tt.t[:, 0, :], in_=pr.t[:], func=AF.Copy, scale=tc_),
                                 reads=[pr, tw], writes=[tt])
                            T.op("act", lambda e: e.activation(out=tt.t[:, 1, :], in_=pi.t[:], func=AF.Copy, scale=tc_),
                                 reads=[pi, tw], writes=[tt])
                            yt = ytring.next()
                            T.op("dve", lambda e: e.scalar_tensor_tensor(out=yt.t[:, 0, :], in0=pi.t[:], scalar=nts_,
                                                                         in1=tt.t[:, 0, :], op0=ALU.mult, op1=ALU.add),
                                 reads=[pi, tw, tt], writes=[yt])
                            T.op("dve", lambda e: e.scalar_tensor_tensor(out=yt.t[:, 1, :], in0=pr.t[:], scalar=ts_,
                                                                         in1=tt.t[:, 1, :], op0=ALU.mult, op1=ALU.add),
                                 reads=[pr, tw, tt], writes=[yt])
                            T.dma("pool", yt3[:, beta, :], yt.t[:].rearrange("p a c -> p (a c)"), reads=[yt])
                    T.dma_barrier()
                    stop_here("D1")
                    kr = krb[L]
                    for g2 in range(2):
                      for ka in range(L):
                        z = zring.next()
                        T.dma("sp", z.t[:].rearrange("p a c -> p (a c)"), YT[s][ka * 128:(ka + 1) * 128, :], writes=[z])
                        if True:
                            pms = [next_mm(), next_mm()]
                            for gg in range(2):
                                g = g2 * 2 + gg
                                pm = pms[gg]
                                T.op("pe", lambda e: e.matmul(pm.t[:, 0:256], lhsT=z.t[:, 0, g * 128:(g + 1) * 128],
                                                              rhs=kr.t[:, 0:256], start=True, stop=False),
                                     reads=[z, kr], writes=[pm])
                                T.op("pe", lambda e: e.matmul(pm.t[:, 0:256], lhsT=z.t[:, 1, g * 128:(g + 1) * 128],
                                                              rhs=kr.t[:, 256:512], start=False, stop=True),
                                     reads=[z, kr], writes=[pm])
                                src = pm.t[:, 0:256].rearrange("p (c b q) -> p c b q", c=2, q=Q)
                                dst = PQ.t[:, gg * 2:gg * 2 + 2, 0:S].rearrange("p c (b r) -> p c b r", r=128)[:, :, :, Q * ka:Q * ka + Q]
                                T.op("dve", lambda e: e.tensor_copy(dst, src), reads=[pm], writes=[PQd])
                      for c4 in range(4):
                        T.dma("pool", PQD[s][(g2 * 4 + c4) * 128:(g2 * 4 + c4 + 1) * 128, :], PQ.t[:, c4, 0:S], reads=[PQa, PQd], key="PQ")
                    T.dma_barrier()
                    stop_here("D2")
                    for t0 in range(0, S, TB):
                        pqb = pqring.next()
                        T.dma("sp", pqb.t[:], fm(PQD[s])[:, :, t0:t0 + TB], writes=[pqb])
                        for nh in range(2):
                            tb = tbring.next()
                            T.dma("sp", tb.t[:], fm(TBG[s])[:, nh * 4:(nh + 1) * 4, t0:t0 + TB], writes=[tb])
                            yb = ybring.next()
                            for j4 in range(4):
                                nt = nh * 4 + j4
                                pm = next_mm()
                                for k8 in range(8):
                                    T.op("pe", lambda e: e.matmul(pm.t[:], lhsT=Wp.t[:, k8, nt * 128:(nt + 1) * 128],
                                                                  rhs=pqb.t[:, k8, :], start=(k8 == 0), stop=(k8 == 7)),
                                         reads=[Wp, pqb], writes=[pm])
                                T.op("dve", lambda e: e.scalar_tensor_tensor(out=yb.t[:, j4, :], in0=tb.t[:, j4, :], scalar=1.0,
                                                                             in1=pm.t[:], op0=ALU.add, op1=ALU.mult),
                                     reads=[tb, pm], writes=[yb])
                            T.dma("pool", fm(YBG[s])[:, nh * 4:(nh + 1) * 4, t0:t0 + TB], yb.t[:], reads=[yb])
                T.finish("sp")
            T.dma_barrier()

            stop_here("D")
            with contextlib.ExitStack() as esB:
                def sbB(name, shape, dt=F32):
                    return Buf(esB.enter_context(nc.sbuf_tensor(name, list(shape), dt)), name)

                def mkring(stack, name, shape, dt, n):
                    r = Ring.__new__(Ring)
                    r.b = [Buf(stack.enter_context(nc.sbuf_tensor("%s%d" % (name, i), list(shape), dt)), "%s%d" % (name, i))
                           for i in range(n)]
                    r.i = 0
                    return r

                wla = sbB("wla", [128, 16, 128], BF16)
                wlx = sbB("wlx", [128, 16, 128], BF16)
                wao = sbB("wao", [128, 8, D], BF16)
                wo = sbB("wo", [128, 8, D], BF16)
                stg = mkring(esB, "stgB", [128, D], F32, 2)
                for (dst, src) in ((wla, w_lru_a), (wlx, w_lru_x)):
                    src3 = src.rearrange("h i j -> i h j")
                    for h0 in range(0, 16, 8):
                        st = stg.next()
                        T.dma("sp", st.t[:].rearrange("p (h j) -> p h j", h=8), src3[:, h0:h0 + 8, :], writes=[st])
                        T.op("dve", lambda e: e.tensor_copy(dst.t[:, h0:h0 + 8, :],
                                                            st.t[:].rearrange("p (h j) -> p h j", h=8)),
                             reads=[st], writes=[dst])
                load_w_bf16(wao, w_a_out.rearrange("(kc p) n -> p kc n", p=128), 8, D, stg, nchunk=1024)
                load_w_bf16(wo, w_o.rearrange("(kc p) n -> p kc n", p=128), 8, D, stg, nchunk=1024)
                ocw = VOFF["conv_w"]
                ocb = VOFF["conv_b"]
                carry = sbB("carry", [128, 8])

                xcring = mkring(esB, "xc", [128, 8, TB], BF16, 2)
                trr = mkring(esB, "trB", [128, TB], F32, 1)
                br = mkring(esB, "bB", [128, TB], F32, 2)
                hr = mkring(esB, "hB", [128, TB], F32, 2)
                a2all_t = esB.enter_context(nc.sbuf_tensor("a2all", [128, 8, TB], F32))
                aall_t = esB.enter_context(nc.sbuf_tensor("aall", [128, 8, TB], F32))
                mall_t = esB.enter_context(nc.sbuf_tensor("mall", [128, 8, TB], F32))
                tiall_t = [esB.enter_context(nc.sbuf_tensor("tiall%d" % k, [128, 8, TB], BF16)) for k in range(2)]
                a2s = [Buf(a2all_t, "a2s%d" % i) for i in range(8)]
                aas = [Buf(aall_t, "aas%d" % i) for i in range(8)]
                mms = [Buf(mall_t, "mms%d" % i) for i in range(8)]
                tis = [[Buf(tiall_t[k], "tis%d_%d" % (k, i)) for i in range(8)] for k in range(2)]

                def lru_phase1(xc, d, k):
                    for ft in range(8):
                        pr = next_mm()
                        pi = next_mm()
                        T.op("pe", lambda e: e.matmul(pr.t[:], lhsT=wla.t[:, d * 8 + ft, :], rhs=xc.t[:, ft, :], start=True,
                                                      stop=True), reads=[wla, xc], writes=[pr])
                        T.op("pe", lambda e: e.matmul(pi.t[:], lhsT=wlx.t[:, d * 8 + ft, :], rhs=xc.t[:, ft, :], start=True,
                                                      stop=True), reads=[wlx, xc], writes=[pi])
                        ci = d * 8 + ft
                        t_r = trr.next()
                        T.op("act", lambda e: e.activation(out=t_r.t[:], in_=pr.t[:], func=AF.Tanh, scale=0.5,
                                                           bias=dcols.t[:, ci:ci + 1]), reads=[pr, dcols], writes=[t_r])
                        T.op("act", lambda e: e.activation(out=tiall_t[k][:, ft, :], in_=pi.t[:], func=AF.Tanh, scale=0.5,
                                                           bias=dcols.t[:, 16 + ci:16 + ci + 1]), reads=[pi, dcols],
                             writes=[tis[k][ft]])
                        T.op("act", lambda e: e.activation(out=a2all_t[:, ft, :], in_=t_r.t[:], func=AF.Exp,
                                                           scale=dcols.t[:, 48 + ci:48 + ci + 1],
                                                           bias=dcols.t[:, 48 + ci:48 + ci + 1]), reads=[t_r, dcols],
                             writes=[a2s[ft]])

                def lru_sqrt():
                    for ft in range(8):
                        T.op("act", lambda e: e.activation(out=aall_t[:, ft, :], in_=a2all_t[:, ft, :], func=AF.Sqrt),
                             reads=[a2s[ft]], writes=[aas[ft]])
                        T.op("act", lambda e: e.activation(out=mall_t[:, ft, :], in_=a2all_t[:, ft, :], func=AF.Sqrt,
                                                           scale=-(1.0 - 2.0 ** -22), bias=1.0), reads=[a2s[ft]], writes=[mms[ft]])

                def lru_dve(xc, k, reverse, post):
                    for ft in range(8):
                        b = br.next()
                        T.op("dve", lambda e: e.scalar_tensor_tensor(out=b.t[:], in0=tiall_t[k][:, ft, :], scalar=1.0,
                                                                     in1=mall_t[:, ft, :], op0=ALU.add, op1=ALU.mult),
                             reads=[tis[k][ft], mms[ft]], writes=[b])
                        T.op("dve", lambda e: e.scalar_tensor_tensor(out=b.t[:], in0=b.t[:], scalar=0.5, in1=xc.t[:, ft, :],
                                                                     op0=ALU.mult, op1=ALU.mult), reads=[b, xc], writes=[b])
                        ec = (TB - 1) if reverse else 0
                        T.op("dve", lambda e: e.scalar_tensor_tensor(out=b.t[:, ec:ec + 1], in0=aall_t[:, ft, ec:ec + 1],
                                                                     scalar=carry.t[:, ft:ft + 1], in1=b.t[:, ec:ec + 1],
                                                                     op0=ALU.mult, op1=ALU.add),
                             reads=[aas[ft], b, carry], writes=[b])
                        h = hr.next()
                        if not reverse:
                            T.op("dve", lambda e: e.tensor_tensor_scan(out=h.t[:], data0=aall_t[:, ft, :], data1=b.t[:],
                                                                       initial=0.0, op0=ALU.mult, op1=ALU.add),
                                 reads=[aas[ft], b], writes=[h])
                            T.op("dve", lambda e: e.tensor_copy(carry.t[:, ft:ft + 1], h.t[:, TB - 1:TB]),
                                 reads=[h], writes=[carry])
                        else:
                            T.op("dve", lambda e: e.tensor_tensor_scan(out=h.t[:, ::-1], data0=aall_t[:, ft, ::-1],
                                                                       data1=b.t[:, ::-1], initial=0.0, op0=ALU.mult,
                                                                       op1=ALU.add), reads=[aas[ft], b], writes=[h])
                            T.op("dve", lambda e: e.tensor_copy(carry.t[:, ft:ft + 1], h.t[:, 0:1]),
                                 reads=[h], writes=[carry])
                        post(ft, h)

                with contextlib.ExitStack() as esF:
                    dg = Buf(esF.enter_context(nc.sbuf_tensor("dg", [128, 4, 8, 128], BF16)), "dg")
                    for k in range(4):
                        for ft in range(8):
                            T.op("dve", lambda e: e.tensor_scalar(out=dg.t[:, k, ft, :], in0=ident_f.t[:],
                                                                  scalar1=cols.t[:, ocw + k * 8 + ft:ocw + k * 8 + ft + 1],
                                                                  scalar2=None, op0=ALU.mult), reads=[ident_f, cols], writes=[dg])
                    xaring = mkring(esF, "xah", [128, 8, TB + 3], BF16, 2)
                    hbring = mkring(esF, "hbf", [128, 8, TB], BF16, 2)
                    blocksF = [(s, t0) for s, S in enumerate(seq_lens) for t0 in range(0, S, TB)]

                    def prep_f(bi):
                        s, t0 = blocksF[bi]
                        S = seq_lens[s]
                        xa = xaring.next()
                        lo, hi = max(0, t0 - 2), min(S, t0 + TB + 1)
                        if lo != t0 - 2 or hi != t0 + TB + 1:
                            T.op("dve", lambda e: e.memset(xa.t[:], 0.0), writes=[xa])
                        c0 = lo - (t0 - 2)
                        T.dma("sp", xa.t[:, :, c0:c0 + (hi - lo)], fm(XA[s])[:, :, lo:hi], writes=[xa])
                        xc = xcring.next()
                        for ft in range(8):
                            pm = next_mm()
                            for k in range(4):
                                T.op("pe", lambda e: e.matmul(pm.t[:], lhsT=dg.t[:, k, ft, :], rhs=xa.t[:, ft, k:k + TB],
                                                              start=(k == 0), stop=(k == 3)), reads=[dg, xa], writes=[pm])
                            T.op("act", lambda e: e.activation(out=xc.t[:, ft, :], in_=pm.t[:], func=AF.Identity,
                                                               bias=cols.t[:, ocb + ft:ocb + ft + 1], scale=1.0),
                                 reads=[pm, cols], writes=[xc])
                        T.dma("pool", fm(XC[s])[:, :, t0:t0 + TB], xc.t[:], reads=[xc])
                        lru_phase1(xc, 0, bi % 2)
                        return xc, bi % 2

                    cur = prep_f(0)
                    for bi, (s, t0) in enumerate(blocksF):
                        xc, k = cur
                        if t0 == 0:
                            T.op("dve", lambda e: e.memset(carry.t[:], 0.0), writes=[carry])
                        lru_sqrt()
                        if bi + 1 < len(blocksF):
                            cur = prep_f(bi + 1)
                        hb = hbring.next()

                        def post_f(ft, h, hb=hb):
                            T.op("dve", lambda e: e.tensor_copy(hb.t[:, ft, :], h.t[:]), reads=[h], writes=[hb])
                        lru_dve(xc, k, False, post_f)
                        T.dma("pool", fm(HF[s])[:, :, t0:t0 + TB], hb.t[:], reads=[hb])
                    T.finish("sp")
                T.dma_barrier()

                with contextlib.ExitStack() as esR:
                    hfring = mkring(esR, "hfB", [128, 8, TB], BF16, 1)
                    gyring = mkring(esR, "gyB", [128, 8, TB], BF16, 1)
                    taring = mkring(esR, "taB", [128, 8, TB], BF16, 1)
                    ybgring = mkring(esR, "ybgB", [128, 8, TB], BF16, 1)
                    tmpr = mkring(esR, "tmpB", [128, TB], F32, 1)
                    x1ring = mkring(esR, "x1B", [128, D], F32, 1)
                    ssring = mkring(esR, "ssB", [128, 2], F32, 8)
                    xring = stg
                    blocksR = [(s, t0) for s, S in enumerate(seq_lens) for t0 in range(S - TB, -1, -TB)]

                    def prep_b(bi):
                        s, t0 = blocksR[bi]
                        xc = xcring.next()
                        T.dma("sp", xc.t[:], fm(XC[s])[:, :, t0:t0 + TB], writes=[xc])
                        lru_phase1(xc, 1, bi % 2)
                        return xc, bi % 2

                    cur = prep_b(0)
                    for bi, (s, t0) in enumerate(blocksR):
                        S = seq_lens[s]
                        xc, k = cur
                        if t0 == S - TB:
                            T.op("dve", lambda e: e.memset(carry.t[:], 0.0), writes=[carry])
                        lru_sqrt()
                        if bi + 1 < len(blocksR):
                            cur = prep_b(bi + 1)
                        hf = hfring.next()
                        T.dma("sp", hf.t[:], fm(HF[s])[:, :, t0:t0 + TB], writes=[hf])
                        gy = gyring.next()
                        T.dma("sp", gy.t[:], fm(GY[s])[:, :, t0:t0 + TB], writes=[gy])
                        ta = taring.next()
                        T.dma("sp", ta.t[:], fm(TA[s])[:, :, t0:t0 + TB], writes=[ta])
                        ybg = ybgring.next()
                        T.dma("sp", ybg.t[:], fm(YBG[s])[:, :, t0:t0 + TB], writes=[ybg])

                        def post_b(ft, h, hf=hf, gy=gy):
                            tmp = tmpr.next()
                            T.op("dve", lambda e: e.tensor_tensor(out=tmp.t[:], in0=h.t[:], in1=hf.t[:, ft, :],
                                                                  op=ALU.add), reads=[h, hf], writes=[tmp])
                            T.op("dve", lambda e: e.tensor_tensor(out=gy.t[:, ft, :], in0=tmp.t[:], in1=gy.t[:, ft, :],
                                                                  op=ALU.mult), reads=[tmp, gy], writes=[gy])
                        lru_dve(xc, k, True, post_b)
                        z = gy
                        for nt in range(8):
                            pm = next_mm()
                            for kc in range(8):
                                T.op("pe", lambda e: e.matmul(pm.t[:], lhsT=wao.t[:, kc, nt * 128:(nt + 1) * 128],
                                                              rhs=z.t[:, kc, :], start=(kc == 0), stop=(kc == 7)),
                                     reads=[wao, z], writes=[pm])
                            tmp = tmpr.next()
                            T.op("dve", lambda e: e.scalar_tensor_tensor(out=tmp.t[:], in0=ta.t[:, nt, :], scalar=1.0,
                                                                         in1=pm.t[:], op0=ALU.add, op1=ALU.mult),
                                 reads=[ta, pm], writes=[tmp])
                            T.op("dve", lambda e: e.tensor_tensor(out=ybg.t[:, nt, :], in0=tmp.t[:], in1=ybg.t[:, nt, :],
                                                                  op=ALU.add), reads=[tmp, ybg], writes=[ybg])
                        m = ybg
                        for i in range(4):
                            oh = [next_mm(), next_mm()]
                            for nh in range(2):
                                for kc in range(8):
                                    T.op("pe", lambda e: e.matmul(oh[nh].t[:],
                                                                  lhsT=m.t[:, kc, i * 128:(i + 1) * 128],
                                                                  rhs=wo.t[:, kc, nh * 512:(nh + 1) * 512],
                                                                  start=(kc == 0), stop=(kc == 7)),
                                         reads=[m, wo], writes=[oh[nh]])
                            xt = xring.next()
                            r0 = seq_off[s] + t0 + i * 128
                            T.dma("sp", xt.t[:], xin[r0:r0 + 128, :], writes=[xt])
                            ss = ssring.next()
                            rs = ssring.next()
                            x1 = x1ring.next()
                            tm_epilogue(oh, 0.5, ss, rs, x1, GT[0][s], xt, x1)
                            T.dma("pool", X1[s][t0 + i * 128:t0 + (i + 1) * 128, :], x1.t[:], reads=[x1])
                    T.finish("sp")
                T.finish("sp")
            T.dma_barrier()

        stop_here("B")
        with contextlib.ExitStack() as esC:
            def sbC(name, shape, dt=F32):
                return Buf(esC.enter_context(nc.sbuf_tensor(name, list(shape), dt)), name)

            class RingC(Ring):
                def __init__(self, name, shape, dt, n):
                    self.b = [sbC("%s%d" % (name, i), shape, dt) for i in range(n)]
                    self.i = 0

            wup = sbC("wup", [128, 8, 2 * DFF], BF16)
            wdn = sbC("wdn", [128, NFF, D], BF16)
            stg = RingC("stgC", [128, D], F32, 2)
            load_w_bf16(wup, w_up.rearrange("(kc p) n -> p kc n", p=128), 8, 2 * DFF, stg, nchunk=1024)
            load_w_bf16(wdn, w_down.rearrange("(kc p) n -> p kc n", p=128), NFF, D, stg, nchunk=1024)
            xring = Ring.__new__(Ring)
            xring.b = [stg.b[0]]
            xring.i = 0
            xnring = RingC("xnC", [128, D], BF16, 1)
            ssring = RingC("ssC", [128, 1], F32, 8)
            hT2 = RingC("h2T", [128, 8, TB], BF16, 2)
            U = RingC("U", [128, 2, TB + 2], BF16, 2)
            UP = sbC("UP", [128, NFF, 2, 2], BF16)
            tv = RingC("tv", [128, TB], F32, 2)
            tg = RingC("tg", [128, TB], F32, 1)
            gg = RingC("gg", [128, TB], BF16, 1)
            GTt = sbC("GTt", [128, NFF, TB], BF16)
            x1l = Ring.__new__(Ring)
            x1l.b = [stg.b[1]]
            x1l.i = 0
            ofw, ofb = VOFF["ffn_conv_w"], VOFF["ffn_conv_b"]
            x1l0 = x1l.b[0]
            T.op("pool", lambda e: e.memset(x1l0.t[:], 0.0), writes=[x1l0])

            ssring2 = RingC("ssC2", [128, 2], F32, 8)
            xjunk = stg.b[0]

            SEGP = 128
            if seg is not None:
                NTV = (seg[1] + 2 * SEGP) // 128
                idx_sb = sbC("segidx_sb", [128, NTV], mybir.dt.int32)
                idxl_sb = sbC("segidxl_sb", [128, NTV + 1], mybir.dt.int32)
                flag_sb = sbC("segflag_sb", [128, 2])
                T.dma("sp", idx_sb.t[:], segidx, writes=[idx_sb])
                T.dma("sp", idxl_sb.t[:], segidxl, writes=[idxl_sb])
                T.dma("sp", flag_sb.t[:], segflag, writes=[flag_sb])

            def gather_rows(dst, src_rows, idx_ap, idx_buf):
                T._deps("pool", [idx_buf], [dst])
                key = (dst.name, "pool")
                if key not in T.dsem:
                    T.dsem[key] = [es.enter_context(nc.semaphore("d%d" % len(T.dsem))), 0]
                d = T.dsem[key]
                inst = nc.gpsimd.indirect_dma_start(out=dst.t[:], out_offset=None, in_=src_rows,
                                                    in_offset=bass.IndirectOffsetOnAxis(ap=idx_ap, axis=0))
                d[1] += 16
                inst.then_inc(d[0], 16)
                T._mark(("dma", key), (("dma", key), d[1], d[0]), [idx_buf], [dst])

            def emit_norm_tile(s, t0n, i, h2, segm):
                xt = xring.next()
                if not segm:
                    T.dma("sp", xt.t[:], X1[s][t0n + i * 128:t0n + (i + 1) * 128, :], writes=[xt])
                else:
                    tl = (t0n + i * 128) // 128
                    gather_rows(xt, X1[s], idx_sb.t[:, tl:tl + 1], idx_sb)
                norm_transpose(xt, h2, i * 128, G2, 24, s, xnring, ssring, None)
                if segm:
                    tl = (t0n + i * 128) // 128
                    fl = 0 if tl == 0 else (1 if tl == NTV - 1 else None)
                    if fl is not None:
                        T.op("dve", lambda e: e.tensor_scalar(out=h2.t[:, :, i * 128:(i + 1) * 128],
                                                              in0=h2.t[:, :, i * 128:(i + 1) * 128],
                                                              scalar1=flag_sb.t[:, fl:fl + 1], scalar2=None, op0=ALU.mult),
                             reads=[h2, flag_sb], writes=[h2])

            for s, S in enumerate(seq_lens):
                segm = (seg is not None and seg[0] == s)
                if not segm:
                    blocks = [(j * TB, TB, False) for j in range(S // TB)] + [(S, 128, True)]
                else:
                    VL = seg[1] + 2 * SEGP
                    blocks = [(t, min(TB, VL - t), False) for t in range(0, VL, TB)]
                T.op("dve", lambda e: e.memset(UP.t[:], 0.0), writes=[UP])
                h2_next = hT2.next()
                for i in range(blocks[0][1] // 128):
                    emit_norm_tile(s, blocks[0][0], i, h2_next, segm)
                for bj, (t0, ncol, last) in enumerate(blocks):
                    h2 = h2_next
                    nxt = blocks[bj + 1] if (bj + 1 < len(blocks) and not blocks[bj + 1][2]) else None
                    if nxt is not None:
                        h2_next = hT2.next()
                    pend = None

                    def back(p):
                        pft, ptv, ptg = p
                        g_ = gg.next()
                        T.op("act", lambda e: e.activation(out=g_.t[:, 0:ncol], in_=ptg.t[:, 0:ncol],
                                                           func=AF.Gelu_apprx_tanh), reads=[ptg], writes=[g_])
                        return g_

                    def back2(p, g_):
                        pft, ptv, ptg = p
                        T.op("dve", lambda e: e.tensor_tensor(out=GTt.t[:, pft, 0:ncol], in0=g_.t[:, 0:ncol],
                                                              in1=ptv.t[:, 0:ncol], op=ALU.mult),
                             reads=[g_, ptv], writes=[GTt])

                    for ft in range(NFF):
                        u = U.next()
                        T.op("act", lambda e: e.activation(out=u.t[:, :, 0:2], in_=UP.t[:, ft, :, :], func=AF.Copy),
                             reads=[UP], writes=[u])
                        if not last:
                            for vg in range(2):
                                pm = next_mm()
                                c0 = vg * DFF + ft * 128
                                for kc in range(8):
                                    T.op("pe", lambda e: e.matmul(pm.t[:, 0:ncol], lhsT=wup.t[:, kc, c0:c0 + 128],
                                                                  rhs=h2.t[:, kc, 0:ncol], start=(kc == 0), stop=(kc == 7)),
                                         reads=[wup, h2], writes=[pm])
                                T.op("act", lambda e: e.activation(out=u.t[:, vg, 2:2 + ncol], in_=pm.t[:, 0:ncol], func=AF.Copy),
                                     reads=[pm], writes=[u])
                            T.op("act", lambda e: e.activation(out=UP.t[:, ft, :, :], in_=u.t[:, :, ncol:ncol + 2], func=AF.Copy),
                                 reads=[u], writes=[UP])
                        else:
                            T.op("dve", lambda e: e.memset(u.t[:, :, 2:2 + TB], 0.0), writes=[u])
                        tvv = tv.next()
                        tgg = tg.next()
                        cfv, cfg = ft, NFF + ft
                        T.op("act", lambda e: e.activation(out=tvv.t[:, 0:ncol], in_=u.t[:, 0, 0:ncol], func=AF.Identity,
                                                           scale=cols.t[:, ofw + cfv:ofw + cfv + 1],
                                                           bias=cols.t[:, ofb + cfv:ofb + cfv + 1]),
                             reads=[u, cols], writes=[tvv])
                        g_prev = back(pend) if pend is not None else None
                        T.op("act", lambda e: e.activation(out=tgg.t[:, 0:ncol], in_=u.t[:, 1, 0:ncol], func=AF.Identity,
                                                           scale=cols.t[:, ofw + cfg:ofw + cfg + 1],
                                                           bias=cols.t[:, ofb + cfg:ofb + cfg + 1]),
                             reads=[u, cols], writes=[tgg])
                        for k in (1, 2):
                            T.op("dve", lambda e: e.scalar_tensor_tensor(
                                out=tvv.t[:, 0:ncol], in0=u.t[:, 0, k:k + ncol],
                                scalar=cols.t[:, ofw + k * 44 + cfv:ofw + k * 44 + cfv + 1], in1=tvv.t[:, 0:ncol],
                                op0=ALU.mult, op1=ALU.add), reads=[u, cols, tvv], writes=[tvv])
                        if pend is not None:
                            back2(pend, g_prev)
                        for k in (1, 2):
                            T.op("dve", lambda e: e.scalar_tensor_tensor(
                                out=tgg.t[:, 0:ncol], in0=u.t[:, 1, k:k + ncol],
                                scalar=cols.t[:, ofw + k * 44 + cfg:ofw + k * 44 + cfg + 1], in1=tgg.t[:, 0:ncol],
                                op0=ALU.mult, op1=ALU.add), reads=[u, cols, tgg], writes=[tgg])
                        pend = (ft, tvv, tgg)
                        if nxt is not None and ft in (3, 7, 11, 15) and (ft - 3) // 4 < nxt[1] // 128:
                            emit_norm_tile(s, nxt[0], (ft - 3) // 4, h2_next, segm)
                    g_prev = back(pend)
                    back2(pend, g_prev)
                    for i in range(ncol // 128):
                        tk0 = t0 - 1 + i * 128
                        if not segm:
                            lo, hi = max(0, tk0), min(S, tk0 + 128)
                        else:
                            lo, hi = max(SEGP, tk0), min(SEGP + seg[1], tk0 + 128)
                        if hi <= lo:
                            continue
                        oh = [next_mm(), next_mm()]
                        for nh in range(2):
                            for kt in range(NFF):
                                T.op("pe", lambda e: e.matmul(oh[nh].t[:],
                                                              lhsT=GTt.t[:, kt, i * 128:(i + 1) * 128],
                                                              rhs=wdn.t[:, kt, nh * 512:(nh + 1) * 512],
                                                              start=(kt == 0), stop=(kt == NFF - 1)),
                                     reads=[GTt, wdn], writes=[oh[nh]])
                        p0 = lo - tk0
                        xl = x1l.next()
                        if not segm:
                            T.dma("sp", xl.t[p0:p0 + (hi - lo), :], X1[s][lo:hi, :], writes=[xl])
                            orow = out_off[s] + lo
                        else:
                            tl = (t0 + i * 128) // 128
                            gather_rows(xl, X1[s], idxl_sb.t[:, tl:tl + 1], idxl_sb)
                            orow = out_off[s] + lo - SEGP
                        ss = ssring2.next()
                        rs = ssring2.next()
                        tm_epilogue(oh, 1.0, ss, rs, xjunk, GT[1][s], xl, xl)
                        T.dma("pool", yout[orow:orow + (hi - lo), :], xl.t[p0:p0 + (hi - lo), :], reads=[xl])
            T.finish("sp")
        T.finish("sp")
    except _Stop:
        pass
    return nc


def dft_tables(seq_lens):
    p = np.arange(128)
    ang = 2.0 * np.pi * np.outer(p, p) / 128.0
    C, Sn = np.cos(ang), np.sin(ang)
    tabs = {"tabA": np.concatenate([C, Sn], 1).astype(np.float32),
            "tabW": (np.concatenate([C, -Sn], 1) / np.sqrt(128.0)).astype(np.float32)}
    for S in sorted(set(seq_lens)):
        L = S // 128
        Q = 128 // L
        ka = np.arange(128)[:, None]
        be = np.arange(L)[None, :]
        th = 2.0 * np.pi * (ka * be % S) / S
        sc = 1.0 / np.sqrt(float(S))
        tabs["tw%d" % L] = np.concatenate([np.cos(th) * sc, np.sin(th) * sc, -np.sin(th) * sc], 1).astype(np.float32)
        b = np.arange(L)
        a2 = 2.0 * np.pi * np.outer(b, b) / L
        KC = np.einsum("qp,bk->qbkp", np.eye(Q), np.cos(a2)).reshape(128, 128)
        KS = np.einsum("qp,bk->qbkp", np.eye(Q), np.sin(a2)).reshape(128, 128)
        tabs["kron%d" % L] = np.concatenate([KC, KS, -KS, KC], 1).astype(np.float32)
    return tabs


def make_vec(nseq, g_pre_mix, conv_w, conv_b, b_lru_a, b_lru_x, lru_lambda, g_pre_ffn, ffn_conv_w, ffn_conv_b,
             b_ada, cs):
    parts = [g_pre_mix.reshape(-1, 128), conv_w.reshape(-1, 128), conv_b.reshape(-1, 128),
             b_lru_a.reshape(-1, 128), b_lru_x.reshape(-1, 128), lru_lambda.reshape(-1, 128),
             g_pre_ffn.reshape(-1, 128), ffn_conv_w.reshape(-1, 128), ffn_conv_b.reshape(-1, 128),
             b_ada.reshape(-1, 128)] + [c.reshape(-1, 128) for c in cs]
    return np.ascontiguousarray(np.concatenate(parts, 0).astype(np.float32))


def seg_arrays(S, A, seglen, pad=128):
    ntv = (seglen + 2 * pad) // 128
    p = np.arange(128)[:, None]
    t = np.arange(ntv)[None, :]
    idx = np.clip(A - pad + 128 * t + p, 0, S - 1).astype(np.int32)
    tl = np.arange(ntv + 1)[None, :]
    idxl = np.clip(A - pad + 128 * tl - 1 + p, 0, S - 1).astype(np.int32)
    flag = np.zeros((128, 2), np.float32)
    flag[:, 0] = 1.0 if A - pad >= 0 else 0.0
    flag[:, 1] = 1.0 if A + seglen + pad <= S else 0.0
    return {"segidx": np.ascontiguousarray(idx), "segidxl": np.ascontiguousarray(idxl), "segflag": flag}


_NC_CACHE = {}


def kernel(x_prompt, x_sample, c_prompt, c_sample, w_ada, b_ada, g_pre_mix, w_in, conv_w, conv_b,
           w_lru_a, b_lru_a, w_lru_x, b_lru_x, lru_lambda, w_a_out, w_b_out, w_o, g_post_mix,
           g_pre_ffn, w_up, ffn_conv_w, ffn_conv_b, w_down, g_post_ffn):
    f = lambda a: np.ascontiguousarray(np.asarray(a, dtype=np.float32))
    x_prompt, x_sample, c_prompt, c_sample = f(x_prompt), f(x_sample), f(c_prompt), f(c_sample)
    B, S1, _ = x_prompt.shape
    B2, S2, _ = x_sample.shape
    per = B // N_CORES
    seq_lens = [S1] * per + [S2]
    nq = N_CORES // B2
    seglen = S2 // nq
    key = tuple(seq_lens)
    if key not in _NC_CACHE:
        _NC_CACHE[key] = build_nc(seq_lens, seg=(per, seglen))
    nc = _NC_CACHE[key]
    tabs = dft_tables(seq_lens)
    shared = {
        "w_ada": f(w_ada[0]), "b_ada": f(b_ada[0]).reshape(1, -1), "w_in": f(w_in[0]),
        "w_lru_a": f(w_lru_a[0]).reshape(16, 128, 128), "w_lru_x": f(w_lru_x[0]).reshape(16, 128, 128),
        "w_a_out": f(w_a_out[0]), "w_b_out": f(w_b_out[0]), "w_o": f(w_o[0]), "w_up": f(w_up[0]),
        "w_down": f(w_down[0]), "g_post_mix": f(g_post_mix[0]).reshape(1, -1),
        "g_post_ffn": f(g_post_ffn[0]).reshape(1, -1),
    }
    shared.update(tabs)
    in_maps = []
    for c in range(N_CORES):
        ps_ = list(range(c * per, (c + 1) * per))
        sm = c % B2
        xin = np.concatenate([x_prompt[p] for p in ps_] + [x_sample[sm]], 0)
        cs = [c_prompt[p] for p in ps_] + [c_sample[sm]]
        vecp = make_vec(len(seq_lens), f(g_pre_mix[0]), f(conv_w[0]), f(conv_b[0]), f(b_lru_a[0]), f(b_lru_x[0]),
                        f(lru_lambda[0]), f(g_pre_ffn[0]), f(ffn_conv_w[0]), f(ffn_conv_b[0]), f(b_ada[0]), cs)
        m = dict(shared)
        m["xin"] = np.ascontiguousarray(xin)
        m["vec"] = vecp
        m.update(seg_arrays(S2, (c // B2) * seglen, seglen))
        in_maps.append(m)
    res = run_bass_kernel_spmd(nc, in_maps, core_ids=list(range(N_CORES)))
    y_prompt = np.empty_like(x_prompt)
    y_sample = np.empty_like(x_sample)
    sm_of = lambda c: c % B2
    for c in range(N_CORES):
        yo = res.results[c]["yout"]
        for i in range(per):
            y_prompt[c * per + i] = yo[i * S1:(i + 1) * S1]
        q = c // B2
        y_sample[sm_of(c)][q * seglen:(q + 1) * seglen] = yo[per * S1:per * S1 + seglen]
    return (y_prompt, y_sample)
```
